# Optimizing a Trainium2 kernel written in Bass

```python
import math
import jax, jax.numpy as jnp
from jax import lax
import numpy as np

D_MODEL = 2048
BATCH = 4
SEQ = 2048
DEPTH = 1
DEC_BATCH = 128
DEC_SEQ = 1
PAST_LEN = 16384
PAGE_SIZE = 128

D_LRU = D_MODEL
H_LRU = 8
LRU_BW = D_LRU // H_LRU
LRU_C = 8.0
CONV_W = 4
D_S5 = D_MODEL // 2
S5_H = 16
S5_G = D_S5 // S5_H
S5_P = 64
D_FF = ((8 * D_MODEL // 3 + 255) // 256) * 256
IN_WIDTH = D_LRU + D_S5 + 2 * D_MODEL
EPS = 1e-6

kernel_name = "hybrid_rglru_s5_gated_merge_decoder_step"


def rmsnorm(x, g):
    xf = x.astype(jnp.float32)
    y = xf * lax.rsqrt(jnp.mean(xf * xf, axis=-1, keepdims=True) + EPS)
    return (y * g.astype(jnp.float32)).astype(x.dtype)


def causal_dwconv(x, buf, w, b):
    L = x.shape[1]
    xx = jnp.concatenate([buf.astype(x.dtype), x], axis=1)
    y = b + xx[:, 0:L] * w[0]
    for k in range(1, CONV_W):
        y = y + xx[:, k:k + L] * w[k]
    return y.astype(x.dtype), xx[:, L:]


def rg_lru(x, h0, pos, wa, ba, wx, bx, lam):
    Bsz, L, _ = x.shape
    xf = x.astype(jnp.float32)
    xh = xf.reshape(Bsz, L, H_LRU, LRU_BW)
    r = jax.nn.sigmoid(jnp.einsum('blhi,hij->blhj', xh, wa.astype(jnp.float32)) + ba.astype(jnp.float32))
    i = jax.nn.sigmoid(jnp.einsum('blhi,hij->blhj', xh, wx.astype(jnp.float32)) + bx.astype(jnp.float32))
    r = r.reshape(Bsz, L, D_LRU)
    i = i.reshape(Bsz, L, D_LRU)
    log_a = -LRU_C * r * jax.nn.softplus(-lam.astype(jnp.float32))
    a = jnp.exp(log_a)
    mult = jnp.where((pos == 0)[None, :, None], 1.0, jnp.sqrt(-jnp.expm1(2.0 * log_a)))
    bt = mult * (i * xf)

    def step(h, ab):
        a_t, b_t = ab
        h = a_t * h + b_t
        return h, h

    hT, hs = lax.scan(step, h0.astype(jnp.float32), (jnp.swapaxes(a, 0, 1), jnp.swapaxes(bt, 0, 1)))
    return jnp.swapaxes(hs, 0, 1).astype(x.dtype), hT


def s5_layer(u, h0_re, h0_im, lam_re, lam_im, log_dt, b_re, b_im, c_re, c_im, d):
    Bsz, L, _ = u.shape
    f32 = jnp.float32
    uf = u.astype(f32).reshape(Bsz, L, S5_G, S5_H)
    lre, lim = lam_re.astype(f32), lam_im.astype(f32)
    dt = jnp.exp(log_dt.astype(f32))[:, None]
    mag = jnp.exp(lre * dt)
    ab_re = mag * jnp.cos(lim * dt)
    ab_im = mag * jnp.sin(lim * dt)
    e_re, e_im = ab_re - 1.0, ab_im
    den = lre * lre + lim * lim
    co_re = (e_re * lre + e_im * lim) / den
    co_im = (e_im * lre - e_re * lim) / den
    br, bi = b_re.astype(f32), b_im.astype(f32)
    bb_re = co_re[..., None] * br - co_im[..., None] * bi
    bb_im = co_re[..., None] * bi + co_im[..., None] * br
    bu_re = jnp.einsum('blgh,gph->blgp', uf, bb_re)
    bu_im = jnp.einsum('blgh,gph->blgp', uf, bb_im)
    a_re = jnp.broadcast_to(ab_re, (1, L, S5_G, S5_P))
    a_im = jnp.broadcast_to(ab_im, (1, L, S5_G, S5_P))

    def combine(e1, e2):
        a1r, a1i, b1r, b1i = e1
        a2r, a2i, b2r, b2i = e2
        return (a2r * a1r - a2i * a1i,
                a2r * a1i + a2i * a1r,
                a2r * b1r - a2i * b1i + b2r,
                a2r * b1i + a2i * b1r + b2i)

    pr, pim, sr, si = lax.associative_scan(combine, (a_re, a_im, bu_re, bu_im), axis=1)
    h0r = h0_re.astype(f32)[:, None]
    h0i = h0_im.astype(f32)[:, None]
    xr = pr * h0r - pim * h0i + sr
    xi = pr * h0i + pim * h0r + si
    y = (jnp.einsum('blgp,ghp->blgh', xr, c_re.astype(f32))
         - jnp.einsum('blgp,ghp->blgh', xi, c_im.astype(f32))
         + d.astype(f32) * uf)
    return y.reshape(Bsz, L, D_S5).astype(u.dtype), xr[:, -1], xi[:, -1]


def mixer_block(u, conv_buf, h0, s5_re0, s5_im0, pos, p):
    z = u @ p['w_in']
    x_lru = z[..., :D_LRU]
    x_s5 = z[..., D_LRU:D_LRU + D_S5]
    g_a = z[..., D_LRU + D_S5:D_LRU + D_S5 + D_MODEL]
    g_b = z[..., D_LRU + D_S5 + D_MODEL:]
    xc, new_buf = causal_dwconv(x_lru, conv_buf, p['lru_conv_w'], p['lru_conv_b'])
    ya, hT = rg_lru(xc, h0, pos, p['lru_wa'], p['lru_ba'], p['lru_wx'], p['lru_bx'], p['lru_lambda'])
    ya = ya @ p['lru_proj']
    ys, s_re, s_im = s5_layer(x_s5, s5_re0, s5_im0, p['s5_lambda_re'], p['s5_lambda_im'], p['s5_log_dt'],
                              p['s5_b_re'], p['s5_b_im'], p['s5_c_re'], p['s5_c_im'], p['s5_d'])
    v = jax.nn.gelu(ys)
    yb = (v @ p['s5_glu_wv']) * jax.nn.sigmoid(v @ p['s5_glu_wg'])
    merged = jax.nn.sigmoid(g_a) * ya + jax.nn.sigmoid(g_b) * yb
    return merged @ p['w_out'], new_buf, hT, s_re, s_im


def swiglu(u, wg, wu, wd):
    return (jax.nn.silu(u @ wg) * (u @ wu)) @ wd


def setup_inputs(seed: int = 0) -> dict:
    key = jax.random.key(seed)
    ks = jax.random.split(key, 32)
    nrm = jax.random.normal
    f32 = jnp.float32
    a0 = jax.random.uniform(ks[10], (DEPTH, D_LRU), f32, 0.9, 0.999)
    return {
        "x_prompt": nrm(ks[0], (BATCH, SEQ, D_MODEL), f32),
        "x_sample": nrm(ks[1], (DEC_BATCH, DEC_SEQ, D_MODEL), f32),
        "state_lru_conv": nrm(ks[2], (DEPTH, DEC_BATCH, CONV_W - 1, D_LRU), f32) * 0.5,
        "state_lru_h": nrm(ks[3], (DEPTH, DEC_BATCH, D_LRU), f32) * 0.5,
        "state_s5_re": nrm(ks[4], (DEPTH, DEC_BATCH, S5_G, S5_P), f32) * 0.5,
        "state_s5_im": nrm(ks[5], (DEPTH, DEC_BATCH, S5_G, S5_P), f32) * 0.5,
        "norm_mix_g": 1.0 + 0.01 * nrm(ks[6], (DEPTH, D_MODEL), f32),
        "w_in": nrm(ks[7], (DEPTH, D_MODEL, IN_WIDTH), f32) * D_MODEL ** -0.5,
        "lru_conv_w": nrm(ks[8], (DEPTH, CONV_W, D_LRU), f32) * CONV_W ** -0.5,
        "lru_conv_b": 0.01 * nrm(ks[9], (DEPTH, D_LRU), f32),
        "lru_wa": nrm(ks[11], (DEPTH, H_LRU, LRU_BW, LRU_BW), f32) * LRU_BW ** -0.5,
        "lru_ba": 0.01 * nrm(ks[12], (DEPTH, H_LRU, LRU_BW), f32),
        "lru_wx": nrm(ks[13], (DEPTH, H_LRU, LRU_BW, LRU_BW), f32) * LRU_BW ** -0.5,
        "lru_bx": 0.01 * nrm(ks[14], (DEPTH, H_LRU, LRU_BW), f32),
        "lru_lambda": jnp.log(a0) - jnp.log1p(-a0),
        "lru_proj": nrm(ks[15], (DEPTH, D_LRU, D_MODEL), f32) * D_LRU ** -0.5,
        "s5_lambda_re": -0.5 + 0.01 * nrm(ks[16], (DEPTH, S5_G, S5_P), f32),
        "s5_lambda_im": math.pi * jnp.arange(S5_P, dtype=f32)[None, None, :] + 0.01 * nrm(ks[17], (DEPTH, S5_G, S5_P), f32),
        "s5_log_dt": jax.random.uniform(ks[18], (DEPTH, S5_G), f32, math.log(0.001), math.log(0.1)),
        "s5_b_re": nrm(ks[19], (DEPTH, S5_G, S5_P, S5_H), f32) * (2.0 * S5_H) ** -0.5,
        "s5_b_im": nrm(ks[20], (DEPTH, S5_G, S5_P, S5_H), f32) * (2.0 * S5_H) ** -0.5,
        "s5_c_re": nrm(ks[21], (DEPTH, S5_G, S5_H, S5_P), f32) * (2.0 * S5_P) ** -0.5,
        "s5_c_im": nrm(ks[22], (DEPTH, S5_G, S5_H, S5_P), f32) * (2.0 * S5_P) ** -0.5,
        "s5_d": 0.5 * nrm(ks[23], (DEPTH, S5_G, S5_H), f32),
        "s5_glu_wv": nrm(ks[24], (DEPTH, D_S5, D_MODEL), f32) * D_S5 ** -0.5,
        "s5_glu_wg": nrm(ks[25], (DEPTH, D_S5, D_MODEL), f32) * D_S5 ** -0.5,
        "w_out": nrm(ks[26], (DEPTH, D_MODEL, D_MODEL), f32) * D_MODEL ** -0.5,
        "norm_ffn_g": 1.0 + 0.01 * nrm(ks[27], (DEPTH, D_MODEL), f32),
        "ffn_w_gate": nrm(ks[28], (DEPTH, D_MODEL, D_FF), f32) * D_MODEL ** -0.5,
        "ffn_w_up": nrm(ks[29], (DEPTH, D_MODEL, D_FF), f32) * D_MODEL ** -0.5,
        "ffn_w_down": nrm(ks[30], (DEPTH, D_FF, D_MODEL), f32) * D_FF ** -0.5,
        "norm_final_g": 1.0 + 0.01 * nrm(ks[31], (D_MODEL,), f32),
    }


def reference(x_prompt, x_sample, state_lru_conv, state_lru_h, state_s5_re, state_s5_im,
              norm_mix_g, w_in, lru_conv_w, lru_conv_b, lru_wa, lru_ba, lru_wx, lru_bx, lru_lambda,
              lru_proj, s5_lambda_re, s5_lambda_im, s5_log_dt, s5_b_re, s5_b_im, s5_c_re, s5_c_im,
              s5_d, s5_glu_wv, s5_glu_wg, w_out, norm_ffn_g, ffn_w_gate, ffn_w_up, ffn_w_down,
              norm_final_g):
    pos_p = jnp.arange(SEQ, dtype=jnp.int32)
    pos_s = PAST_LEN + jnp.arange(DEC_SEQ, dtype=jnp.int32)
    xp, xs = x_prompt, x_sample
    conv_p_l, h_p_l, sre_p_l, sim_p_l = [], [], [], []
    conv_s_l, h_s_l, sre_s_l, sim_s_l = [], [], [], []
    for l in range(DEPTH):
        p = dict(w_in=w_in[l], lru_conv_w=lru_conv_w[l], lru_conv_b=lru_conv_b[l],
                 lru_wa=lru_wa[l], lru_ba=lru_ba[l], lru_wx=lru_wx[l], lru_bx=lru_bx[l],
                 lru_lambda=lru_lambda[l], lru_proj=lru_proj[l],
                 s5_lambda_re=s5_lambda_re[l], s5_lambda_im=s5_lambda_im[l], s5_log_dt=s5_log_dt[l],
                 s5_b_re=s5_b_re[l], s5_b_im=s5_b_im[l], s5_c_re=s5_c_re[l], s5_c_im=s5_c_im[l],
                 s5_d=s5_d[l], s5_glu_wv=s5_glu_wv[l], s5_glu_wg=s5_glu_wg[l], w_out=w_out[l])
        up = rmsnorm(xp, norm_mix_g[l])
        yp, cb_p, h_p, sr_p, si_p = mixer_block(
            up, jnp.zeros((BATCH, CONV_W - 1, D_LRU), xp.dtype), jnp.zeros((BATCH, D_LRU), jnp.float32),
            jnp.zeros((BATCH, S5_G, S5_P), jnp.float32), jnp.zeros((BATCH, S5_G, S5_P), jnp.float32), pos_p, p)
        xp = xp + yp
        xp = xp + swiglu(rmsnorm(xp, norm_ffn_g[l]), ffn_w_gate[l], ffn_w_up[l], ffn_w_down[l])
        us = rmsnorm(xs, norm_mix_g[l])
        ys, cb_s, h_s, sr_s, si_s = mixer_block(
            us, state_lru_conv[l], state_lru_h[l], state_s5_re[l], state_s5_im[l], pos_s, p)
        xs = xs + ys
        xs = xs + swiglu(rmsnorm(xs, norm_ffn_g[l]), ffn_w_gate[l], ffn_w_up[l], ffn_w_down[l])
        conv_p_l.append(cb_p); h_p_l.append(h_p); sre_p_l.append(sr_p); sim_p_l.append(si_p)
        conv_s_l.append(cb_s); h_s_l.append(h_s); sre_s_l.append(sr_s); sim_s_l.append(si_s)
    y_prompt = rmsnorm(xp, norm_final_g)
    y_sample = rmsnorm(xs, norm_final_g)
    return (y_prompt, y_sample,
            jnp.stack(conv_p_l, 0), jnp.stack(h_p_l, 0), jnp.stack(sre_p_l, 0), jnp.stack(sim_p_l, 0),
            jnp.stack(conv_s_l, 0), jnp.stack(h_s_l, 0), jnp.stack(sre_s_l, 0), jnp.stack(sim_s_l, 0))
```

```python
import math, os
from contextlib import ExitStack
import numpy as np
import concourse.bass as bass
import concourse.mybir as mybir
from concourse.bass_utils import run_bass_kernel_spmd

F32 = mybir.dt.float32
BF16 = mybir.dt.bfloat16
I32 = mybir.dt.int32
AF = mybir.ActivationFunctionType
ALU = mybir.AluOpType

D = 2048
DFF = 5632
NCH = 16
NS = 16
NT = 528
SAME_SYNC = True
GELU_C0 = math.sqrt(2.0 / math.pi)
GELU_C1 = 0.044715


class Res:
    __slots__ = ("name", "w", "r")

    def __init__(self, name):
        self.name = name
        self.w = None
        self.r = []


class Prog:
    ENG = ["pe", "act", "dve", "pool", "sp"]

    def __init__(self, nc, es):
        self.nc = nc
        self.es = es
        self.q = {e: [] for e in self.ENG}
        self.cnt = {e: 0 for e in self.ENG}
        self.sems = {}
        self.seen = {e: {} for e in self.ENG}
        self.dcnt = {}
        for e in ["pe", "act", "dve"]:
            self.sems[e] = es.enter_context(nc.semaphore("s_" + e))

    def dma_sem(self, name):
        self.sems[name] = self.es.enter_context(self.nc.semaphore(name))
        self.dcnt[name] = 0
        return name

    def _waits(self, eng, reads, writes, after=()):
        need = {}
        def add(m):
            if m is None:
                return
            k, v = m
            if k == eng and not SAME_SYNC:
                return
            if v > need.get(k, 0):
                need[k] = v
        for m in after:
            add(m)
        for r in reads:
            add(r.w)
        for w in writes:
            add(w.w)
            for m in w.r:
                add(m)
        out = []
        for k, v in need.items():
            if self.seen[eng].get(k, 0) >= v:
                continue
            self.seen[eng][k] = v
            out.append((k, v))
        return out

    def _push_waits(self, eng, waits):
        for k, v in waits:
            sem = self.sems[k]
            self.q[eng].append(lambda e, sem=sem, v=v: e.wait_ge(sem, v))

    def op(self, eng, fn, reads=(), writes=(), after=()):
        waits = self._waits(eng, reads, writes, after)
        self._push_waits(eng, waits)
        self.cnt[eng] += 1
        c = self.cnt[eng]
        sem = self.sems[eng]
        self.q[eng].append(lambda e, fn=fn, sem=sem: fn(e).then_inc(sem, 1))
        m = (eng, c)
        for r in reads:
            r.r.append(m)
        for w in writes:
            w.w = m
            w.r = []
        return m

    def dma(self, eng, semname, fn, reads=(), writes=()):
        waits = self._waits(eng, reads, writes)
        self._push_waits(eng, waits)
        self.dcnt[semname] += 16
        v = self.dcnt[semname]
        sem = self.sems[semname]
        self.q[eng].append(lambda e, fn=fn, sem=sem: fn(e).then_inc(sem, 16))
        m = (semname, v)
        for r in reads:
            r.r.append(m)
        for w in writes:
            w.w = m
            w.r = []

    def barrier(self, engs=("pe", "act", "dve", "sp")):
        for e in engs:
            waits = []
            for k in ["pe", "act", "dve"]:
                v = self.cnt[k]
                if k == e and not SAME_SYNC:
                    continue
                if v > 0 and self.seen[e].get(k, 0) < v:
                    self.seen[e][k] = v
                    waits.append((k, v))
            for k, v in self.dcnt.items():
                if k.startswith("w"):
                    continue
                if v > 0 and self.seen[e].get(k, 0) < v:
                    self.seen[e][k] = v
                    waits.append((k, v))
            self._push_waits(e, waits)


def build_program():
    nc = bass.Bass("TRN2", target_bir_lowering=False)
    es = ExitStack()

    def din(name, shape, dt=F32):
        return nc.dram_tensor(name, list(shape), dt, kind="ExternalInput").ap()

    def dout(name, shape):
        return nc.dram_tensor(name, list(shape), F32, kind="ExternalOutput").ap()

    xf = din("xf", [2064, D])
    flag_d = din("flag", [128, 2])
    ident_d = din("ident", [128, 128])
    maskc_d = din("maskc", [128, 4 * 128])
    cs_d = din("csT", [128, NCH * NS * 3])
    h0_d = din("h0T", [128, NCH * NS])
    s5r_d = din("s5rT", [128, 32 * NS])
    s5i_d = din("s5iT", [128, 32 * NS])
    ldt_d = din("ldt", [128, 32])
    lre_d = din("lre", [128, 32])
    lim_d = din("lim", [128, 32])
    bre_d = din("bre", [128, 32 * 16])
    bim_d = din("bim", [128, 32 * 16])
    cre_d = din("cre", [128, 8 * 64])
    cim_d = din("cim", [128, 8 * 64])
    sd_d = din("s5d", [128, 8])
    cw_d = din("convw", [128, NCH * 4])
    pv_d = din("pvec", [128, 7 * NCH])
    gfin_d = din("gfin", [128, D])
    w_in = din("w_in", [D, 7168])
    lru_wa = din("lru_wa", [8 * 256, 256])
    lru_wx = din("lru_wx", [8 * 256, 256])
    lru_proj = din("lru_proj", [D, D])
    wv_d = din("s5_glu_wv", [1024, D])
    wg_d = din("s5_glu_wg", [1024, D])
    w_out = din("w_out", [D, D])
    w_gate = din("ffn_w_gate", [D, DFF])
    w_up = din("ffn_w_up", [D, DFF])
    w_down = din("ffn_w_down", [DFF, D])

    o_y = dout("o_y", [1040, D])
    o_cp = dout("o_cp", [128, NCH * 3])
    o_hp = dout("o_hp", [128, NCH])
    o_sp = dout("o_sp", [128, 64])
    o_cs = dout("o_cs", [128, NCH * NS * 3])
    o_hs = dout("o_hs", [128, NCH * NS])
    o_ss = dout("o_ss", [128, 2 * 32 * NS])

    def sb(name, shape, dt=F32):
        return es.enter_context(nc.sbuf_tensor(name, list(shape), dt))

    P = Prog(nc, es)
    PS = es.enter_context(nc.psum_tensor("ps", [128, 8, 512], F32))
    ps_res = [Res("ps%d" % i) for i in range(8)]
    ps_ptr = [0]

    def bank():
        b = ps_ptr[0] % 6
        ps_ptr[0] += 1
        return PS[:, b, :], ps_res[b]

    def ybank(i):
        return PS[:, 6 + i, :], ps_res[6 + i]

    UT = sb("UT", [128, NCH, NT], BF16)
    R2 = sb("R2", [128, 44 * NT], BF16)
    R1 = sb("R1", [128, 5 * D], F32)
    WR = sb("WR", [128, 4, 16, 256], BF16)
    WB = sb("WB", [128, 32, 2, 128], BF16)
    CW = sb("CW", [128, 32, 2, 128], BF16)
    IDN = sb("IDN", [128, 128])
    FLG = sb("FLG", [128, 2])
    CS = sb("CS", [128, NCH, NS, 3])
    H0 = sb("H0", [128, NCH, NS])
    S5R = sb("S5R", [128, 32, NS])
    S5I = sb("S5I", [128, 32, NS])
    SD = sb("SD", [128, 8])
    CWV = sb("CWV", [128, NCH, 4])
    PV = sb("PV", [128, 7, NCH])
    LP2 = sb("LP2", [128, 11, 32, 4])
    HC = sb("HC", [128, NCH])
    XLT = sb("XLT", [128, NCH, 3])
    S5C = sb("S5C", [128, 32, 2])
    OCS = sb("OCS", [128, NCH, NS, 3])
    OHS = sb("OHS", [128, NCH, NS])
    OSS = sb("OSS", [128, 2, 32, NS])
    XTN = sb("XTN", [128, D])
    TGF = XTN[:, 0:1024].rearrange("p (a b) -> p a b", b=512)
    GW = sb("GW", [128, 2, 2, 2, 256], BF16)
    r_GW = [Res("GW0"), Res("GW1")]
    r_XTN_g = Res("XTN")
    SM = sb("SM", [128, 64])
    HALFB = sb("HALFB", [128, 2 * NCH])
    CNG = sb("CNG", [128, 2 * NCH])

    r_UT = [Res("UT%d" % k) for k in range(NCH)]
    r_WR = [Res("WR%d" % k) for k in range(4)]
    r_WBCW = Res("WBCW")
    r_small = Res("small")
    r_HC = Res("HC")
    r_XLT = Res("XLT")
    r_S5C = Res("S5C")
    r_out = Res("outs")
    r_SM = Res("SM")

    HS = R2[:, 0:NCH * NT].rearrange("p (a b) -> p a b", b=NT)
    MG = R2[:, NCH * NT:2 * NCH * NT].rearrange("p (a b) -> p a b", b=NT)
    VT = R2[:, 32 * NT:40 * NT].rearrange("p (a b) -> p a b", b=NT)
    XCB = R2[:, 40 * NT:42 * NT].rearrange("p (a b) -> p a b", b=NT)
    XS = R2[:, 42 * NT:44 * NT].rearrange("p (a b) -> p a b", b=NT)
    HM = R2[:, :].rearrange("p (a b) -> p a b", b=NT)
    r_HS = [Res("HS%d" % k) for k in range(NCH)]
    r_MG = [Res("MG%d" % k) for k in range(NCH)]
    r_VT = [Res("VT%d" % k) for k in range(8)]
    r_XCB = [Res("XCB%d" % k) for k in range(2)]
    r_XS = [Res("XS%d" % k) for k in range(2)]
    r_HM = [Res("HM%d" % k) for k in range(44)]
    XP1 = R1[:, :].rearrange("p (a b) -> p a b", b=D)
    r_XP1 = [Res("XP1_%d" % k) for k in range(5)]

    def r1v(off, a, b):
        return R1[:, off:off + a * b].rearrange("p (a b) -> p a b", b=b)

    wsem = [P.dma_sem("w%d" % i) for i in range(4)]
    xsem = [P.dma_sem("x%d" % i) for i in range(5)]
    osem = P.dma_sem("o")
    gwsem = [P.dma_sem("wg0"), P.dma_sem("wg1")]
    isem = P.dma_sem("i")
    wr_ptr = [0]
    cssem = [P.dma_sem("wcs%d" % i) for i in range(4)]
    bsem = [P.dma_sem("wb%d" % i) for i in range(4)]
    SCR = nc.dram_tensor("wscr", [108, 128, 4096], BF16, kind="Internal").ap()
    r_scr = [Res("scr%d" % i) for i in range(108)]

    def wslot(skip=()):
        while True:
            s = wr_ptr[0] % 4
            wr_ptr[0] += 1
            if s not in skip:
                return s

    def wload(s, dst_ap, src_ap):
        P.dma("pool", wsem[s], lambda e: e.dma_start(out=dst_ap, in_=src_ap), writes=[r_WR[s]])

    USE_CACHE = os.environ.get("KNOCACHE", "") == ""
    back_jobs = []

    def _jobs():
        for sl in range(8):
            back_jobs.append([(0, 16, wview(w_in, 0, D, 3072 + sl * 256, 256))])
            back_jobs.append([(0, 16, wview(lru_proj, 0, D, sl * 256, 256))])
            back_jobs.append([(0, 16, wview(w_in, 0, D, 5120 + sl * 256, 256))])
            back_jobs.append([(0, 8, wview(wv_d, 0, 1024, sl * 256, 256)), (8, 16, wview(wg_d, 0, 1024, sl * 256, 256))])
        for nb in range(8):
            back_jobs.append([(0, 16, wview(w_out, 0, D, nb * 256, 256))])
        for fp in range(22):
            back_jobs.append([(0, 16, wview(w_gate, 0, D, fp * 256, 256))])
            back_jobs.append([(0, 16, wview(w_up, 0, D, fp * 256, 256))])
        for nb in range(8):
            for (f0, nf) in [(0, 16), (16, 16), (32, 12)]:
                back_jobs.append([(0, nf, wview(w_down, f0 * 128, nf * 128, nb * 256, 256))])

    conv_ptr = [0]
    back_ptr = [0]
    pinned = []

    def convert_some(n):
        if not USE_CACHE:
            return
        for _ in range(n):
            j = conv_ptr[0]
            if j >= len(back_jobs):
                return
            conv_ptr[0] += 1
            s = wslot(skip=tuple(pinned))
            for (r0, r1, srcap) in back_jobs[j]:
                wload(s, WR[:, s, r0:r1, :], srcap)
            P.dma("sp", cssem[s], lambda e, s=s, j=j: e.dma_start(out=SCR[j], in_=WR[:, s, :, :].rearrange("p a b -> p (a b)")),
                  reads=[r_WR[s]], writes=[r_scr[j]])

    def bload(s):
        j = back_ptr[0]
        back_ptr[0] += 1
        if USE_CACHE:
            P.dma("sp", bsem[s], lambda e, s=s, j=j: e.dma_start(out=WR[:, s, :, :].rearrange("p a b -> p (a b)"), in_=SCR[j]),
                  reads=[r_scr[j]], writes=[r_WR[s]])
        else:
            for (r0, r1, srcap) in back_jobs[j]:
                wload(s, WR[:, s, r0:r1, :], srcap)

    def wview(w, r0, nr, c0, ncols):
        return w[r0:r0 + nr, c0:c0 + ncols].rearrange("(kt p) n -> p kt n", p=128)

    _jobs()
    assert len(back_jobs) == 108
    def ld(dst, src):
        P.dma("sp", isem, lambda e: e.dma_start(out=dst, in_=src), writes=[r_small])

    ld(IDN[:], ident_d[:, :])
    ld(FLG[:], flag_d[:, :])
    ld(CS[:].rearrange("p a b c -> p (a b c)"), cs_d[:, :])
    ld(H0[:].rearrange("p a b -> p (a b)"), h0_d[:, :])
    ld(S5R[:].rearrange("p a b -> p (a b)"), s5r_d[:, :])
    ld(S5I[:].rearrange("p a b -> p (a b)"), s5i_d[:, :])
    ld(SD[:], sd_d[:, :])
    ld(CWV[:].rearrange("p a b -> p (a b)"), cw_d[:, :])
    ld(PV[:].rearrange("p a b -> p (a b)"), pv_d[:, :])
    o = [0]

    def t1(n):
        v = R1[:, o[0]:o[0] + n]
        o[0] += n
        return v

    LDT = t1(32); LRE = t1(32); LIM = t1(32)
    BRE = t1(512); BIM = t1(512); CRE = t1(512); CIM = t1(512)
    MASKC = t1(512)
    ld(LDT, ldt_d[:, :]); ld(LRE, lre_d[:, :]); ld(LIM, lim_d[:, :])
    ld(BRE, bre_d[:, :]); ld(BIM, bim_d[:, :]); ld(CRE, cre_d[:, :]); ld(CIM, cim_d[:, :])
    ld(MASKC, maskc_d[:, :])
    rs = [r_small]

    def V(fn, reads=None, writes=None):
        P.op("dve", fn, reads=rs if reads is None else reads, writes=rs if writes is None else writes)

    def A(fn, reads=None, writes=None):
        P.op("act", fn, reads=rs if reads is None else reads, writes=rs if writes is None else writes)

    def ts(out, in0, s1, s2, op0, op1=None):
        if op1 is None:
            return lambda e: e.tensor_scalar(out=out, in0=in0, scalar1=s1, scalar2=None, op0=op0)
        return lambda e: e.tensor_scalar(out=out, in0=in0, scalar1=s1, scalar2=s2, op0=op0, op1=op1)

    def tt(out, in0, in1, op):
        return lambda e: e.tensor_tensor(out=out, in0=in0, in1=in1, op=op)

    def stt(out, in0, s, in1, op0, op1):
        return lambda e: e.scalar_tensor_tensor(out=out, in0=in0, scalar=s, in1=in1, op0=op0, op1=op1)

    def act(out, in_, func, bias=None, scale=None, accum_out=None):
        kw = {}
        if bias is not None:
            kw["bias"] = bias
        if scale is not None:
            kw["scale"] = scale
        if accum_out is not None:
            kw["accum_out"] = accum_out
        return lambda e: e.activation(out=out, in_=in_, func=func, **kw)

    def poly_exp(out, x, nterm, tmp):
        V(ts(out, x, 1.0 / nterm, 1.0, ALU.mult, ALU.add))
        for k in range(nterm - 1, 0, -1):
            V(tt(tmp, out, x, ALU.mult))
            V(ts(out, tmp, 1.0 / k, 1.0, ALU.mult, ALU.add))

    T = [t1(32) for _ in range(16)]
    DT, AL, TH, MAG, C_, S_, ABR, ABI, COR, COI = T[:10]
    X0, X1, X2, X3, X4, X5 = T[10:16]
    V(ts(X0, LDT, 1.0 / 32, None, ALU.mult))
    poly_exp(DT, X0, 9, X1)
    for _ in range(5):
        V(tt(DT, DT, DT, ALU.mult))
    V(tt(AL, LRE, DT, ALU.mult))
    V(tt(TH, LIM, DT, ALU.mult))
    poly_exp(MAG, AL, 7, X1)
    KI = sb("KI", [128, 32], I32)
    V(ts(X0, TH, 1.0 / (2 * math.pi), None, ALU.mult))
    V(lambda e: e.tensor_copy(out=KI[:], in_=X0))
    V(lambda e: e.tensor_copy(out=X1, in_=KI[:]))
    C1 = float(np.float32(2 * math.pi))
    C2 = 2 * math.pi - C1
    V(stt(X2, X1, -C1, TH, ALU.mult, ALU.add))
    V(stt(X2, X1, -C2, X2, ALU.mult, ALU.add))
    V(ts(X2, X2, 0.25, None, ALU.mult))
    V(tt(X3, X2, X2, ALU.mult))
    def poly_trig(out, q2, denoms):
        V(ts(out, q2, -1.0 / denoms[-1], 1.0, ALU.mult, ALU.add))
        for dd in reversed(denoms[:-1]):
            V(tt(X5, out, q2, ALU.mult))
            V(ts(out, X5, -1.0 / dd, 1.0, ALU.mult, ALU.add))
    poly_trig(X4, X3, [6.0, 20.0, 42.0, 72.0, 110.0, 156.0, 210.0])
    V(tt(S_, X4, X2, ALU.mult))
    poly_trig(C_, X3, [2.0, 12.0, 30.0, 56.0, 90.0, 132.0, 182.0])
    for _ in range(2):
        V(tt(X0, C_, C_, ALU.mult))
        V(tt(X1, S_, S_, ALU.mult))
        V(tt(X4, S_, C_, ALU.mult))
        V(tt(C_, X0, X1, ALU.subtract))
        V(ts(S_, X4, 2.0, None, ALU.mult))
    V(tt(ABR, MAG, C_, ALU.mult))
    V(tt(ABI, MAG, S_, ALU.mult))
    V(ts(X0, ABR, -1.0, None, ALU.add))
    V(tt(X1, LRE, LRE, ALU.mult))
    V(tt(X2, LIM, LIM, ALU.mult))
    V(tt(X1, X1, X2, ALU.add))
    V(lambda e: e.reciprocal(out=X1, in_=X1))
    V(tt(X2, X0, LRE, ALU.mult))
    V(tt(X3, ABI, LIM, ALU.mult))
    V(tt(X2, X2, X3, ALU.add))
    V(tt(COR, X2, X1, ALU.mult))
    V(tt(X2, ABI, LRE, ALU.mult))
    V(tt(X3, X0, LIM, ALU.mult))
    V(tt(X2, X2, X3, ALU.subtract))
    V(tt(COI, X2, X1, ALU.mult))
    V(lambda e: e.tensor_copy(out=LP2[:, 0, :, 0], in_=ABR))
    V(lambda e: e.tensor_copy(out=LP2[:, 0, :, 2], in_=ABI))
    for l in range(1, 11):
        pr, pi_ = LP2[:, l - 1, :, 0], LP2[:, l - 1, :, 2]
        V(tt(X0, pr, pr, ALU.mult))
        V(tt(X1, pi_, pi_, ALU.mult))
        V(tt(LP2[:, l, :, 0], X0, X1, ALU.subtract))
        V(tt(X2, pr, pi_, ALU.mult))
        V(ts(LP2[:, l, :, 2], X2, 2.0, None, ALU.mult))
    V(lambda e: e.tensor_copy(out=LP2[:, :, :, 1], in_=LP2[:, :, :, 0]))
    V(ts(LP2[:, :, :, 3], LP2[:, :, :, 2], -1.0, None, ALU.mult))
    BBR = t1(512); BBI = t1(512); TB = t1(512)
    def v3(x):
        return x.rearrange("p (a b) -> p a b", b=16)
    def bc(x):
        return x.unsqueeze(2).broadcast_to([128, 32, 16])
    V(tt(v3(BBR), v3(BRE), bc(COR), ALU.mult))
    V(tt(v3(TB), v3(BIM), bc(COI), ALU.mult))
    V(tt(BBR, BBR, TB, ALU.subtract))
    V(tt(v3(BBI), v3(BIM), bc(COR), ALU.mult))
    V(tt(v3(TB), v3(BRE), bc(COI), ALU.mult))
    V(tt(BBI, BBI, TB, ALU.add))
    Z = t1(32 * 128)
    Z4 = Z.rearrange("p (k j c) -> p k j c", k=8, j=4)
    for ri, BB in enumerate([BBR, BBI]):
        V(lambda e: e.memset(Z, 0.0))
        BB4 = BB.rearrange("p (k j h) -> p k j h", k=8, j=4)
        for jl in range(4):
            for gl in range(2):
                pp = slice(64 * gl, 64 * gl + 64)
                c0 = (2 * jl + gl) * 16
                V(lambda e, pp=pp, jl=jl, c0=c0, BB4=BB4: e.tensor_copy(out=Z4[pp, :, jl, c0:c0 + 16], in_=BB4[pp, :, jl, :]))
        for j in range(32):
            pb, rb = bank()
            P.op("pe", lambda e, j=j, pb=pb: e.transpose(out=pb[:, 0:128], in_=Z[:, j * 128:(j + 1) * 128], identity=IDN[:]),
                 reads=rs, writes=[rb])
            P.op("act", act(WB[:, j, ri, :], pb[:, 0:128], AF.Copy), reads=[rb], writes=[r_WBCW])
    Y = t1(128)
    for ri, CC in enumerate([CRE, CIM]):
        for j in range(32):
            kt, jl = j // 4, j % 4
            cc = CC[:, kt * 64:(kt + 1) * 64]
            mk = MASKC[:, jl * 128:(jl + 1) * 128]
            V(tt(Y[:, 0:64], cc, mk[:, 0:64], ALU.mult))
            V(tt(Y[:, 64:128], cc, mk[:, 64:128], ALU.mult))
            pb, rb = bank()
            P.op("pe", lambda e, pb=pb: e.transpose(out=pb[:, 0:128], in_=Y, identity=IDN[:]), reads=rs, writes=[rb])
            P.op("act", act(CW[:, j, ri, :], pb[:, 0:128], AF.Copy, scale=(1.0 if ri == 0 else -1.0)),
                 reads=[rb], writes=[r_WBCW])
    LAM = PV[:, 3, :]
    A(act(X0[:, 0:NCH], LAM, AF.Exp, scale=-1.0))
    V(ts(X0[:, 0:NCH], X0[:, 0:NCH], 1.0, None, ALU.add))
    A(act(X1[:, 0:NCH], X0[:, 0:NCH], AF.Ln))
    V(ts(CNG[:, 0:NCH], X1[:, 0:NCH], -8.0, None, ALU.mult))
    V(ts(CNG[:, NCH:2 * NCH], X1[:, 0:NCH], -4.0, None, ALU.mult))
    V(ts(HALFB[:, 0:NCH], PV[:, 1, :], 0.5, None, ALU.mult))
    V(ts(HALFB[:, NCH:2 * NCH], PV[:, 2, :], 0.5, None, ALU.mult))
    V(lambda e: e.memset(HC[:], 0.0), writes=[r_HC])
    V(lambda e: e.memset(XLT[:], 0.0), writes=[r_XLT])
    V(lambda e: e.memset(S5C[:], 0.0), writes=[r_S5C])
    EPS = 1e-6
    EPSB = sb("EPSB", [128, 1])
    V(lambda e: e.memset(EPSB[:], EPS))
    ONEB = sb("ONEB", [128, 1])
    V(lambda e: e.memset(ONEB[:], 1.0))
    P.barrier()

    groups = [
        dict(name="P1", row0=0, npr=512, ns=0, full=False, first=True, mask=False),
        dict(name="P2", row0=512, npr=512, ns=0, full=False, first=False, mask=False),
        dict(name="M1", row0=1024, npr=512, ns=0, full=True, first=False, mask=True),
        dict(name="M2", row0=1536, npr=512, ns=NS, full=True, first=False, mask=False),
    ]

    def rms_stats(src, np_, junk, col):
        ssq = SM[0:np_, col:col + 1]
        A(act(junk, src, AF.Square, accum_out=ssq), reads=[r_SM], writes=[r_SM])
        A(act(ssq, ssq, AF.Sqrt, bias=EPSB[0:np_, :], scale=1.0 / D), reads=[r_SM], writes=[r_SM])
        V(lambda e: e.reciprocal(out=ssq, in_=ssq), reads=[r_SM], writes=[r_SM])
        return ssq

    def do_group(g, chk):
        npr, ns, full = g["npr"], g["ns"], g["full"]
        N = npr + ns
        ntiles = [(c, 512) for c in range(0, npr, 512)] + ([(npr, ns)] if ns else [])
        ttiles = [(c, 128) for c in range(0, npr, 128)] + ([(npr, ns)] if ns else [])
        row0 = g["row0"]

        XT = r1v(0, 2, D)
        JK = R1[:, 2 * D:3 * D]
        r_XT = [Res("XT0"), Res("XT1")]
        r_JK = Res("JK")
        for ti, (c0, n) in enumerate(ttiles):
            s = ti % 2
            xt = XT[0:n, s, :]
            P.dma("sp", xsem[s], lambda e, xt=xt, c0=c0, n=n: e.dma_start(out=xt, in_=xf[row0 + c0:row0 + c0 + n, :]),
                  writes=[r_XT[s]])
            chk("A1")
            ssq = SM[0:n, ti:ti + 1]
            P.op("act", act(JK[0:n, :], xt, AF.Square, accum_out=ssq), reads=[r_XT[s]], writes=[r_JK, r_SM])
            chk("A2")
            P.op("act", act(ssq, ssq, AF.Sqrt, bias=EPSB[0:n, :], scale=1.0 / D), reads=[r_SM], writes=[r_SM])
            chk("A3")
            P.op("dve", lambda e, ssq=ssq: e.reciprocal(out=ssq, in_=ssq), reads=[r_SM], writes=[r_SM])
            P.op("act", act(xt, xt, AF.Identity, scale=ssq), reads=[r_SM, r_XT[s]], writes=[r_XT[s]])
            chk("A4")
            for k4 in range(4):
                pb, rb = bank()
                def tr(e, pb=pb, xt=xt, k4=k4, n=n):
                    ins = None
                    for kk in range(4):
                        k = k4 * 4 + kk
                        ins = e.transpose(out=pb[:, kk * 128:kk * 128 + n], in_=xt[:, k * 128:(k + 1) * 128],
                                          identity=IDN[0:n, 0:n])
                    return ins
                P.op("pe", tr, reads=[r_XT[s]], writes=[rb])
                chk("A5")
                for kk in range(4):
                    k = k4 * 4 + kk
                    eng = "dve"
                    if eng == "dve":
                        P.op("dve", ts(UT[:, k, c0:c0 + n], pb[:, kk * 128:kk * 128 + n], PV[:, 4, k:k + 1], None, ALU.mult),
                             reads=[rb], writes=[r_UT[k]])
                        chk("A6")
                    else:
                        P.op("act", act(UT[:, k, c0:c0 + n], pb[:, kk * 128:kk * 128 + n], AF.Identity, scale=PV[:, 4, k:k + 1]),
                             reads=[rb], writes=[r_UT[k]])
                        chk("A7")
            chk("A8")
        P.barrier()
        chk(g["name"] + "A")

        NP3 = N + 3
        off = [0]
        def tl(a, b):
            v = r1v(off[0], a, b)
            off[0] += a * b
            return v
        XL = tl(2, NP3); XC = tl(2, N); TR = tl(2, N); TI = tl(2, N); AA = tl(2, N)
        G1 = XL[:, :, 0:N]
        HF = TR
        A2 = TI
        X4 = R1[:, off[0]:off[0] + 8 * N].rearrange("p (t r n) -> p t r n", t=4, r=2)
        off[0] += 8 * N
        TM1 = R1[:, off[0]:off[0] + 256].rearrange("p (t r n) -> p t r n", t=4, r=2)
        TM2 = R1[:, off[0] + 256:off[0] + 384].rearrange("p (t n) -> p t n", t=4)
        TM3 = R1[:, off[0] + 384:off[0] + 512].rearrange("p (t n) -> p t n", t=4)
        off[0] += 512
        assert off[0] <= 5 * D, off[0]
        YT = XTN[:, 0:N]; Y2 = XTN[:, N:2 * N]; Y3 = XTN[:, 2 * N:3 * N]
        r_t = {n_: [Res(n_ + "0"), Res(n_ + "1")] for n_ in ["XL", "XC", "TR", "TI", "AA"]}
        r_t["G1"] = r_t["XL"]
        r_t["HF"] = r_t["TR"]
        r_t["A2"] = r_t["TI"]
        XRB = XCB
        r_XRe = [Res("XRe0"), Res("XRe1")]
        r_XIm = [Res("XIm0"), Res("XIm1")]
        r_XSF = [Res("XSF0"), Res("XSF1")]
        r_Y = r_XTN_g
        L = 9
        gw_ptr = [0]

        def lru_block(hb, nxt=None):
            gb = gw_ptr[0] % 2
            gw_ptr[0] += 1
            P.dma("pool", gwsem[gb], lambda e: e.dma_start(out=GW[:, gb, 0, :, :], in_=lru_wa[hb * 256:(hb + 1) * 256, :].rearrange("(k p) n -> p k n", p=128)),
                  writes=[r_GW[gb]])
            P.dma("pool", gwsem[gb], lambda e: e.dma_start(out=GW[:, gb, 1, :, :], in_=lru_wx[hb * 256:(hb + 1) * 256, :].rearrange("(k p) n -> p k n", p=128)),
                  writes=[r_GW[gb]])
            s = wslot(skip=tuple(pinned))
            wload(s, WR[:, s, :, :], wview(w_in, 0, D, hb * 256, 256))
            for j in range(2):
                c = hb * 2 + j
                mc = j
                P.op("dve", lambda e, j=j, c=c: e.tensor_copy(out=XL[:, j, 0:3], in_=XLT[:, c, :]),
                     reads=[r_XLT], writes=[r_t["XL"][j]])
                if g["mask"]:
                    P.op("dve", ts(XL[:, j, 0:3], XL[:, j, 0:3], FLG[:, 0:1], None, ALU.mult),
                         reads=[r_small, r_t["XL"][j]], writes=[r_t["XL"][j]])
                for (n0, nn) in ntiles:
                    pb, rb = bank()
                    def mm(e, pb=pb, s=s, mc=mc, n0=n0, nn=nn):
                        ins = None
                        for k in range(NCH):
                            ins = e.matmul(out=pb[:, 0:nn], lhsT=WR[:, s, k, mc * 128:(mc + 1) * 128],
                                           rhs=UT[:, k, n0:n0 + nn], start=(k == 0), stop=(k == NCH - 1))
                        return ins
                    P.op("pe", mm, reads=[r_WR[s]] + r_UT, writes=[rb])
                    P.op("act", act(XL[:, j, 3 + n0:3 + n0 + nn], pb[:, 0:nn], AF.Copy), reads=[rb], writes=[r_t["XL"][j]])
            yield
            if nxt is not None:
                s5_inproj(nxt)
            yield
            for j in range(2):
                c = hb * 2 + j
                rr = [r_t["XL"][j], r_small]
                P.op("act", act(XC[:, j, 0:npr], XL[:, j, 3:3 + npr], AF.Identity, bias=PV[:, 0, c:c + 1],
                                scale=CWV[:, c, 3:4]), reads=rr, writes=[r_t["XC"][j]])
                if ns:
                    P.op("act", act(XC[:, j, npr:N], XL[:, j, 3 + npr:3 + N], AF.Identity, bias=PV[:, 0, c:c + 1], scale=CWV[:, c, 3:4]),
                         reads=rr + [r_t["XC"][j]], writes=[r_t["XC"][j]])
            yield
            for k in range(3):
                if k:
                    yield
                for j in range(2):
                    c = hb * 2 + j
                    rr = [r_t["XL"][j], r_small]
                    P.op("dve", stt(XC[:, j, 0:npr], XL[:, j, k:k + npr], CWV[:, c, k:k + 1], XC[:, j, 0:npr], ALU.mult, ALU.add),
                         reads=rr + [r_t["XC"][j]], writes=[r_t["XC"][j]])
                    if ns:
                        xs_ = XC[:, j, npr:N]
                        P.op("dve", stt(xs_, CS[:, c, :, k], CWV[:, c, k:k + 1], xs_, ALU.mult, ALU.add),
                             reads=rr + [r_t["XC"][j]], writes=[r_t["XC"][j]])
            for j in range(2):
                c = hb * 2 + j
                if ns:
                    P.op("dve", lambda e, c=c: e.tensor_copy(out=OCS[:, c, :, 0:2], in_=CS[:, c, :, 1:3]),
                         reads=[r_small], writes=[r_out])
                    P.op("dve", lambda e, c=c, j=j: e.tensor_copy(out=OCS[:, c, :, 2], in_=XL[:, j, 3 + npr:3 + N]),
                         reads=[r_t["XL"][j]], writes=[r_out])
                P.op("dve", lambda e, j=j, c=c: e.tensor_copy(out=XLT[:, c, :], in_=XL[:, j, npr:npr + 3]),
                     reads=[r_t["XL"][j]], writes=[r_XLT])
                P.op("act", act(XCB[:, j, 0:N], XC[:, j, 0:N], AF.Copy), reads=[r_t["XC"][j]], writes=[r_XCB[j]])
            yield
            for j in range(2):
                c = hb * 2 + j
                for (n0, nn) in ntiles:
                    pr_, rr_ = bank()
                    pi_, ri_ = bank()
                    def mg(e, pr_=pr_, pi_=pi_, j=j, n0=n0, nn=nn, gb=gb):
                        ins = None
                        for k in range(2):
                            ins = e.matmul(out=pr_[:, 0:nn], lhsT=GW[:, gb, 0, k, j * 128:(j + 1) * 128],
                                           rhs=XCB[:, k, n0:n0 + nn], start=(k == 0), stop=(k == 1))
                        for k in range(2):
                            ins = e.matmul(out=pi_[:, 0:nn], lhsT=GW[:, gb, 1, k, j * 128:(j + 1) * 128],
                                           rhs=XCB[:, k, n0:n0 + nn], start=(k == 0), stop=(k == 1))
                        return ins
                    P.op("pe", mg, reads=[r_GW[gb]] + r_XCB, writes=[rr_, ri_])
                    P.op("act", act(TR[:, j, n0:n0 + nn], pr_[:, 0:nn], AF.Tanh, bias=HALFB[:, c:c + 1], scale=0.5),
                         reads=[rr_, r_small], writes=[r_t["TR"][j]])
                    P.op("act", act(TI[:, j, n0:n0 + nn], pi_[:, 0:nn], AF.Tanh, bias=HALFB[:, NCH + c:NCH + c + 1], scale=0.5),
                         reads=[ri_, r_small], writes=[r_t["TI"][j]])
                P.op("act", act(AA[:, j, 0:N], TR[:, j, 0:N], AF.Exp, bias=CNG[:, NCH + c:NCH + c + 1], scale=CNG[:, NCH + c:NCH + c + 1]),
                     reads=[r_t["TR"][j], r_small], writes=[r_t["AA"][j]])
                yield
                P.op("dve", stt(G1[:, j, 0:N], TI[:, j, 0:N], 1.0, XC[:, j, 0:N], ALU.add, ALU.mult),
                     reads=[r_t["TI"][j], r_t["XC"][j]], writes=[r_t["G1"][j]])
                P.op("act", act(A2[:, j, 0:N], TR[:, j, 0:N], AF.Exp, bias=CNG[:, c:c + 1], scale=CNG[:, c:c + 1]),
                     reads=[r_t["TR"][j], r_t["G1"][j], r_small], writes=[r_t["A2"][j]])
            for j in range(2):
                P.op("act", act(A2[:, j, 0:N], A2[:, j, 0:N], AF.Sqrt, bias=ONEB[:, 0:1], scale=-1.0),
                     reads=[r_t["A2"][j]], writes=[r_t["A2"][j]])
            yield
            for j in range(2):
                c = hb * 2 + j
                if g["first"]:
                    P.op("dve", lambda e, j=j: e.memset(A2[:, j, 0:1], 1.0), reads=[r_t["A2"][j]], writes=[r_t["A2"][j]])
                if g["mask"]:
                    P.op("dve", ts(A2[:, j, 0:1], A2[:, j, 0:1], FLG[:, 0:1], FLG[:, 1:2], ALU.mult, ALU.add),
                         reads=[r_t["A2"][j], r_small], writes=[r_t["A2"][j]])
                    P.op("dve", ts(AA[:, j, 0:1], AA[:, j, 0:1], FLG[:, 0:1], None, ALU.mult),
                         reads=[r_t["AA"][j], r_small], writes=[r_t["AA"][j]])
                P.op("dve", stt(G1[:, j, 0:N], G1[:, j, 0:N], 0.5, A2[:, j, 0:N], ALU.mult, ALU.mult),
                     reads=[r_t["G1"][j], r_t["A2"][j]], writes=[r_t["G1"][j]])
            yield
            for j in range(2):
                c = hb * 2 + j
                P.op("dve", lambda e, j=j, c=c: e.tensor_tensor_scan(out=HF[:, j, 0:npr], data0=AA[:, j, 0:npr], data1=G1[:, j, 0:npr],
                                                                initial=HC[:, c:c + 1], op0=ALU.mult, op1=ALU.add),
                     reads=[r_t["AA"][j], r_t["G1"][j], r_HC], writes=[r_t["HF"][j]])
            yield
            for j in range(2):
                c = hb * 2 + j
                P.op("dve", lambda e, j=j, c=c: e.tensor_copy(out=HC[:, c:c + 1], in_=HF[:, j, npr - 1:npr]),
                     reads=[r_t["HF"][j]], writes=[r_HC])
                if ns:
                    hs_ = HF[:, j, npr:N]
                    P.op("dve", tt(hs_, AA[:, j, npr:N], H0[:, c, :], ALU.mult), reads=[r_t["AA"][j], r_small], writes=[r_t["HF"][j]])
                    P.op("dve", tt(hs_, hs_, G1[:, j, npr:N], ALU.add), reads=[r_t["HF"][j], r_t["G1"][j]], writes=[r_t["HF"][j]])
                    P.op("dve", lambda e, c=c, hs_=hs_: e.tensor_copy(out=OHS[:, c, :], in_=hs_), reads=[r_t["HF"][j]], writes=[r_out])
                if full:
                    P.op("act", act(HS[:, c, 0:N], HF[:, j, 0:N], AF.Copy), reads=[r_t["HF"][j]], writes=[r_HS[c]])

        s5slot = [None]

        r_X4 = [Res("X4_%d" % t) for t in range(4)]
        tmp_last = [[]]

        lru_gen = [None]
        tick_ctr = [0]
        TICK = 10 if full else 6

        def dv(fn, after):
            m = P.op("dve", fn, after=[m for m in after if m is not None])
            tick_ctr[0] += 1
            if lru_gen[0] is not None and tick_ctr[0] % TICK == 0:
                try:
                    next(lru_gen[0])
                except StopIteration:
                    lru_gen[0] = None
            return m

        def s5_inproj(kt):
            if kt % 2 == 0:
                s5slot[0] = wslot()
                pinned[:] = [s5slot[0]]
                wload(s5slot[0], WR[:, s5slot[0], :, :], wview(w_in, 0, D, 2048 + (kt // 2) * 256, 256))
            s = s5slot[0]
            mc = kt % 2
            xs = kt % 2
            for (n0, nn) in ntiles:
                pb, rb = bank()
                def mm(e, pb=pb, s=s, mc=mc, n0=n0, nn=nn):
                    ins = None
                    for k in range(NCH):
                        ins = e.matmul(out=pb[:, 0:nn], lhsT=WR[:, s, k, mc * 128:(mc + 1) * 128],
                                       rhs=UT[:, k, n0:n0 + nn], start=(k == 0), stop=(k == NCH - 1))
                    return ins
                P.op("pe", mm, reads=[r_WR[s]] + r_UT, writes=[rb])
                P.op("act", act(XS[:, xs, n0:n0 + nn], pb[:, 0:nn], AF.Copy), reads=[rb], writes=[r_XS[xs]])

        def s5_pre(kt):
            xs = kt % 2
            j0 = kt * 4
            for t in range(4):
                j = j0 + t
                for (n0, nn) in ntiles:
                    pr_, rr_ = bank()
                    pi_, ri_ = bank()
                    def mb(e, pr_=pr_, pi_=pi_, j=j, xs=xs, n0=n0, nn=nn):
                        e.matmul(out=pr_[:, 0:nn], lhsT=WB[:, j, 0, :], rhs=XS[:, xs, n0:n0 + nn], start=True, stop=True)
                        return e.matmul(out=pi_[:, 0:nn], lhsT=WB[:, j, 1, :], rhs=XS[:, xs, n0:n0 + nn], start=True, stop=True)
                    P.op("pe", mb, reads=[r_WBCW, r_XS[xs]], writes=[rr_, ri_])
                    P.op("act", act(X4[:, t, 0, n0:n0 + nn], pr_[:, 0:nn], AF.Copy), reads=[rr_], writes=[r_X4[t]])
                    P.op("dve", lambda e, t=t, pi_=pi_, n0=n0, nn=nn: e.tensor_copy(out=X4[:, t, 1, n0:n0 + nn], in_=pi_[:, 0:nn]),
                         reads=[ri_], writes=[r_X4[t]])

        def s5_bk(kt):
            j0 = kt * 4
            C4 = S5C[:, j0:j0 + 4, :]
            if g["mask"]:
                P.op("dve", ts(C4, C4, FLG[:, 0:1], None, ALU.mult), reads=[r_S5C, r_small], writes=[r_S5C])
            prev_t = [[r_X4[t].w] + list(r_X4[t].r) for t in range(4)]
            carry_dep = [r_S5C.w]

            def batched(l, s_both, s_re, s_im, t_re, t_im, cnt, after):
                arp = LP2[:, l, j0:j0 + 4, 0:2].unsqueeze(3).broadcast_to([128, 4, 2, cnt])
                ai_ = LP2[:, l, j0:j0 + 4, 2].unsqueeze(2).broadcast_to([128, 4, cnt])
                nai_ = LP2[:, l, j0:j0 + 4, 3].unsqueeze(2).broadcast_to([128, 4, cnt])
                t1, t2, t3 = TM1[:, :, :, 0:cnt], TM2[:, :, 0:cnt], TM3[:, :, 0:cnt]
                aft = list(after) + tmp_last[0]
                m1 = dv(tt(t1, s_both, arp, ALU.mult), aft)
                m2 = dv(tt(t2, s_im, nai_, ALU.mult), aft)
                m3 = dv(tt(t3, s_re, ai_, ALU.mult), aft)
                c2 = dv(tt(t2, t2, TM1[:, :, 0, 0:cnt], ALU.add), [m1, m2])
                c3 = dv(tt(t3, t3, TM1[:, :, 1, 0:cnt], ALU.add), [m1, m3])
                a_re = dv(tt(t_re, t_re, t2, ALU.add), [c2] + list(after))
                a_im = dv(tt(t_im, t_im, t3, ALU.add), [c3] + list(after))
                tmp_last[0] = [a_re, a_im]
                return [a_re, a_im]

            def per_tile(l, sl_s, sl_t, prevs):
                ab = []
                for t in range(4):
                    ar = LP2[:, l, j0 + t, 0:1]
                    ab.append(dv(stt(X4[:, t, :, sl_t], X4[:, t, :, sl_s], ar, X4[:, t, :, sl_t], ALU.mult, ALU.add), prevs[t]))
                out = [[] for _ in range(4)]
                for t in range(4):
                    nai = LP2[:, l, j0 + t, 3:4]
                    out[t].append(dv(stt(X4[:, t, 0, sl_t], X4[:, t, 1, sl_s], nai, X4[:, t, 0, sl_t], ALU.mult, ALU.add), [ab[t]] + prevs[t]))
                for t in range(4):
                    ai = LP2[:, l, j0 + t, 2:3]
                    out[t].append(dv(stt(X4[:, t, 1, sl_t], X4[:, t, 0, sl_s], ai, X4[:, t, 1, sl_t], ALU.mult, ALU.add), [ab[t]] + prevs[t]))
                return out

            BL = 3
            allprev = [m for p in prev_t for m in p]
            d_all = batched(0, C4, C4[:, :, 0:1], C4[:, :, 1:2], X4[:, :, 0, 0:1], X4[:, :, 1, 0:1], 1, allprev + carry_dep)
            prevs = [list(d_all) for _ in range(4)]
            for l in range(L):
                d = 1 << l
                sl_t = slice(2 * d - 1, npr, 2 * d)
                sl_s = slice(d - 1, npr - d, 2 * d)
                cnt = npr // (2 * d)
                if l < BL:
                    prevs = per_tile(l, sl_s, sl_t, prevs)
                else:
                    flat = [m for p in prevs for m in p]
                    d_all = batched(l, X4[:, :, :, sl_s], X4[:, :, 0, sl_s], X4[:, :, 1, sl_s], X4[:, :, 0, sl_t], X4[:, :, 1, sl_t], cnt, flat)
                    prevs = [list(d_all) for _ in range(4)]
            if full:
                for l in range(L - 2, -1, -1):
                    d = 1 << l
                    sl_t = slice(3 * d - 1, npr, 2 * d)
                    sl_s = slice(2 * d - 1, npr - d, 2 * d)
                    cnt = npr // (2 * d) - 1
                    if l < BL:
                        prevs = per_tile(l, sl_s, sl_t, prevs)
                    else:
                        flat = [m for p in prevs for m in p]
                        d_all = batched(l, X4[:, :, :, sl_s], X4[:, :, 0, sl_s], X4[:, :, 1, sl_s], X4[:, :, 0, sl_t], X4[:, :, 1, sl_t], cnt, flat)
                        prevs = [list(d_all) for _ in range(4)]
            if ns:
                flat = [m for p in prevs for m in p]
                d_all = batched(0, None, None, None, None, None, 0, flat) if False else None
                cnt = ns
                arp = LP2[:, 0, j0:j0 + 4, 0:1].broadcast_to([128, 4, cnt])
                ai_ = LP2[:, 0, j0:j0 + 4, 2:3].broadcast_to([128, 4, cnt])
                nai_ = LP2[:, 0, j0:j0 + 4, 3:4].broadcast_to([128, 4, cnt])
                sR, sI = S5R[:, j0:j0 + 4, :], S5I[:, j0:j0 + 4, :]
                t1r, t1i = TM1[:, :, 0, 0:cnt], TM1[:, :, 1, 0:cnt]
                t2, t3 = TM2[:, :, 0:cnt], TM3[:, :, 0:cnt]
                aft = flat + tmp_last[0] + [r_small.w]
                m1 = dv(tt(t1r, sR, arp, ALU.mult), aft)
                m1b = dv(tt(t1i, sI, arp, ALU.mult), aft)
                m2 = dv(tt(t2, sI, nai_, ALU.mult), aft)
                m3 = dv(tt(t3, sR, ai_, ALU.mult), aft)
                c2 = dv(tt(t2, t2, t1r, ALU.add), [m1, m2])
                c3 = dv(tt(t3, t3, t1i, ALU.add), [m1b, m3])
                a_re = dv(tt(X4[:, :, 0, npr:N], X4[:, :, 0, npr:N], t2, ALU.add), [c2] + flat)
                a_im = dv(tt(X4[:, :, 1, npr:N], X4[:, :, 1, npr:N], t3, ALU.add), [c3] + flat)
                tmp_last[0] = [a_re, a_im]
                prevs = [p + [a_re, a_im] for p in prevs]
            for t in range(4):
                last = max(m[1] for m in prevs[t] if m[0] == "dve")
                r_X4[t].w = ("dve", last); r_X4[t].r = []
            if lru_gen[0] is not None:
                for _ in lru_gen[0]:
                    pass
                lru_gen[0] = None
            P.op("dve", lambda e: e.tensor_copy(out=C4, in_=X4[:, :, :, npr - 1]), reads=r_X4, writes=[r_S5C])
            if ns:
                P.op("dve", lambda e: e.tensor_copy(out=OSS[:, 0, j0:j0 + 4, :], in_=X4[:, :, 0, npr:N]), reads=r_X4, writes=[r_out])
                P.op("dve", lambda e: e.tensor_copy(out=OSS[:, 1, j0:j0 + 4, :], in_=X4[:, :, 1, npr:N]), reads=r_X4, writes=[r_out])

        def s5_post_a(kt):
            j0 = kt * 4
            ybanks = [ybank(i) for i in range(len(ntiles))]
            if True:
                for t in range(4):
                    j = j0 + t
                    P.op("act", act(XRB[:, 0, 0:N], X4[:, t, 0, 0:N], AF.Copy), reads=[r_X4[t]], writes=[r_XCB[0]])
                    P.op("dve", lambda e, t=t: e.tensor_copy(out=XRB[:, 1, 0:N], in_=X4[:, t, 1, 0:N]), reads=[r_X4[t]], writes=[r_XCB[1]])
                    for ti_n, (n0, nn) in enumerate(ntiles):
                        yb, ryb = ybanks[ti_n]
                        def mc_(e, yb=yb, j=j, t=t, n0=n0, nn=nn):
                            e.matmul(out=yb[:, 0:nn], lhsT=CW[:, j, 0, :], rhs=XRB[:, 0, n0:n0 + nn], start=(t == 0), stop=False)
                            return e.matmul(out=yb[:, 0:nn], lhsT=CW[:, j, 1, :], rhs=XRB[:, 1, n0:n0 + nn], start=False, stop=(t == 3))
                        P.op("pe", mc_, reads=[r_WBCW] + r_XCB, writes=[ryb])

        def s5_post_b(kt):
            xs = kt % 2
            ybanks = [ybank(i) for i in range(len(ntiles))]
            if True:
                for ti_n, (n0, nn) in enumerate(ntiles):
                    yb, ryb = ybanks[ti_n]
                    P.op("dve", stt(YT[:, n0:n0 + nn], XS[:, xs, n0:n0 + nn], SD[:, kt:kt + 1], yb[:, 0:nn], ALU.mult, ALU.add),
                         reads=[ryb, r_XS[xs], r_small], writes=[r_Y])
                yy, y2, y3 = YT[:, 0:N], Y2[:, 0:N], Y3[:, 0:N]
                P.op("act", act(y2, yy, AF.Square), reads=[r_Y], writes=[r_Y])
                yield
                P.op("dve", ts(y2, y2, GELU_C1, 1.0, ALU.mult, ALU.add), reads=[r_Y], writes=[r_Y])
                P.op("dve", tt(y2, y2, yy, ALU.mult), reads=[r_Y], writes=[r_Y])
                P.op("act", act(y3, y2, AF.Tanh, scale=GELU_C0), reads=[r_Y], writes=[r_Y])
                yield
                P.op("dve", stt(y3, y3, 1.0, yy, ALU.add, ALU.mult), reads=[r_Y], writes=[r_Y])
                P.op("act", act(VT[:, kt, 0:N], y3, AF.Copy, scale=0.5), reads=[r_Y], writes=[r_VT[kt]])
                yield

        import itertools
        s5_inproj(0)
        pending = None
        for kt in range(8):
            s5_pre(kt)
            lg = lru_block(kt, kt + 1 if kt < 7 else None)
            next(lg)
            convert_some(5)
            lru_gen[0] = itertools.chain(pending, lg) if pending is not None else lg
            s5_bk(kt)
            if full:
                s5_post_a(kt)
                pending = s5_post_b(kt)
        if pending is not None:
            for _ in pending:
                pass
        P.barrier()
        chk(g["name"] + "C")

        if not full:
            return

        pinned[:] = []
        back_ptr[0] = 0
        convert_some(1000)
        off[0] = 0
        MA = tl(4, N); TG = tl(2, N); TQ = tl(2, N)
        r_MA = [Res("MA%d" % k) for k in range(4)]
        r_TG = [Res("TG0"), Res("TG1")]
        r_TQ = [Res("TQ0"), Res("TQ1")]
        tcount = [0]
        for sl in range(8):
            s1 = wslot(); bload(s1)
            s2 = wslot(); bload(s2)
            for mc in range(2):
                for (n0, nn) in ntiles:
                    pg, rg = bank()
                    py, ry = bank()
                    def mm(e, pg=pg, py=py, mc=mc, n0=n0, nn=nn, s1=s1, s2=s2):
                        ins = None
                        for k in range(NCH):
                            ins = e.matmul(out=pg[:, 0:nn], lhsT=WR[:, s1, k, mc * 128:(mc + 1) * 128], rhs=UT[:, k, n0:n0 + nn],
                                           start=(k == 0), stop=(k == NCH - 1))
                        for k in range(NCH):
                            ins = e.matmul(out=py[:, 0:nn], lhsT=WR[:, s2, k, mc * 128:(mc + 1) * 128], rhs=HS[:, k, n0:n0 + nn],
                                           start=(k == 0), stop=(k == NCH - 1))
                        return ins
                    P.op("pe", mm, reads=[r_WR[s1], r_WR[s2]] + r_UT + r_HS, writes=[rg, ry])
                    tb = tcount[0] % 2; tcount[0] += 1
                    P.op("act", act(TG[:, tb, 0:nn], pg[:, 0:nn], AF.Tanh, scale=0.5), reads=[rg], writes=[r_TG[tb]])
                    P.op("dve", stt(MA[:, mc, n0:n0 + nn], TG[:, tb, 0:nn], 1.0, py[:, 0:nn], ALU.add, ALU.mult),
                         reads=[r_TG[tb], ry], writes=[r_MA[mc]])
            s3 = wslot(); bload(s3)
            s4 = wslot(); bload(s4)
            for mc in range(2):
                m = sl * 2 + mc
                for (n0, nn) in ntiles:
                    pg, rg = bank()
                    pv_, rv = bank()
                    pw, rw = bank()
                    def mm(e, pg=pg, pv_=pv_, pw=pw, mc=mc, n0=n0, nn=nn, s3=s3, s4=s4):
                        ins = None
                        for k in range(NCH):
                            ins = e.matmul(out=pg[:, 0:nn], lhsT=WR[:, s3, k, mc * 128:(mc + 1) * 128], rhs=UT[:, k, n0:n0 + nn],
                                           start=(k == 0), stop=(k == NCH - 1))
                        for k in range(8):
                            ins = e.matmul(out=pv_[:, 0:nn], lhsT=WR[:, s4, k, mc * 128:(mc + 1) * 128], rhs=VT[:, k, n0:n0 + nn],
                                           start=(k == 0), stop=(k == 7))
                        for k in range(8):
                            ins = e.matmul(out=pw[:, 0:nn], lhsT=WR[:, s4, 8 + k, mc * 128:(mc + 1) * 128], rhs=VT[:, k, n0:n0 + nn],
                                           start=(k == 0), stop=(k == 7))
                        return ins
                    P.op("pe", mm, reads=[r_WR[s3], r_WR[s4]] + r_UT + r_VT, writes=[rg, rv, rw])
                    tb = tcount[0] % 2; tcount[0] += 1
                    P.op("act", act(TG[:, tb, 0:nn], pg[:, 0:nn], AF.Tanh, scale=0.5), reads=[rg], writes=[r_TG[tb]])
                    P.op("act", act(TQ[:, tb, 0:nn], pw[:, 0:nn], AF.Tanh, scale=0.5), reads=[rw], writes=[r_TQ[tb]])
                    P.op("dve", stt(TQ[:, tb, 0:nn], TQ[:, tb, 0:nn], 1.0, pv_[:, 0:nn], ALU.add, ALU.mult),
                         reads=[r_TQ[tb], rv], writes=[r_TQ[tb]])
                    P.op("dve", stt(TQ[:, tb, 0:nn], TG[:, tb, 0:nn], 1.0, TQ[:, tb, 0:nn], ALU.add, ALU.mult),
                         reads=[r_TQ[tb], r_TG[tb]], writes=[r_TQ[tb]])
                    P.op("dve", stt(TQ[:, tb, 0:nn], TQ[:, tb, 0:nn], 0.5, MA[:, mc, n0:n0 + nn], ALU.mult, ALU.add),
                         reads=[r_TQ[tb], r_MA[mc]], writes=[r_TQ[tb]])
                    P.op("act", act(MG[:, m, n0:n0 + nn], TQ[:, tb, 0:nn], AF.Copy, scale=0.5), reads=[r_TQ[tb]], writes=[r_MG[m]])
        P.barrier()
        chk(g["name"] + "D")

        for ti, (c0, n) in enumerate(ttiles):
            P.dma("sp", xsem[ti], lambda e, ti=ti, c0=c0, n=n: e.dma_start(out=XP1[0:n, ti, :], in_=xf[row0 + c0:row0 + c0 + n, :]),
                  writes=[r_XP1[ti]])
        for nb in range(8):
            s = wslot(); bload(s)
            for ti, (c0, n) in enumerate(ttiles):
                pb, rb = bank()
                def mm(e, pb=pb, s=s, c0=c0, n=n):
                    ins = None
                    for k in range(NCH):
                        ins = e.matmul(out=pb[0:n, 0:256], lhsT=MG[:, k, c0:c0 + n], rhs=WR[:, s, k, :], start=(k == 0), stop=(k == NCH - 1))
                    return ins
                P.op("pe", mm, reads=[r_WR[s]] + r_MG, writes=[rb])
                dst = XP1[0:n, ti, nb * 256:(nb + 1) * 256]
                P.op("dve", tt(dst, pb[0:n, 0:256], dst, ALU.add), reads=[rb, r_XP1[ti]], writes=[r_XP1[ti]])
        P.barrier()
        chk(g["name"] + "E")

        JK2 = HM[:, 40:44, :].rearrange("p a b -> p (a b)")
        r_JK2 = Res("JK2")
        r_XTN = r_XTN_g
        for ti, (c0, n) in enumerate(ttiles):
            xt = XP1[0:n, ti, :]
            ssq = SM[0:n, 8 + ti:9 + ti]
            P.op("act", act(JK2[0:n, 0:D], xt, AF.Square, accum_out=ssq), reads=[r_XP1[ti]], writes=[r_JK2, r_SM])
            P.op("act", act(ssq, ssq, AF.Sqrt, bias=EPSB[0:n, :], scale=1.0 / D), reads=[r_SM], writes=[r_SM])
            P.op("dve", lambda e, ssq=ssq: e.reciprocal(out=ssq, in_=ssq), reads=[r_SM], writes=[r_SM])
            xn = XTN[0:n, :]
            P.op("act", act(xn, xt, AF.Identity, scale=ssq), reads=[r_SM, r_XP1[ti]], writes=[r_XTN])
            for k4 in range(4):
                pb, rb = bank()
                def tr(e, pb=pb, xn=xn, k4=k4, n=n):
                    ins = None
                    for kk in range(4):
                        k = k4 * 4 + kk
                        ins = e.transpose(out=pb[:, kk * 128:kk * 128 + n], in_=xn[:, k * 128:(k + 1) * 128], identity=IDN[0:n, 0:n])
                    return ins
                P.op("pe", tr, reads=[r_XTN], writes=[rb])
                for kk in range(4):
                    k = k4 * 4 + kk
                    if True:
                        P.op("dve", ts(UT[:, k, c0:c0 + n], pb[:, kk * 128:kk * 128 + n], PV[:, 5, k:k + 1], None, ALU.mult),
                             reads=[rb], writes=[r_UT[k]])
                    else:
                        P.op("act", act(UT[:, k, c0:c0 + n], pb[:, kk * 128:kk * 128 + n], AF.Identity, scale=PV[:, 5, k:k + 1]),
                             reads=[rb], writes=[r_UT[k]])
        P.barrier()
        chk(g["name"] + "F")

        r_TGF = [Res("TGF0"), Res("TGF1")]
        tcount[0] = 0
        for f in range(44):
            fi = f % 2
            if fi == 0:
                sg = wslot()
                bload(sg)
                su = wslot()
                bload(su)
            for (n0, nn) in ntiles:
                pg, rg = bank()
                pu, ru = bank()
                def mm(e, pg=pg, pu=pu, sg=sg, su=su, fi=fi, n0=n0, nn=nn):
                    ins = None
                    for k in range(NCH):
                        ins = e.matmul(out=pg[:, 0:nn], lhsT=WR[:, sg, k, fi * 128:(fi + 1) * 128], rhs=UT[:, k, n0:n0 + nn], start=(k == 0), stop=(k == NCH - 1))
                    for k in range(NCH):
                        ins = e.matmul(out=pu[:, 0:nn], lhsT=WR[:, su, k, fi * 128:(fi + 1) * 128], rhs=UT[:, k, n0:n0 + nn], start=(k == 0), stop=(k == NCH - 1))
                    return ins
                P.op("pe", mm, reads=[r_WR[sg], r_WR[su]] + r_UT, writes=[rg, ru])
                tb = tcount[0] % 2; tcount[0] += 1
                P.op("act", act(TGF[:, tb, 0:nn], pg[:, 0:nn], AF.Tanh, scale=0.5), reads=[rg], writes=[r_TGF[tb]])
                P.op("dve", stt(TGF[:, tb, 0:nn], TGF[:, tb, 0:nn], 1.0, pg[:, 0:nn], ALU.add, ALU.mult), reads=[rg, r_TGF[tb]], writes=[r_TGF[tb]])
                P.op("dve", stt(HM[:, f, n0:n0 + nn], TGF[:, tb, 0:nn], 0.5, pu[:, 0:nn], ALU.mult, ALU.mult), reads=[ru, r_TGF[tb]], writes=[r_HM[f]])

        fgroups = [(0, 16), (16, 16), (32, 12)]
        for nb in range(8):
            banks = [bank() for _ in ttiles]
            for (f0, nf) in fgroups:
                s = wslot()
                bload(s)
                def mm(e, s=s, f0=f0, nf=nf, banks=banks):
                    ins = None
                    for fk in range(nf):
                        f = f0 + fk
                        for ti, (c0, n) in enumerate(ttiles):
                            ins = e.matmul(out=banks[ti][0][0:n, 0:256], lhsT=HM[:, f, c0:c0 + n], rhs=WR[:, s, fk, :],
                                           start=(f == 0), stop=(f == 43))
                    return ins
                P.op("pe", mm, reads=[r_WR[s]] + r_HM, writes=[b[1] for b in banks])
            for ti, (c0, n) in enumerate(ttiles):
                dst = XP1[0:n, ti, nb * 256:(nb + 1) * 256]
                P.op("dve", tt(dst, banks[ti][0][0:n, 0:256], dst, ALU.add), reads=[banks[ti][1], r_XP1[ti]], writes=[r_XP1[ti]])

        P.dma("sp", isem, lambda e: e.dma_start(out=XTN[:], in_=gfin_d[:, :]), writes=[r_XTN] + r_TGF)
        for ti, (c0, n) in enumerate(ttiles):
            xt = XP1[0:n, ti, :]
            ssq = SM[0:n, 16 + ti:17 + ti]
            P.op("act", act(JK2[0:n, 0:D], xt, AF.Square, accum_out=ssq), reads=[r_XP1[ti]], writes=[r_JK2, r_SM])
            P.op("act", act(ssq, ssq, AF.Sqrt, bias=EPSB[0:n, :], scale=1.0 / D), reads=[r_SM], writes=[r_SM])
            P.op("dve", lambda e, ssq=ssq: e.reciprocal(out=ssq, in_=ssq), reads=[r_SM], writes=[r_SM])
            P.op("dve", stt(xt, xt, ssq, XTN[0:n, :], ALU.mult, ALU.mult), reads=[r_SM, r_XP1[ti], r_XTN], writes=[r_XP1[ti]])
            orow = row0 - 1024 + c0
            P.dma("sp", osem, lambda e, xt=xt, orow=orow, n=n: e.dma_start(out=o_y[orow:orow + n, :], in_=xt), reads=[r_XP1[ti]])
        P.barrier()


    stop = os.environ.get("KSTOP", "")
    class _Stop(Exception):
        pass
    def chk(tag):
        if stop and tag == stop:
            raise _Stop()
    try:
        chk("setup")
        for g_ in groups:
            do_group(g_, chk)
    except _Stop:
        pass

    def st(dst, src, res):
        P.dma("sp", osem, lambda e: e.dma_start(out=dst, in_=src), reads=res)
    st(o_cp[:, :], XLT[:].rearrange("p a b -> p (a b)"), [r_XLT])
    st(o_hp[:, :], HC[:], [r_HC])
    st(o_sp[:, :], S5C[:].rearrange("p a b -> p (a b)"), [r_S5C])
    st(o_cs[:, :], OCS[:].rearrange("p a b c -> p (a b c)"), [r_out])
    st(o_hs[:, :], OHS[:].rearrange("p a b -> p (a b)"), [r_out])
    st(o_ss[:, :], OSS[:].rearrange("p a b c -> p (a b c)"), [r_out])
    fin = P.dcnt[osem]
    P.q["sp"].append(lambda e: e.wait_ge(P.sems[osem], fin))

    with nc.Block() as block:
        @block.tensor
        def _(e):
            for f in P.q["pe"]:
                f(e)

        @block.scalar
        def _(e):
            for f in P.q["act"]:
                f(e)

        @block.vector
        def _(e):
            for f in P.q["dve"]:
                f(e)

        @block.gpsimd
        def _(e):
            for f in P.q["pool"]:
                f(e)

        @block.sync
        def _(e):
            for f in P.q["sp"]:
                f(e)
    es.close()
    return nc


_NC = None


def kernel(**inp):
    global _NC
    f32 = np.float32
    g = lambda k: np.asarray(inp[k], dtype=f32)
    xp = g("x_prompt"); xs = g("x_sample").reshape(128, D)
    conv = g("state_lru_conv")[0]; h0 = g("state_lru_h")[0]
    s5r = g("state_s5_re")[0].reshape(128, 4096); s5i = g("state_s5_im")[0].reshape(128, 4096)
    ident = np.eye(128, dtype=f32)
    maskc = np.zeros((128, 4, 128), f32)
    for jl in range(4):
        for gl in range(2):
            g8 = 2 * jl + gl
            maskc[g8 * 16:(g8 + 1) * 16, jl, gl * 64:(gl + 1) * 64] = 1.0
    maskc = maskc.reshape(128, 512)
    def fm16(v):
        return np.ascontiguousarray(v.reshape(16, 128).T)
    def st32(v):
        return np.ascontiguousarray(v.reshape(32, 128).T)
    ldt = np.repeat(g("s5_log_dt")[0], 64)
    pvec = np.zeros((128, 7, 16), f32)
    pvec[:, 0] = fm16(g("lru_conv_b")[0]); pvec[:, 1] = fm16(g("lru_ba")[0].reshape(-1)); pvec[:, 2] = fm16(g("lru_bx")[0].reshape(-1))
    pvec[:, 3] = fm16(g("lru_lambda")[0]); pvec[:, 4] = fm16(g("norm_mix_g")[0]); pvec[:, 5] = fm16(g("norm_ffn_g")[0])
    convw = np.ascontiguousarray(g("lru_conv_w")[0].reshape(4, 16, 128).transpose(2, 1, 0)).reshape(128, 64)
    bre = np.ascontiguousarray(g("s5_b_re")[0].reshape(32, 128, 16).transpose(1, 0, 2)).reshape(128, 512)
    bim = np.ascontiguousarray(g("s5_b_im")[0].reshape(32, 128, 16).transpose(1, 0, 2)).reshape(128, 512)
    cre = np.ascontiguousarray(g("s5_c_re")[0].reshape(8, 128, 64).transpose(1, 0, 2)).reshape(128, 512)
    cim = np.ascontiguousarray(g("s5_c_im")[0].reshape(8, 128, 64).transpose(1, 0, 2)).reshape(128, 512)
    s5d = np.ascontiguousarray(g("s5_d")[0].reshape(8, 128).T)
    gfin = np.ascontiguousarray(np.broadcast_to(g("norm_final_g")[None, :], (128, D)))
    shared = dict(
        ident=ident, maskc=maskc, ldt=st32(ldt), lre=st32(g("s5_lambda_re")[0].reshape(-1)), lim=st32(g("s5_lambda_im")[0].reshape(-1)),
        bre=bre, bim=bim, cre=cre, cim=cim, s5d=s5d, convw=convw, pvec=pvec.reshape(128, 112), gfin=gfin,
        w_in=g("w_in")[0], lru_wa=g("lru_wa")[0].reshape(2048, 256), lru_wx=g("lru_wx")[0].reshape(2048, 256),
        lru_proj=g("lru_proj")[0], s5_glu_wv=g("s5_glu_wv")[0], s5_glu_wg=g("s5_glu_wg")[0], w_out=g("w_out")[0],
        ffn_w_gate=g("ffn_w_gate")[0], ffn_w_up=g("ffn_w_up")[0], ffn_w_down=g("ffn_w_down")[0],
    )
    in_maps = []
    for c in range(8):
        b, half = c // 2, c % 2
        xf = np.zeros((2064, D), f32)
        if half == 1:
            xf[0:1024] = xp[b, 0:1024]
        xf[1024:2048] = xp[b, half * 1024:(half + 1) * 1024]
        xf[2048:2064] = xs[c * 16:(c + 1) * 16]
        flag = np.zeros((128, 2), f32); flag[:, 0] = float(half); flag[:, 1] = 1.0 - float(half)
        sl = slice(c * 16, (c + 1) * 16)
        csT = np.ascontiguousarray(conv[sl].reshape(16, 3, 16, 128).transpose(3, 2, 0, 1)).reshape(128, 768)
        h0T = np.ascontiguousarray(h0[sl].reshape(16, 16, 128).transpose(2, 1, 0)).reshape(128, 256)
        s5rT = np.ascontiguousarray(s5r[sl].reshape(16, 32, 128).transpose(2, 1, 0)).reshape(128, 512)
        s5iT = np.ascontiguousarray(s5i[sl].reshape(16, 32, 128).transpose(2, 1, 0)).reshape(128, 512)
        m = dict(shared); m.update(xf=xf, flag=flag, csT=csT, h0T=h0T, s5rT=s5rT, s5iT=s5iT)
        in_maps.append(m)
    if _NC is None:
        _NC = build_program()
    res = run_bass_kernel_spmd(_NC, in_maps, core_ids=list(range(8)))
    R = res.results
    y_prompt = np.zeros((4, 2048, D), f32); y_sample = np.zeros((128, 1, D), f32)
    convp = np.zeros((1, 4, 3, D), f32); hp = np.zeros((1, 4, D), f32)
    srp = np.zeros((1, 4, 64, 64), f32); sip = np.zeros((1, 4, 64, 64), f32)
    convs = np.zeros((1, 128, 3, D), f32); hs = np.zeros((1, 128, D), f32)
    srs = np.zeros((1, 128, 64, 64), f32); sis = np.zeros((1, 128, 64, 64), f32)
    for c in range(8):
        b, half = c // 2, c % 2
        r = R[c]
        y_prompt[b, half * 1024:(half + 1) * 1024] = r["o_y"][0:1024]
        y_sample[c * 16:(c + 1) * 16, 0] = r["o_y"][1024:1040]
        sl = slice(c * 16, (c + 1) * 16)
        convs[0, sl] = r["o_cs"].reshape(128, 16, 16, 3).transpose(2, 3, 1, 0).reshape(16, 3, D)
        hs[0, sl] = r["o_hs"].reshape(128, 16, 16).transpose(2, 1, 0).reshape(16, D)
        ss = r["o_ss"].reshape(128, 2, 32, 16)
        srs[0, sl] = ss[:, 0].transpose(2, 1, 0).reshape(16, 64, 64)
        sis[0, sl] = ss[:, 1].transpose(2, 1, 0).reshape(16, 64, 64)
        if half == 1:
            convp[0, b] = r["o_cp"].reshape(128, 16, 3).transpose(2, 1, 0).reshape(3, D)
            hp[0, b] = r["o_hp"].T.reshape(D)
            sp = r["o_sp"].reshape(128, 32, 2)
            srp[0, b] = sp[:, :, 0].T.reshape(64, 64)
            sip[0, b] = sp[:, :, 1].T.reshape(64, 64)
    return (y_prompt, y_sample, convp, hp, srp, sip, convs, hs, srs, sis)
```

```python
import math, os
from contextlib import ExitStack
import numpy as np
import concourse.bass as bass
import concourse.mybir as mybir
from concourse.bass_utils import run_bass_kernel_spmd

F32 = mybir.dt.float32
BF16 = mybir.dt.bfloat16
I32 = mybir.dt.int32
AF = mybir.ActivationFunctionType
ALU = mybir.AluOpType

D = 2048
DFF = 5632
NCH = 16
NS = 16
NT = 528
SAME_SYNC = True
GELU_C0 = math.sqrt(2.0 / math.pi)
GELU_C1 = 0.044715


class Res:
    __slots__ = ("name", "w", "r")

    def __init__(self, name):
        self.name = name
        self.w = None
        self.r = []


class Prog:
    ENG = ["pe", "act", "dve", "pool", "sp"]

    def __init__(self, nc, es):
        self.nc = nc
        self.es = es
        self.q = {e: [] for e in self.ENG}
        self.cnt = {e: 0 for e in self.ENG}
        self.sems = {}
        self.seen = {e: {} for e in self.ENG}
        self.dcnt = {}
        for e in ["pe", "act", "dve"]:
            self.sems[e] = es.enter_context(nc.semaphore("s_" + e))

    def dma_sem(self, name):
        self.sems[name] = self.es.enter_context(self.nc.semaphore(name))
        self.dcnt[name] = 0
        return name

    def _waits(self, eng, reads, writes, after=()):
        need = {}
        def add(m):
            if m is None:
                return
            k, v = m
            if k == eng and not SAME_SYNC:
                return
            if v > need.get(k, 0):
                need[k] = v
        for m in after:
            add(m)
        for r in reads:
            add(r.w)
        for w in writes:
            add(w.w)
            for m in w.r:
                add(m)
        out = []
        for k, v in need.items():
            if self.seen[eng].get(k, 0) >= v:
                continue
            self.seen[eng][k] = v
            out.append((k, v))
        return out

    def _push_waits(self, eng, waits):
        for k, v in waits:
            sem = self.sems[k]
            self.q[eng].append(lambda e, sem=sem, v=v: e.wait_ge(sem, v))

    def op(self, eng, fn, reads=(), writes=(), after=()):
        waits = self._waits(eng, reads, writes, after)
        self._push_waits(eng, waits)
        self.cnt[eng] += 1
        c = self.cnt[eng]
        sem = self.sems[eng]
        self.q[eng].append(lambda e, fn=fn, sem=sem: fn(e).then_inc(sem, 1))
        m = (eng, c)
        for r in reads:
            r.r.append(m)
        for w in writes:
            w.w = m
            w.r = []
        return m

    def dma(self, eng, semname, fn, reads=(), writes=()):
        waits = self._waits(eng, reads, writes)
        self._push_waits(eng, waits)
        self.dcnt[semname] += 16
        v = self.dcnt[semname]
        sem = self.sems[semname]
        self.q[eng].append(lambda e, fn=fn, sem=sem: fn(e).then_inc(sem, 16))
        m = (semname, v)
        for r in reads:
            r.r.append(m)
        for w in writes:
            w.w = m
            w.r = []

    def barrier(self, engs=("pe", "act", "dve", "sp")):
        for e in engs:
            waits = []
            for k in ["pe", "act", "dve"]:
                v = self.cnt[k]
                if k == e and not SAME_SYNC:
                    continue
                if v > 0 and self.seen[e].get(k, 0) < v:
                    self.seen[e][k] = v
                    waits.append((k, v))
            for k, v in self.dcnt.items():
                if k.startswith("w"):
                    continue
                if v > 0 and self.seen[e].get(k, 0) < v:
                    self.seen[e][k] = v
                    waits.append((k, v))
            self._push_waits(e, waits)


def build_program():
    nc = bass.Bass("TRN2", target_bir_lowering=False)
    es = ExitStack()

    def din(name, shape, dt=F32):
        return nc.dram_tensor(name, list(shape), dt, kind="ExternalInput").ap()

    def dout(name, shape):
        return nc.dram_tensor(name, list(shape), F32, kind="ExternalOutput").ap()

    xf = din("xf", [2064, D])
    flag_d = din("flag", [128, 2])
    ident_d = din("ident", [128, 128])
    maskc_d = din("maskc", [128, 4 * 128])
    cs_d = din("csT", [128, NCH * NS * 3])
    h0_d = din("h0T", [128, NCH * NS])
    s5r_d = din("s5rT", [128, 32 * NS])
    s5i_d = din("s5iT", [128, 32 * NS])
    ldt_d = din("ldt", [128, 32])
    lre_d = din("lre", [128, 32])
    lim_d = din("lim", [128, 32])
    bre_d = din("bre", [128, 32 * 16])
    bim_d = din("bim", [128, 32 * 16])
    cre_d = din("cre", [128, 8 * 64])
    cim_d = din("cim", [128, 8 * 64])
    sd_d = din("s5d", [128, 8])
    cw_d = din("convw", [128, NCH * 4])
    pv_d = din("pvec", [128, 7 * NCH])
    gfin_d = din("gfin", [128, D])
    w_in = din("w_in", [D, 7168])
    lru_wa = din("lru_wa", [8 * 256, 256])
    lru_wx = din("lru_wx", [8 * 256, 256])
    lru_proj = din("lru_proj", [D, D])
    wv_d = din("s5_glu_wv", [1024, D])
    wg_d = din("s5_glu_wg", [1024, D])
    w_out = din("w_out", [D, D])
    w_gate = din("ffn_w_gate", [D, DFF])
    w_up = din("ffn_w_up", [D, DFF])
    w_down = din("ffn_w_down", [DFF, D])

    o_y = dout("o_y", [1040, D])
    o_cp = dout("o_cp", [128, NCH * 3])
    o_hp = dout("o_hp", [128, NCH])
    o_sp = dout("o_sp", [128, 64])
    o_cs = dout("o_cs", [128, NCH * NS * 3])
    o_hs = dout("o_hs", [128, NCH * NS])
    o_ss = dout("o_ss", [128, 2 * 32 * NS])

    def sb(name, shape, dt=F32):
        return es.enter_context(nc.sbuf_tensor(name, list(shape), dt))

    P = Prog(nc, es)
    PS = es.enter_context(nc.psum_tensor("ps", [128, 8, 512], F32))
    ps_res = [Res("ps%d" % i) for i in range(8)]
    ps_ptr = [0]

    def bank():
        b = ps_ptr[0] % 6
        ps_ptr[0] += 1
        return PS[:, b, :], ps_res[b]

    def ybank(i):
        return PS[:, 6 + i, :], ps_res[6 + i]

    UT = sb("UT", [128, NCH, NT], BF16)
    R2 = sb("R2", [128, 44 * NT], BF16)
    R1 = sb("R1", [128, 5 * D], F32)
    WR = sb("WR", [128, 4, 16, 256], BF16)
    WB = sb("WB", [128, 32, 2, 128], BF16)
    CW = sb("CW", [128, 32, 2, 128], BF16)
    IDN = sb("IDN", [128, 128])
    FLG = sb("FLG", [128, 2])
    CS = sb("CS", [128, NCH, NS, 3])
    H0 = sb("H0", [128, NCH, NS])
    S5R = sb("S5R", [128, 32, NS])
    S5I = sb("S5I", [128, 32, NS])
    SD = sb("SD", [128, 8])
    CWV = sb("CWV", [128, NCH, 4])
    PV = sb("PV", [128, 7, NCH])
    LP2 = sb("LP2", [128, 11, 32, 4])
    HC = sb("HC", [128, NCH])
    XLT = sb("XLT", [128, NCH, 3])
    S5C = sb("S5C", [128, 32, 2])
    OCS = sb("OCS", [128, NCH, NS, 3])
    OHS = sb("OHS", [128, NCH, NS])
    OSS = sb("OSS", [128, 2, 32, NS])
    XTN = sb("XTN", [128, D])
    TGF = XTN[:, 0:1024].rearrange("p (a b) -> p a b", b=512)
    GW = sb("GW", [128, 2, 2, 2, 256], BF16)
    r_GW = [Res("GW0"), Res("GW1")]
    r_XTN_g = Res("XTN")
    SM = sb("SM", [128, 64])
    HALFB = sb("HALFB", [128, 2 * NCH])
    CNG = sb("CNG", [128, 2 * NCH])

    r_UT = [Res("UT%d" % k) for k in range(NCH)]
    r_WR = [Res("WR%d" % k) for k in range(4)]
    r_WBCW = Res("WBCW")
    r_small = Res("small")
    r_HC = Res("HC")
    r_XLT = Res("XLT")
    r_S5C = Res("S5C")
    r_out = Res("outs")
    r_SM = Res("SM")

    HS = R2[:, 0:NCH * NT].rearrange("p (a b) -> p a b", b=NT)
    MG = R2[:, NCH * NT:2 * NCH * NT].rearrange("p (a b) -> p a b", b=NT)
    VT = R2[:, 32 * NT:40 * NT].rearrange("p (a b) -> p a b", b=NT)
    XCB = R2[:, 40 * NT:42 * NT].rearrange("p (a b) -> p a b", b=NT)
    XS = R2[:, 42 * NT:44 * NT].rearrange("p (a b) -> p a b", b=NT)
    HM = R2[:, :].rearrange("p (a b) -> p a b", b=NT)
    r_HS = [Res("HS%d" % k) for k in range(NCH)]
    r_MG = [Res("MG%d" % k) for k in range(NCH)]
    r_VT = [Res("VT%d" % k) for k in range(8)]
    r_XCB = [Res("XCB%d" % k) for k in range(2)]
    r_XS = [Res("XS%d" % k) for k in range(2)]
    r_HM = [Res("HM%d" % k) for k in range(44)]
    XP1 = R1[:, :].rearrange("p (a b) -> p a b", b=D)
    r_XP1 = [Res("XP1_%d" % k) for k in range(5)]

    def r1v(off, a, b):
        return R1[:, off:off + a * b].rearrange("p (a b) -> p a b", b=b)

    wsem = [P.dma_sem("w%d" % i) for i in range(4)]
    xsem = [P.dma_sem("x%d" % i) for i in range(5)]
    osem = P.dma_sem("o")
    gwsem = [P.dma_sem("wg0"), P.dma_sem("wg1")]
    isem = P.dma_sem("i")
    wr_ptr = [0]

    def wslot(skip=()):
        while True:
            s = wr_ptr[0] % 4
            wr_ptr[0] += 1
            if s not in skip:
                return s

    def wload(s, dst_ap, src_ap):
        P.dma("pool", wsem[s], lambda e: e.dma_start(out=dst_ap, in_=src_ap), writes=[r_WR[s]])

    def wview(w, r0, nr, c0, ncols):
        return w[r0:r0 + nr, c0:c0 + ncols].rearrange("(kt p) n -> p kt n", p=128)

    def ld(dst, src):
        P.dma("sp", isem, lambda e: e.dma_start(out=dst, in_=src), writes=[r_small])

    ld(IDN[:], ident_d[:, :])
    ld(FLG[:], flag_d[:, :])
    ld(CS[:].rearrange("p a b c -> p (a b c)"), cs_d[:, :])
    ld(H0[:].rearrange("p a b -> p (a b)"), h0_d[:, :])
    ld(S5R[:].rearrange("p a b -> p (a b)"), s5r_d[:, :])
    ld(S5I[:].rearrange("p a b -> p (a b)"), s5i_d[:, :])
    ld(SD[:], sd_d[:, :])
    ld(CWV[:].rearrange("p a b -> p (a b)"), cw_d[:, :])
    ld(PV[:].rearrange("p a b -> p (a b)"), pv_d[:, :])
    o = [0]

    def t1(n):
        v = R1[:, o[0]:o[0] + n]
        o[0] += n
        return v

    LDT = t1(32); LRE = t1(32); LIM = t1(32)
    BRE = t1(512); BIM = t1(512); CRE = t1(512); CIM = t1(512)
    MASKC = t1(512)
    ld(LDT, ldt_d[:, :]); ld(LRE, lre_d[:, :]); ld(LIM, lim_d[:, :])
    ld(BRE, bre_d[:, :]); ld(BIM, bim_d[:, :]); ld(CRE, cre_d[:, :]); ld(CIM, cim_d[:, :])
    ld(MASKC, maskc_d[:, :])
    rs = [r_small]

    def V(fn, reads=None, writes=None):
        P.op("dve", fn, reads=rs if reads is None else reads, writes=rs if writes is None else writes)

    def A(fn, reads=None, writes=None):
        P.op("act", fn, reads=rs if reads is None else reads, writes=rs if writes is None else writes)

    def ts(out, in0, s1, s2, op0, op1=None):
        if op1 is None:
            return lambda e: e.tensor_scalar(out=out, in0=in0, scalar1=s1, scalar2=None, op0=op0)
        return lambda e: e.tensor_scalar(out=out, in0=in0, scalar1=s1, scalar2=s2, op0=op0, op1=op1)

    def tt(out, in0, in1, op):
        return lambda e: e.tensor_tensor(out=out, in0=in0, in1=in1, op=op)

    def stt(out, in0, s, in1, op0, op1):
        return lambda e: e.scalar_tensor_tensor(out=out, in0=in0, scalar=s, in1=in1, op0=op0, op1=op1)

    def act(out, in_, func, bias=None, scale=None, accum_out=None):
        kw = {}
        if bias is not None:
            kw["bias"] = bias
        if scale is not None:
            kw["scale"] = scale
        if accum_out is not None:
            kw["accum_out"] = accum_out
        return lambda e: e.activation(out=out, in_=in_, func=func, **kw)

    def poly_exp(out, x, nterm, tmp):
        V(ts(out, x, 1.0 / nterm, 1.0, ALU.mult, ALU.add))
        for k in range(nterm - 1, 0, -1):
            V(tt(tmp, out, x, ALU.mult))
            V(ts(out, tmp, 1.0 / k, 1.0, ALU.mult, ALU.add))

    T = [t1(32) for _ in range(16)]
    DT, AL, TH, MAG, C_, S_, ABR, ABI, COR, COI = T[:10]
    X0, X1, X2, X3, X4, X5 = T[10:16]
    V(ts(X0, LDT, 1.0 / 32, None, ALU.mult))
    poly_exp(DT, X0, 9, X1)
    for _ in range(5):
        V(tt(DT, DT, DT, ALU.mult))
    V(tt(AL, LRE, DT, ALU.mult))
    V(tt(TH, LIM, DT, ALU.mult))
    poly_exp(MAG, AL, 7, X1)
    KI = sb("KI", [128, 32], I32)
    V(ts(X0, TH, 1.0 / (2 * math.pi), None, ALU.mult))
    V(lambda e: e.tensor_copy(out=KI[:], in_=X0))
    V(lambda e: e.tensor_copy(out=X1, in_=KI[:]))
    C1 = float(np.float32(2 * math.pi))
    C2 = 2 * math.pi - C1
    V(stt(X2, X1, -C1, TH, ALU.mult, ALU.add))
    V(stt(X2, X1, -C2, X2, ALU.mult, ALU.add))
    V(ts(X2, X2, 0.25, None, ALU.mult))
    V(tt(X3, X2, X2, ALU.mult))
    def poly_trig(out, q2, denoms):
        V(ts(out, q2, -1.0 / denoms[-1], 1.0, ALU.mult, ALU.add))
        for dd in reversed(denoms[:-1]):
            V(tt(X5, out, q2, ALU.mult))
            V(ts(out, X5, -1.0 / dd, 1.0, ALU.mult, ALU.add))
    poly_trig(X4, X3, [6.0, 20.0, 42.0, 72.0, 110.0, 156.0, 210.0])
    V(tt(S_, X4, X2, ALU.mult))
    poly_trig(C_, X3, [2.0, 12.0, 30.0, 56.0, 90.0, 132.0, 182.0])
    for _ in range(2):
        V(tt(X0, C_, C_, ALU.mult))
        V(tt(X1, S_, S_, ALU.mult))
        V(tt(X4, S_, C_, ALU.mult))
        V(tt(C_, X0, X1, ALU.subtract))
        V(ts(S_, X4, 2.0, None, ALU.mult))
    V(tt(ABR, MAG, C_, ALU.mult))
    V(tt(ABI, MAG, S_, ALU.mult))
    V(ts(X0, ABR, -1.0, None, ALU.add))
    V(tt(X1, LRE, LRE, ALU.mult))
    V(tt(X2, LIM, LIM, ALU.mult))
    V(tt(X1, X1, X2, ALU.add))
    V(lambda e: e.reciprocal(out=X1, in_=X1))
    V(tt(X2, X0, LRE, ALU.mult))
    V(tt(X3, ABI, LIM, ALU.mult))
    V(tt(X2, X2, X3, ALU.add))
    V(tt(COR, X2, X1, ALU.mult))
    V(tt(X2, ABI, LRE, ALU.mult))
    V(tt(X3, X0, LIM, ALU.mult))
    V(tt(X2, X2, X3, ALU.subtract))
    V(tt(COI, X2, X1, ALU.mult))
    V(lambda e: e.tensor_copy(out=LP2[:, 0, :, 0], in_=ABR))
    V(lambda e: e.tensor_copy(out=LP2[:, 0, :, 2], in_=ABI))
    for l in range(1, 11):
        pr, pi_ = LP2[:, l - 1, :, 0], LP2[:, l - 1, :, 2]
        V(tt(X0, pr, pr, ALU.mult))
        V(tt(X1, pi_, pi_, ALU.mult))
        V(tt(LP2[:, l, :, 0], X0, X1, ALU.subtract))
        V(tt(X2, pr, pi_, ALU.mult))
        V(ts(LP2[:, l, :, 2], X2, 2.0, None, ALU.mult))
    V(lambda e: e.tensor_copy(out=LP2[:, :, :, 1], in_=LP2[:, :, :, 0]))
    V(ts(LP2[:, :, :, 3], LP2[:, :, :, 2], -1.0, None, ALU.mult))
    BBR = t1(512); BBI = t1(512); TB = t1(512)
    def v3(x):
        return x.rearrange("p (a b) -> p a b", b=16)
    def bc(x):
        return x.unsqueeze(2).broadcast_to([128, 32, 16])
    V(tt(v3(BBR), v3(BRE), bc(COR), ALU.mult))
    V(tt(v3(TB), v3(BIM), bc(COI), ALU.mult))
    V(tt(BBR, BBR, TB, ALU.subtract))
    V(tt(v3(BBI), v3(BIM), bc(COR), ALU.mult))
    V(tt(v3(TB), v3(BRE), bc(COI), ALU.mult))
    V(tt(BBI, BBI, TB, ALU.add))
    Z = t1(32 * 128)
    Z4 = Z.rearrange("p (k j c) -> p k j c", k=8, j=4)
    for ri, BB in enumerate([BBR, BBI]):
        V(lambda e: e.memset(Z, 0.0))
        BB4 = BB.rearrange("p (k j h) -> p k j h", k=8, j=4)
        for jl in range(4):
            for gl in range(2):
                pp = slice(64 * gl, 64 * gl + 64)
                c0 = (2 * jl + gl) * 16
                V(lambda e, pp=pp, jl=jl, c0=c0, BB4=BB4: e.tensor_copy(out=Z4[pp, :, jl, c0:c0 + 16], in_=BB4[pp, :, jl, :]))
        for j in range(32):
            pb, rb = bank()
            P.op("pe", lambda e, j=j, pb=pb: e.transpose(out=pb[:, 0:128], in_=Z[:, j * 128:(j + 1) * 128], identity=IDN[:]),
                 reads=rs, writes=[rb])
            P.op("act", act(WB[:, j, ri, :], pb[:, 0:128], AF.Copy), reads=[rb], writes=[r_WBCW])
    Y = t1(128)
    for ri, CC in enumerate([CRE, CIM]):
        for j in range(32):
            kt, jl = j // 4, j % 4
            cc = CC[:, kt * 64:(kt + 1) * 64]
            mk = MASKC[:, jl * 128:(jl + 1) * 128]
            V(tt(Y[:, 0:64], cc, mk[:, 0:64], ALU.mult))
            V(tt(Y[:, 64:128], cc, mk[:, 64:128], ALU.mult))
            pb, rb = bank()
            P.op("pe", lambda e, pb=pb: e.transpose(out=pb[:, 0:128], in_=Y, identity=IDN[:]), reads=rs, writes=[rb])
            P.op("act", act(CW[:, j, ri, :], pb[:, 0:128], AF.Copy, scale=(1.0 if ri == 0 else -1.0)),
                 reads=[rb], writes=[r_WBCW])
    LAM = PV[:, 3, :]
    A(act(X0[:, 0:NCH], LAM, AF.Exp, scale=-1.0))
    V(ts(X0[:, 0:NCH], X0[:, 0:NCH], 1.0, None, ALU.add))
    A(act(X1[:, 0:NCH], X0[:, 0:NCH], AF.Ln))
    V(ts(CNG[:, 0:NCH], X1[:, 0:NCH], -8.0, None, ALU.mult))
    V(ts(CNG[:, NCH:2 * NCH], X1[:, 0:NCH], -4.0, None, ALU.mult))
    V(ts(HALFB[:, 0:NCH], PV[:, 1, :], 0.5, None, ALU.mult))
    V(ts(HALFB[:, NCH:2 * NCH], PV[:, 2, :], 0.5, None, ALU.mult))
    V(lambda e: e.memset(HC[:], 0.0), writes=[r_HC])
    V(lambda e: e.memset(XLT[:], 0.0), writes=[r_XLT])
    V(lambda e: e.memset(S5C[:], 0.0), writes=[r_S5C])
    EPS = 1e-6
    EPSB = sb("EPSB", [128, 1])
    V(lambda e: e.memset(EPSB[:], EPS))
    ONEB = sb("ONEB", [128, 1])
    V(lambda e: e.memset(ONEB[:], 1.0))
    P.barrier()

    groups = [
        dict(name="P1", row0=0, npr=512, ns=0, full=False, first=True, mask=False),
        dict(name="P2", row0=512, npr=512, ns=0, full=False, first=False, mask=False),
        dict(name="M1", row0=1024, npr=512, ns=0, full=True, first=False, mask=True),
        dict(name="M2", row0=1536, npr=512, ns=NS, full=True, first=False, mask=False),
    ]

    def rms_stats(src, np_, junk, col):
        ssq = SM[0:np_, col:col + 1]
        A(act(junk, src, AF.Square, accum_out=ssq), reads=[r_SM], writes=[r_SM])
        A(act(ssq, ssq, AF.Sqrt, bias=EPSB[0:np_, :], scale=1.0 / D), reads=[r_SM], writes=[r_SM])
        V(lambda e: e.reciprocal(out=ssq, in_=ssq), reads=[r_SM], writes=[r_SM])
        return ssq

    def do_group(g, chk):
        npr, ns, full = g["npr"], g["ns"], g["full"]
        N = npr + ns
        ntiles = [(c, 512) for c in range(0, npr, 512)] + ([(npr, ns)] if ns else [])
        ttiles = [(c, 128) for c in range(0, npr, 128)] + ([(npr, ns)] if ns else [])
        row0 = g["row0"]

        XT = r1v(0, 2, D)
        JK = R1[:, 2 * D:3 * D]
        r_XT = [Res("XT0"), Res("XT1")]
        r_JK = Res("JK")
        for ti, (c0, n) in enumerate(ttiles):
            s = ti % 2
            xt = XT[0:n, s, :]
            P.dma("sp", xsem[s], lambda e, xt=xt, c0=c0, n=n: e.dma_start(out=xt, in_=xf[row0 + c0:row0 + c0 + n, :]),
                  writes=[r_XT[s]])
            chk("A1")
            ssq = SM[0:n, ti:ti + 1]
            P.op("act", act(JK[0:n, :], xt, AF.Square, accum_out=ssq), reads=[r_XT[s]], writes=[r_JK, r_SM])
            chk("A2")
            P.op("act", act(ssq, ssq, AF.Sqrt, bias=EPSB[0:n, :], scale=1.0 / D), reads=[r_SM], writes=[r_SM])
            chk("A3")
            P.op("dve", lambda e, ssq=ssq: e.reciprocal(out=ssq, in_=ssq), reads=[r_SM], writes=[r_SM])
            P.op("act", act(xt, xt, AF.Identity, scale=ssq), reads=[r_SM, r_XT[s]], writes=[r_XT[s]])
            chk("A4")
            for k4 in range(4):
                pb, rb = bank()
                def tr(e, pb=pb, xt=xt, k4=k4, n=n):
                    ins = None
                    for kk in range(4):
                        k = k4 * 4 + kk
                        ins = e.transpose(out=pb[:, kk * 128:kk * 128 + n], in_=xt[:, k * 128:(k + 1) * 128],
                                          identity=IDN[0:n, 0:n])
                    return ins
                P.op("pe", tr, reads=[r_XT[s]], writes=[rb])
                chk("A5")
                for kk in range(4):
                    k = k4 * 4 + kk
                    eng = "dve"
                    if eng == "dve":
                        P.op("dve", ts(UT[:, k, c0:c0 + n], pb[:, kk * 128:kk * 128 + n], PV[:, 4, k:k + 1], None, ALU.mult),
                             reads=[rb], writes=[r_UT[k]])
                        chk("A6")
                    else:
                        P.op("act", act(UT[:, k, c0:c0 + n], pb[:, kk * 128:kk * 128 + n], AF.Identity, scale=PV[:, 4, k:k + 1]),
                             reads=[rb], writes=[r_UT[k]])
                        chk("A7")
            chk("A8")
        P.barrier()
        chk(g["name"] + "A")

        NP3 = N + 3
        off = [0]
        def tl(a, b):
            v = r1v(off[0], a, b)
            off[0] += a * b
            return v
        XL = tl(2, NP3); XC = tl(2, N); TR = tl(2, N); TI = tl(2, N); AA = tl(2, N)
        G1 = R2[:, NCH * NT:NCH * NT + 4 * N].bitcast(F32).rearrange("p (a b) -> p a b", b=N)
        HF = TR
        A2 = TI
        X4 = R1[:, off[0]:off[0] + 8 * N].rearrange("p (t r n) -> p t r n", t=4, r=2)
        off[0] += 8 * N
        TM1 = R1[:, off[0]:off[0] + 256].rearrange("p (t r n) -> p t r n", t=4, r=2)
        TM2 = R1[:, off[0] + 256:off[0] + 384].rearrange("p (t n) -> p t n", t=4)
        TM3 = R1[:, off[0] + 384:off[0] + 512].rearrange("p (t n) -> p t n", t=4)
        off[0] += 512
        assert off[0] <= 5 * D, off[0]
        YT = XTN[:, 0:N]; Y2 = XTN[:, N:2 * N]; Y3 = XTN[:, 2 * N:3 * N]
        r_t = {n_: [Res(n_ + "0"), Res(n_ + "1")] for n_ in ["XL", "XC", "TR", "TI", "AA"]}
        r_t["G1"] = [Res("G1_0"), Res("G1_1")]
        r_t["HF"] = r_t["TR"]
        r_t["A2"] = r_t["TI"]
        XRB = XCB
        r_XRe = [Res("XRe0"), Res("XRe1")]
        r_XIm = [Res("XIm0"), Res("XIm1")]
        r_XSF = [Res("XSF0"), Res("XSF1")]
        r_Y = r_XTN_g
        L = 9
        gw_ptr = [0]

        def lru_block(hb, nxt=None):
            gb = gw_ptr[0] % 2
            gw_ptr[0] += 1
            P.dma("pool", gwsem[gb], lambda e: e.dma_start(out=GW[:, gb, 0, :, :], in_=lru_wa[hb * 256:(hb + 1) * 256, :].rearrange("(k p) n -> p k n", p=128)),
                  writes=[r_GW[gb]])
            P.dma("pool", gwsem[gb], lambda e: e.dma_start(out=GW[:, gb, 1, :, :], in_=lru_wx[hb * 256:(hb + 1) * 256, :].rearrange("(k p) n -> p k n", p=128)),
                  writes=[r_GW[gb]])
            s = wslot()
            wload(s, WR[:, s, :, :], wview(w_in, 0, D, hb * 256, 256))
            for j in range(2):
                c = hb * 2 + j
                mc = j
                P.op("dve", lambda e, j=j, c=c: e.tensor_copy(out=XL[:, j, 0:3], in_=XLT[:, c, :]),
                     reads=[r_XLT], writes=[r_t["XL"][j]])
                if g["mask"]:
                    P.op("dve", ts(XL[:, j, 0:3], XL[:, j, 0:3], FLG[:, 0:1], None, ALU.mult),
                         reads=[r_small, r_t["XL"][j]], writes=[r_t["XL"][j]])
                for (n0, nn) in ntiles:
                    pb, rb = bank()
                    def mm(e, pb=pb, s=s, mc=mc, n0=n0, nn=nn):
                        ins = None
                        for k in range(NCH):
                            ins = e.matmul(out=pb[:, 0:nn], lhsT=WR[:, s, k, mc * 128:(mc + 1) * 128],
                                           rhs=UT[:, k, n0:n0 + nn], start=(k == 0), stop=(k == NCH - 1))
                        return ins
                    P.op("pe", mm, reads=[r_WR[s]] + r_UT, writes=[rb])
                    P.op("act", act(XL[:, j, 3 + n0:3 + n0 + nn], pb[:, 0:nn], AF.Copy), reads=[rb], writes=[r_t["XL"][j]])
            yield
            if nxt is not None:
                s5_inproj(nxt)
            yield
            for j in range(2):
                c = hb * 2 + j
                rr = [r_t["XL"][j], r_small]
                P.op("act", act(XC[:, j, 0:npr], XL[:, j, 3:3 + npr], AF.Identity, bias=PV[:, 0, c:c + 1],
                                scale=CWV[:, c, 3:4]), reads=rr, writes=[r_t["XC"][j]])
                if ns:
                    P.op("act", act(XC[:, j, npr:N], XL[:, j, 3 + npr:3 + N], AF.Identity, bias=PV[:, 0, c:c + 1], scale=CWV[:, c, 3:4]),
                         reads=rr + [r_t["XC"][j]], writes=[r_t["XC"][j]])
            yield
            for k in range(3):
                if k:
                    yield
                for j in range(2):
                    c = hb * 2 + j
                    rr = [r_t["XL"][j], r_small]
                    P.op("dve", stt(XC[:, j, 0:npr], XL[:, j, k:k + npr], CWV[:, c, k:k + 1], XC[:, j, 0:npr], ALU.mult, ALU.add),
                         reads=rr + [r_t["XC"][j]], writes=[r_t["XC"][j]])
                    if ns:
                        xs_ = XC[:, j, npr:N]
                        P.op("dve", stt(xs_, CS[:, c, :, k], CWV[:, c, k:k + 1], xs_, ALU.mult, ALU.add),
                             reads=rr + [r_t["XC"][j]], writes=[r_t["XC"][j]])
            for j in range(2):
                c = hb * 2 + j
                if ns:
                    P.op("dve", lambda e, c=c: e.tensor_copy(out=OCS[:, c, :, 0:2], in_=CS[:, c, :, 1:3]),
                         reads=[r_small], writes=[r_out])
                    P.op("dve", lambda e, c=c, j=j: e.tensor_copy(out=OCS[:, c, :, 2], in_=XL[:, j, 3 + npr:3 + N]),
                         reads=[r_t["XL"][j]], writes=[r_out])
                P.op("dve", lambda e, j=j, c=c: e.tensor_copy(out=XLT[:, c, :], in_=XL[:, j, npr:npr + 3]),
                     reads=[r_t["XL"][j]], writes=[r_XLT])
                P.op("act", act(XCB[:, j, 0:N], XC[:, j, 0:N], AF.Copy), reads=[r_t["XC"][j]], writes=[r_XCB[j]])
            yield
            for j in range(2):
                c = hb * 2 + j
                for (n0, nn) in ntiles:
                    pr_, rr_ = bank()
                    pi_, ri_ = bank()
                    def mg(e, pr_=pr_, pi_=pi_, j=j, n0=n0, nn=nn, gb=gb):
                        ins = None
                        for k in range(2):
                            ins = e.matmul(out=pr_[:, 0:nn], lhsT=GW[:, gb, 0, k, j * 128:(j + 1) * 128],
                                           rhs=XCB[:, k, n0:n0 + nn], start=(k == 0), stop=(k == 1))
                        for k in range(2):
                            ins = e.matmul(out=pi_[:, 0:nn], lhsT=GW[:, gb, 1, k, j * 128:(j + 1) * 128],
                                           rhs=XCB[:, k, n0:n0 + nn], start=(k == 0), stop=(k == 1))
                        return ins
                    P.op("pe", mg, reads=[r_GW[gb]] + r_XCB, writes=[rr_, ri_])
                    P.op("act", act(TR[:, j, n0:n0 + nn], pr_[:, 0:nn], AF.Tanh, bias=HALFB[:, c:c + 1], scale=0.5),
                         reads=[rr_, r_small], writes=[r_t["TR"][j]])
                    P.op("act", act(TI[:, j, n0:n0 + nn], pi_[:, 0:nn], AF.Tanh, bias=HALFB[:, NCH + c:NCH + c + 1], scale=0.5),
                         reads=[ri_, r_small], writes=[r_t["TI"][j]])
                P.op("act", act(AA[:, j, 0:N], TR[:, j, 0:N], AF.Exp, bias=CNG[:, NCH + c:NCH + c + 1], scale=CNG[:, NCH + c:NCH + c + 1]),
                     reads=[r_t["TR"][j], r_small], writes=[r_t["AA"][j]])
                yield
                P.op("dve", stt(G1[:, j, 0:N], TI[:, j, 0:N], 1.0, XC[:, j, 0:N], ALU.add, ALU.mult),
                     reads=[r_t["TI"][j], r_t["XC"][j]], writes=[r_t["G1"][j]])
                P.op("act", act(A2[:, j, 0:N], TR[:, j, 0:N], AF.Exp, bias=CNG[:, c:c + 1], scale=CNG[:, c:c + 1]),
                     reads=[r_t["TR"][j], r_t["G1"][j], r_small], writes=[r_t["A2"][j]])
            for j in range(2):
                P.op("act", act(A2[:, j, 0:N], A2[:, j, 0:N], AF.Sqrt, bias=ONEB[:, 0:1], scale=-1.0),
                     reads=[r_t["A2"][j]], writes=[r_t["A2"][j]])
            yield
            for j in range(2):
                c = hb * 2 + j
                if g["first"]:
                    P.op("dve", lambda e, j=j: e.memset(A2[:, j, 0:1], 1.0), reads=[r_t["A2"][j]], writes=[r_t["A2"][j]])
                if g["mask"]:
                    P.op("dve", ts(A2[:, j, 0:1], A2[:, j, 0:1], FLG[:, 0:1], FLG[:, 1:2], ALU.mult, ALU.add),
                         reads=[r_t["A2"][j], r_small], writes=[r_t["A2"][j]])
                    P.op("dve", ts(AA[:, j, 0:1], AA[:, j, 0:1], FLG[:, 0:1], None, ALU.mult),
                         reads=[r_t["AA"][j], r_small], writes=[r_t["AA"][j]])
                P.op("dve", stt(G1[:, j, 0:N], G1[:, j, 0:N], 0.5, A2[:, j, 0:N], ALU.mult, ALU.mult),
                     reads=[r_t["G1"][j], r_t["A2"][j]], writes=[r_t["G1"][j]])
            yield
            for j in range(2):
                c = hb * 2 + j
                P.op("dve", lambda e, j=j, c=c: e.tensor_tensor_scan(out=HF[:, j, 0:npr], data0=AA[:, j, 0:npr], data1=G1[:, j, 0:npr],
                                                                initial=HC[:, c:c + 1], op0=ALU.mult, op1=ALU.add),
                     reads=[r_t["AA"][j], r_t["G1"][j], r_HC], writes=[r_t["HF"][j]])
            yield
            for j in range(2):
                c = hb * 2 + j
                P.op("dve", lambda e, j=j, c=c: e.tensor_copy(out=HC[:, c:c + 1], in_=HF[:, j, npr - 1:npr]),
                     reads=[r_t["HF"][j]], writes=[r_HC])
                if ns:
                    hs_ = HF[:, j, npr:N]
                    P.op("dve", tt(hs_, AA[:, j, npr:N], H0[:, c, :], ALU.mult), reads=[r_t["AA"][j], r_small], writes=[r_t["HF"][j]])
                    P.op("dve", tt(hs_, hs_, G1[:, j, npr:N], ALU.add), reads=[r_t["HF"][j], r_t["G1"][j]], writes=[r_t["HF"][j]])
                    P.op("dve", lambda e, c=c, hs_=hs_: e.tensor_copy(out=OHS[:, c, :], in_=hs_), reads=[r_t["HF"][j]], writes=[r_out])
                if full:
                    P.op("act", act(HS[:, c, 0:N], HF[:, j, 0:N], AF.Copy), reads=[r_t["HF"][j]], writes=[r_HS[c]])

        s5slot = [None]

        r_X4 = [Res("X4_%d" % t) for t in range(4)]
        tmp_last = [[]]

        lru_gen = [None]
        tick_ctr = [0]
        TICK = 10 if full else 6

        def dv(fn, after):
            m = P.op("dve", fn, after=[m for m in after if m is not None])
            tick_ctr[0] += 1
            if lru_gen[0] is not None and tick_ctr[0] % TICK == 0:
                try:
                    next(lru_gen[0])
                except StopIteration:
                    lru_gen[0] = None
            return m

        def s5_inproj(kt):
            if kt % 2 == 0:
                s5slot[0] = wslot()
                wload(s5slot[0], WR[:, s5slot[0], :, :], wview(w_in, 0, D, 2048 + (kt // 2) * 256, 256))
            s = s5slot[0]
            mc = kt % 2
            xs = kt % 2
            for (n0, nn) in ntiles:
                pb, rb = bank()
                def mm(e, pb=pb, s=s, mc=mc, n0=n0, nn=nn):
                    ins = None
                    for k in range(NCH):
                        ins = e.matmul(out=pb[:, 0:nn], lhsT=WR[:, s, k, mc * 128:(mc + 1) * 128],
                                       rhs=UT[:, k, n0:n0 + nn], start=(k == 0), stop=(k == NCH - 1))
                    return ins
                P.op("pe", mm, reads=[r_WR[s]] + r_UT, writes=[rb])
                P.op("act", act(XS[:, xs, n0:n0 + nn], pb[:, 0:nn], AF.Copy), reads=[rb], writes=[r_XS[xs]])

        def s5_chunk(kt):
            xs = kt % 2
            ybanks = [ybank(i) for i in range(len(ntiles))] if full else []
            j0 = kt * 4
            for t in range(4):
                j = j0 + t
                for (n0, nn) in ntiles:
                    pr_, rr_ = bank()
                    pi_, ri_ = bank()
                    def mb(e, pr_=pr_, pi_=pi_, j=j, xs=xs, n0=n0, nn=nn):
                        e.matmul(out=pr_[:, 0:nn], lhsT=WB[:, j, 0, :], rhs=XS[:, xs, n0:n0 + nn], start=True, stop=True)
                        return e.matmul(out=pi_[:, 0:nn], lhsT=WB[:, j, 1, :], rhs=XS[:, xs, n0:n0 + nn], start=True, stop=True)
                    P.op("pe", mb, reads=[r_WBCW, r_XS[xs]], writes=[rr_, ri_])
                    P.op("act", act(X4[:, t, 0, n0:n0 + nn], pr_[:, 0:nn], AF.Copy), reads=[rr_], writes=[r_X4[t]])
                    P.op("dve", lambda e, t=t, pi_=pi_, n0=n0, nn=nn: e.tensor_copy(out=X4[:, t, 1, n0:n0 + nn], in_=pi_[:, 0:nn]),
                         reads=[ri_], writes=[r_X4[t]])
            C4 = S5C[:, j0:j0 + 4, :]
            if g["mask"]:
                P.op("dve", ts(C4, C4, FLG[:, 0:1], None, ALU.mult), reads=[r_S5C, r_small], writes=[r_S5C])
            prev_t = [[r_X4[t].w] + list(r_X4[t].r) for t in range(4)]
            carry_dep = [r_S5C.w]

            def batched(l, s_both, s_re, s_im, t_re, t_im, cnt, after):
                arp = LP2[:, l, j0:j0 + 4, 0:2].unsqueeze(3).broadcast_to([128, 4, 2, cnt])
                ai_ = LP2[:, l, j0:j0 + 4, 2].unsqueeze(2).broadcast_to([128, 4, cnt])
                nai_ = LP2[:, l, j0:j0 + 4, 3].unsqueeze(2).broadcast_to([128, 4, cnt])
                t1, t2, t3 = TM1[:, :, :, 0:cnt], TM2[:, :, 0:cnt], TM3[:, :, 0:cnt]
                aft = list(after) + tmp_last[0]
                m1 = dv(tt(t1, s_both, arp, ALU.mult), aft)
                m2 = dv(tt(t2, s_im, nai_, ALU.mult), aft)
                m3 = dv(tt(t3, s_re, ai_, ALU.mult), aft)
                c2 = dv(tt(t2, t2, TM1[:, :, 0, 0:cnt], ALU.add), [m1, m2])
                c3 = dv(tt(t3, t3, TM1[:, :, 1, 0:cnt], ALU.add), [m1, m3])
                a_re = dv(tt(t_re, t_re, t2, ALU.add), [c2] + list(after))
                a_im = dv(tt(t_im, t_im, t3, ALU.add), [c3] + list(after))
                tmp_last[0] = [a_re, a_im]
                return [a_re, a_im]

            def per_tile(l, sl_s, sl_t, prevs):
                ab = []
                for t in range(4):
                    ar = LP2[:, l, j0 + t, 0:1]
                    ab.append(dv(stt(X4[:, t, :, sl_t], X4[:, t, :, sl_s], ar, X4[:, t, :, sl_t], ALU.mult, ALU.add), prevs[t]))
                out = [[] for _ in range(4)]
                for t in range(4):
                    nai = LP2[:, l, j0 + t, 3:4]
                    out[t].append(dv(stt(X4[:, t, 0, sl_t], X4[:, t, 1, sl_s], nai, X4[:, t, 0, sl_t], ALU.mult, ALU.add), [ab[t]] + prevs[t]))
                for t in range(4):
                    ai = LP2[:, l, j0 + t, 2:3]
                    out[t].append(dv(stt(X4[:, t, 1, sl_t], X4[:, t, 0, sl_s], ai, X4[:, t, 1, sl_t], ALU.mult, ALU.add), [ab[t]] + prevs[t]))
                return out

            BL = 3
            allprev = [m for p in prev_t for m in p]
            d_all = batched(0, C4, C4[:, :, 0:1], C4[:, :, 1:2], X4[:, :, 0, 0:1], X4[:, :, 1, 0:1], 1, allprev + carry_dep)
            prevs = [list(d_all) for _ in range(4)]
            for l in range(L):
                d = 1 << l
                sl_t = slice(2 * d - 1, npr, 2 * d)
                sl_s = slice(d - 1, npr - d, 2 * d)
                cnt = npr // (2 * d)
                if l < BL:
                    prevs = per_tile(l, sl_s, sl_t, prevs)
                else:
                    flat = [m for p in prevs for m in p]
                    d_all = batched(l, X4[:, :, :, sl_s], X4[:, :, 0, sl_s], X4[:, :, 1, sl_s], X4[:, :, 0, sl_t], X4[:, :, 1, sl_t], cnt, flat)
                    prevs = [list(d_all) for _ in range(4)]
            if full:
                for l in range(L - 2, -1, -1):
                    d = 1 << l
                    sl_t = slice(3 * d - 1, npr, 2 * d)
                    sl_s = slice(2 * d - 1, npr - d, 2 * d)
                    cnt = npr // (2 * d) - 1
                    if l < BL:
                        prevs = per_tile(l, sl_s, sl_t, prevs)
                    else:
                        flat = [m for p in prevs for m in p]
                        d_all = batched(l, X4[:, :, :, sl_s], X4[:, :, 0, sl_s], X4[:, :, 1, sl_s], X4[:, :, 0, sl_t], X4[:, :, 1, sl_t], cnt, flat)
                        prevs = [list(d_all) for _ in range(4)]
            if ns:
                flat = [m for p in prevs for m in p]
                d_all = batched(0, None, None, None, None, None, 0, flat) if False else None
                cnt = ns
                arp = LP2[:, 0, j0:j0 + 4, 0:1].broadcast_to([128, 4, cnt])
                ai_ = LP2[:, 0, j0:j0 + 4, 2:3].broadcast_to([128, 4, cnt])
                nai_ = LP2[:, 0, j0:j0 + 4, 3:4].broadcast_to([128, 4, cnt])
                sR, sI = S5R[:, j0:j0 + 4, :], S5I[:, j0:j0 + 4, :]
                t1r, t1i = TM1[:, :, 0, 0:cnt], TM1[:, :, 1, 0:cnt]
                t2, t3 = TM2[:, :, 0:cnt], TM3[:, :, 0:cnt]
                aft = flat + tmp_last[0] + [r_small.w]
                m1 = dv(tt(t1r, sR, arp, ALU.mult), aft)
                m1b = dv(tt(t1i, sI, arp, ALU.mult), aft)
                m2 = dv(tt(t2, sI, nai_, ALU.mult), aft)
                m3 = dv(tt(t3, sR, ai_, ALU.mult), aft)
                c2 = dv(tt(t2, t2, t1r, ALU.add), [m1, m2])
                c3 = dv(tt(t3, t3, t1i, ALU.add), [m1b, m3])
                a_re = dv(tt(X4[:, :, 0, npr:N], X4[:, :, 0, npr:N], t2, ALU.add), [c2] + flat)
                a_im = dv(tt(X4[:, :, 1, npr:N], X4[:, :, 1, npr:N], t3, ALU.add), [c3] + flat)
                tmp_last[0] = [a_re, a_im]
                prevs = [p + [a_re, a_im] for p in prevs]
            for t in range(4):
                last = max(m[1] for m in prevs[t] if m[0] == "dve")
                r_X4[t].w = ("dve", last); r_X4[t].r = []
            if lru_gen[0] is not None:
                for _ in lru_gen[0]:
                    pass
                lru_gen[0] = None
            P.op("dve", lambda e: e.tensor_copy(out=C4, in_=X4[:, :, :, npr - 1]), reads=r_X4, writes=[r_S5C])
            if ns:
                P.op("dve", lambda e: e.tensor_copy(out=OSS[:, 0, j0:j0 + 4, :], in_=X4[:, :, 0, npr:N]), reads=r_X4, writes=[r_out])
                P.op("dve", lambda e: e.tensor_copy(out=OSS[:, 1, j0:j0 + 4, :], in_=X4[:, :, 1, npr:N]), reads=r_X4, writes=[r_out])
            if full:
                for t in range(4):
                    j = j0 + t
                    P.op("act", act(XRB[:, 0, 0:N], X4[:, t, 0, 0:N], AF.Copy), reads=[r_X4[t]], writes=[r_XCB[0]])
                    P.op("act", act(XRB[:, 1, 0:N], X4[:, t, 1, 0:N], AF.Copy), reads=[r_X4[t]], writes=[r_XCB[1]])
                    for ti_n, (n0, nn) in enumerate(ntiles):
                        yb, ryb = ybanks[ti_n]
                        def mc_(e, yb=yb, j=j, t=t, n0=n0, nn=nn):
                            e.matmul(out=yb[:, 0:nn], lhsT=CW[:, j, 0, :], rhs=XRB[:, 0, n0:n0 + nn], start=(t == 0), stop=False)
                            return e.matmul(out=yb[:, 0:nn], lhsT=CW[:, j, 1, :], rhs=XRB[:, 1, n0:n0 + nn], start=False, stop=(t == 3))
                        P.op("pe", mc_, reads=[r_WBCW] + r_XCB, writes=[ryb])
                for ti_n, (n0, nn) in enumerate(ntiles):
                    yb, ryb = ybanks[ti_n]
                    P.op("dve", stt(YT[:, n0:n0 + nn], XS[:, xs, n0:n0 + nn], SD[:, kt:kt + 1], yb[:, 0:nn], ALU.mult, ALU.add),
                         reads=[ryb, r_XS[xs], r_small], writes=[r_Y])
                yy, y2, y3 = YT[:, 0:N], Y2[:, 0:N], Y3[:, 0:N]
                P.op("act", act(y2, yy, AF.Square), reads=[r_Y], writes=[r_Y])
                P.op("dve", ts(y2, y2, GELU_C1, 1.0, ALU.mult, ALU.add), reads=[r_Y], writes=[r_Y])
                P.op("dve", tt(y2, y2, yy, ALU.mult), reads=[r_Y], writes=[r_Y])
                P.op("act", act(y3, y2, AF.Tanh, scale=GELU_C0), reads=[r_Y], writes=[r_Y])
                P.op("dve", stt(y3, y3, 1.0, yy, ALU.add, ALU.mult), reads=[r_Y], writes=[r_Y])
                P.op("act", act(VT[:, kt, 0:N], y3, AF.Copy, scale=0.5), reads=[r_Y], writes=[r_VT[kt]])

        lru_cur = [None]

        def lru_pipeline(kt):
            stage = 0
            for _ in lru_cur[0]:
                stage += 1
                if stage == 5 and kt < 7:
                    nb_ = lru_block(kt + 1, kt + 2 if kt + 1 < 7 else None)
                    next(nb_)
                    lru_cur[0] = nb_
                yield

        s5_inproj(0)
        lru_cur[0] = lru_block(0, 1)
        next(lru_cur[0])
        for kt in range(8):
            lru_gen[0] = lru_pipeline(kt)
            s5_chunk(kt)
        P.barrier()
        chk(g["name"] + "C")

        if not full:
            return

        off[0] = 0
        MA = tl(4, N); TG = tl(2, N); TQ = tl(2, N)
        r_MA = [Res("MA%d" % k) for k in range(4)]
        r_TG = [Res("TG0"), Res("TG1")]
        r_TQ = [Res("TQ0"), Res("TQ1")]
        tcount = [0]
        for sl in range(8):
            s1 = wslot(); wload(s1, WR[:, s1, :, :], wview(w_in, 0, D, 3072 + sl * 256, 256))
            s2 = wslot(); wload(s2, WR[:, s2, :, :], wview(lru_proj, 0, D, sl * 256, 256))
            for mc in range(2):
                for (n0, nn) in ntiles:
                    pg, rg = bank()
                    py, ry = bank()
                    def mm(e, pg=pg, py=py, mc=mc, n0=n0, nn=nn, s1=s1, s2=s2):
                        ins = None
                        for k in range(NCH):
                            ins = e.matmul(out=pg[:, 0:nn], lhsT=WR[:, s1, k, mc * 128:(mc + 1) * 128], rhs=UT[:, k, n0:n0 + nn],
                                           start=(k == 0), stop=(k == NCH - 1))
                        for k in range(NCH):
                            ins = e.matmul(out=py[:, 0:nn], lhsT=WR[:, s2, k, mc * 128:(mc + 1) * 128], rhs=HS[:, k, n0:n0 + nn],
                                           start=(k == 0), stop=(k == NCH - 1))
                        return ins
                    P.op("pe", mm, reads=[r_WR[s1], r_WR[s2]] + r_UT + r_HS, writes=[rg, ry])
                    tb = tcount[0] % 2; tcount[0] += 1
                    P.op("act", act(TG[:, tb, 0:nn], pg[:, 0:nn], AF.Tanh, scale=0.5), reads=[rg], writes=[r_TG[tb]])
                    P.op("dve", stt(MA[:, mc, n0:n0 + nn], TG[:, tb, 0:nn], 1.0, py[:, 0:nn], ALU.add, ALU.mult),
                         reads=[r_TG[tb], ry], writes=[r_MA[mc]])
            s3 = wslot(); wload(s3, WR[:, s3, :, :], wview(w_in, 0, D, 5120 + sl * 256, 256))
            s4 = wslot()
            wload(s4, WR[:, s4, 0:8, :], wview(wv_d, 0, 1024, sl * 256, 256))
            wload(s4, WR[:, s4, 8:16, :], wview(wg_d, 0, 1024, sl * 256, 256))
            for mc in range(2):
                m = sl * 2 + mc
                for (n0, nn) in ntiles:
                    pg, rg = bank()
                    pv_, rv = bank()
                    pw, rw = bank()
                    def mm(e, pg=pg, pv_=pv_, pw=pw, mc=mc, n0=n0, nn=nn, s3=s3, s4=s4):
                        ins = None
                        for k in range(NCH):
                            ins = e.matmul(out=pg[:, 0:nn], lhsT=WR[:, s3, k, mc * 128:(mc + 1) * 128], rhs=UT[:, k, n0:n0 + nn],
                                           start=(k == 0), stop=(k == NCH - 1))
                        for k in range(8):
                            ins = e.matmul(out=pv_[:, 0:nn], lhsT=WR[:, s4, k, mc * 128:(mc + 1) * 128], rhs=VT[:, k, n0:n0 + nn],
                                           start=(k == 0), stop=(k == 7))
                        for k in range(8):
                            ins = e.matmul(out=pw[:, 0:nn], lhsT=WR[:, s4, 8 + k, mc * 128:(mc + 1) * 128], rhs=VT[:, k, n0:n0 + nn],
                                           start=(k == 0), stop=(k == 7))
                        return ins
                    P.op("pe", mm, reads=[r_WR[s3], r_WR[s4]] + r_UT + r_VT, writes=[rg, rv, rw])
                    tb = tcount[0] % 2; tcount[0] += 1
                    P.op("act", act(TG[:, tb, 0:nn], pg[:, 0:nn], AF.Tanh, scale=0.5), reads=[rg], writes=[r_TG[tb]])
                    P.op("act", act(TQ[:, tb, 0:nn], pw[:, 0:nn], AF.Tanh, scale=0.5), reads=[rw], writes=[r_TQ[tb]])
                    P.op("dve", stt(TQ[:, tb, 0:nn], TQ[:, tb, 0:nn], 1.0, pv_[:, 0:nn], ALU.add, ALU.mult),
                         reads=[r_TQ[tb], rv], writes=[r_TQ[tb]])
                    P.op("dve", stt(TQ[:, tb, 0:nn], TG[:, tb, 0:nn], 1.0, TQ[:, tb, 0:nn], ALU.add, ALU.mult),
                         reads=[r_TQ[tb], r_TG[tb]], writes=[r_TQ[tb]])
                    P.op("dve", stt(TQ[:, tb, 0:nn], TQ[:, tb, 0:nn], 0.5, MA[:, mc, n0:n0 + nn], ALU.mult, ALU.add),
                         reads=[r_TQ[tb], r_MA[mc]], writes=[r_TQ[tb]])
                    P.op("act", act(MG[:, m, n0:n0 + nn], TQ[:, tb, 0:nn], AF.Copy, scale=0.5), reads=[r_TQ[tb]], writes=[r_MG[m]])
        P.barrier()
        chk(g["name"] + "D")

        for ti, (c0, n) in enumerate(ttiles):
            P.dma("sp", xsem[ti], lambda e, ti=ti, c0=c0, n=n: e.dma_start(out=XP1[0:n, ti, :], in_=xf[row0 + c0:row0 + c0 + n, :]),
                  writes=[r_XP1[ti]])
        for nb in range(8):
            s = wslot(); wload(s, WR[:, s, :, :], wview(w_out, 0, D, nb * 256, 256))
            for ti, (c0, n) in enumerate(ttiles):
                pb, rb = bank()
                def mm(e, pb=pb, s=s, c0=c0, n=n):
                    ins = None
                    for k in range(NCH):
                        ins = e.matmul(out=pb[0:n, 0:256], lhsT=MG[:, k, c0:c0 + n], rhs=WR[:, s, k, :], start=(k == 0), stop=(k == NCH - 1))
                    return ins
                P.op("pe", mm, reads=[r_WR[s]] + r_MG, writes=[rb])
                dst = XP1[0:n, ti, nb * 256:(nb + 1) * 256]
                P.op("dve", tt(dst, pb[0:n, 0:256], dst, ALU.add), reads=[rb, r_XP1[ti]], writes=[r_XP1[ti]])
        P.barrier()
        chk(g["name"] + "E")

        JK2 = HM[:, 40:44, :].rearrange("p a b -> p (a b)")
        r_JK2 = Res("JK2")
        r_XTN = r_XTN_g
        for ti, (c0, n) in enumerate(ttiles):
            xt = XP1[0:n, ti, :]
            ssq = SM[0:n, 8 + ti:9 + ti]
            P.op("act", act(JK2[0:n, 0:D], xt, AF.Square, accum_out=ssq), reads=[r_XP1[ti]], writes=[r_JK2, r_SM])
            P.op("act", act(ssq, ssq, AF.Sqrt, bias=EPSB[0:n, :], scale=1.0 / D), reads=[r_SM], writes=[r_SM])
            P.op("dve", lambda e, ssq=ssq: e.reciprocal(out=ssq, in_=ssq), reads=[r_SM], writes=[r_SM])
            xn = XTN[0:n, :]
            P.op("act", act(xn, xt, AF.Identity, scale=ssq), reads=[r_SM, r_XP1[ti]], writes=[r_XTN])
            for k4 in range(4):
                pb, rb = bank()
                def tr(e, pb=pb, xn=xn, k4=k4, n=n):
                    ins = None
                    for kk in range(4):
                        k = k4 * 4 + kk
                        ins = e.transpose(out=pb[:, kk * 128:kk * 128 + n], in_=xn[:, k * 128:(k + 1) * 128], identity=IDN[0:n, 0:n])
                    return ins
                P.op("pe", tr, reads=[r_XTN], writes=[rb])
                for kk in range(4):
                    k = k4 * 4 + kk
                    if True:
                        P.op("dve", ts(UT[:, k, c0:c0 + n], pb[:, kk * 128:kk * 128 + n], PV[:, 5, k:k + 1], None, ALU.mult),
                             reads=[rb], writes=[r_UT[k]])
                    else:
                        P.op("act", act(UT[:, k, c0:c0 + n], pb[:, kk * 128:kk * 128 + n], AF.Identity, scale=PV[:, 5, k:k + 1]),
                             reads=[rb], writes=[r_UT[k]])
        P.barrier()
        chk(g["name"] + "F")

        r_TGF = [Res("TGF0"), Res("TGF1")]
        tcount[0] = 0
        for f in range(44):
            fi = f % 2
            if fi == 0:
                sg = wslot()
                wload(sg, WR[:, sg, :, :], wview(w_gate, 0, D, f * 128, 256))
                su = wslot()
                wload(su, WR[:, su, :, :], wview(w_up, 0, D, f * 128, 256))
            for (n0, nn) in ntiles:
                pg, rg = bank()
                pu, ru = bank()
                def mm(e, pg=pg, pu=pu, sg=sg, su=su, fi=fi, n0=n0, nn=nn):
                    ins = None
                    for k in range(NCH):
                        ins = e.matmul(out=pg[:, 0:nn], lhsT=WR[:, sg, k, fi * 128:(fi + 1) * 128], rhs=UT[:, k, n0:n0 + nn], start=(k == 0), stop=(k == NCH - 1))
                    for k in range(NCH):
                        ins = e.matmul(out=pu[:, 0:nn], lhsT=WR[:, su, k, fi * 128:(fi + 1) * 128], rhs=UT[:, k, n0:n0 + nn], start=(k == 0), stop=(k == NCH - 1))
                    return ins
                P.op("pe", mm, reads=[r_WR[sg], r_WR[su]] + r_UT, writes=[rg, ru])
                tb = tcount[0] % 2; tcount[0] += 1
                P.op("act", act(TGF[:, tb, 0:nn], pg[:, 0:nn], AF.Tanh, scale=0.5), reads=[rg], writes=[r_TGF[tb]])
                P.op("dve", stt(TGF[:, tb, 0:nn], TGF[:, tb, 0:nn], 1.0, pg[:, 0:nn], ALU.add, ALU.mult), reads=[rg, r_TGF[tb]], writes=[r_TGF[tb]])
                P.op("dve", stt(HM[:, f, n0:n0 + nn], TGF[:, tb, 0:nn], 0.5, pu[:, 0:nn], ALU.mult, ALU.mult), reads=[ru, r_TGF[tb]], writes=[r_HM[f]])

        fgroups = [(0, 16), (16, 16), (32, 12)]
        for nb in range(8):
            banks = [bank() for _ in ttiles]
            for (f0, nf) in fgroups:
                s = wslot()
                wload(s, WR[:, s, 0:nf, :], wview(w_down, f0 * 128, nf * 128, nb * 256, 256))
                def mm(e, s=s, f0=f0, nf=nf, banks=banks):
                    ins = None
                    for fk in range(nf):
                        f = f0 + fk
                        for ti, (c0, n) in enumerate(ttiles):
                            ins = e.matmul(out=banks[ti][0][0:n, 0:256], lhsT=HM[:, f, c0:c0 + n], rhs=WR[:, s, fk, :],
                                           start=(f == 0), stop=(f == 43))
                    return ins
                P.op("pe", mm, reads=[r_WR[s]] + r_HM, writes=[b[1] for b in banks])
            for ti, (c0, n) in enumerate(ttiles):
                dst = XP1[0:n, ti, nb * 256:(nb + 1) * 256]
                P.op("dve", tt(dst, banks[ti][0][0:n, 0:256], dst, ALU.add), reads=[banks[ti][1], r_XP1[ti]], writes=[r_XP1[ti]])

        P.dma("sp", isem, lambda e: e.dma_start(out=XTN[:], in_=gfin_d[:, :]), writes=[r_XTN] + r_TGF)
        for ti, (c0, n) in enumerate(ttiles):
            xt = XP1[0:n, ti, :]
            ssq = SM[0:n, 16 + ti:17 + ti]
            P.op("act", act(JK2[0:n, 0:D], xt, AF.Square, accum_out=ssq), reads=[r_XP1[ti]], writes=[r_JK2, r_SM])
            P.op("act", act(ssq, ssq, AF.Sqrt, bias=EPSB[0:n, :], scale=1.0 / D), reads=[r_SM], writes=[r_SM])
            P.op("dve", lambda e, ssq=ssq: e.reciprocal(out=ssq, in_=ssq), reads=[r_SM], writes=[r_SM])
            P.op("dve", stt(xt, xt, ssq, XTN[0:n, :], ALU.mult, ALU.mult), reads=[r_SM, r_XP1[ti], r_XTN], writes=[r_XP1[ti]])
            orow = row0 - 1024 + c0
            P.dma("sp", osem, lambda e, xt=xt, orow=orow, n=n: e.dma_start(out=o_y[orow:orow + n, :], in_=xt), reads=[r_XP1[ti]])
        P.barrier()

    import os
    stop = os.environ.get("KSTOP", "")
    class _Stop(Exception):
        pass
    def chk(tag):
        if stop and tag == stop:
            raise _Stop()
    try:
        chk("setup")
        for g_ in groups:
            do_group(g_, chk)
    except _Stop:
        pass

    def st(dst, src, res):
        P.dma("sp", osem, lambda e: e.dma_start(out=dst, in_=src), reads=res)
    st(o_cp[:, :], XLT[:].rearrange("p a b -> p (a b)"), [r_XLT])
    st(o_hp[:, :], HC[:], [r_HC])
    st(o_sp[:, :], S5C[:].rearrange("p a b -> p (a b)"), [r_S5C])
    st(o_cs[:, :], OCS[:].rearrange("p a b c -> p (a b c)"), [r_out])
    st(o_hs[:, :], OHS[:].rearrange("p a b -> p (a b)"), [r_out])
    st(o_ss[:, :], OSS[:].rearrange("p a b c -> p (a b c)"), [r_out])
    fin = P.dcnt[osem]
    P.q["sp"].append(lambda e: e.wait_ge(P.sems[osem], fin))

    with nc.Block() as block:
        @block.tensor
        def _(e):
            for f in P.q["pe"]:
                f(e)

        @block.scalar
        def _(e):
            for f in P.q["act"]:
                f(e)

        @block.vector
        def _(e):
            for f in P.q["dve"]:
                f(e)

        @block.gpsimd
        def _(e):
            for f in P.q["pool"]:
                f(e)

        @block.sync
        def _(e):
            for f in P.q["sp"]:
                f(e)
    es.close()
    return nc


_NC = None


def kernel(**inp):
    global _NC
    f32 = np.float32
    g = lambda k: np.asarray(inp[k], dtype=f32)
    xp = g("x_prompt"); xs = g("x_sample").reshape(128, D)
    conv = g("state_lru_conv")[0]; h0 = g("state_lru_h")[0]
    s5r = g("state_s5_re")[0].reshape(128, 4096); s5i = g("state_s5_im")[0].reshape(128, 4096)
    ident = np.eye(128, dtype=f32)
    maskc = np.zeros((128, 4, 128), f32)
    for jl in range(4):
        for gl in range(2):
            g8 = 2 * jl + gl
            maskc[g8 * 16:(g8 + 1) * 16, jl, gl * 64:(gl + 1) * 64] = 1.0
    maskc = maskc.reshape(128, 512)
    def fm16(v):
        return np.ascontiguousarray(v.reshape(16, 128).T)
    def st32(v):
        return np.ascontiguousarray(v.reshape(32, 128).T)
    ldt = np.repeat(g("s5_log_dt")[0], 64)
    pvec = np.zeros((128, 7, 16), f32)
    pvec[:, 0] = fm16(g("lru_conv_b")[0]); pvec[:, 1] = fm16(g("lru_ba")[0].reshape(-1)); pvec[:, 2] = fm16(g("lru_bx")[0].reshape(-1))
    pvec[:, 3] = fm16(g("lru_lambda")[0]); pvec[:, 4] = fm16(g("norm_mix_g")[0]); pvec[:, 5] = fm16(g("norm_ffn_g")[0])
    convw = np.ascontiguousarray(g("lru_conv_w")[0].reshape(4, 16, 128).transpose(2, 1, 0)).reshape(128, 64)
    bre = np.ascontiguousarray(g("s5_b_re")[0].reshape(32, 128, 16).transpose(1, 0, 2)).reshape(128, 512)
    bim = np.ascontiguousarray(g("s5_b_im")[0].reshape(32, 128, 16).transpose(1, 0, 2)).reshape(128, 512)
    cre = np.ascontiguousarray(g("s5_c_re")[0].reshape(8, 128, 64).transpose(1, 0, 2)).reshape(128, 512)
    cim = np.ascontiguousarray(g("s5_c_im")[0].reshape(8, 128, 64).transpose(1, 0, 2)).reshape(128, 512)
    s5d = np.ascontiguousarray(g("s5_d")[0].reshape(8, 128).T)
    gfin = np.ascontiguousarray(np.broadcast_to(g("norm_final_g")[None, :], (128, D)))
    shared = dict(
        ident=ident, maskc=maskc, ldt=st32(ldt), lre=st32(g("s5_lambda_re")[0].reshape(-1)), lim=st32(g("s5_lambda_im")[0].reshape(-1)),
        bre=bre, bim=bim, cre=cre, cim=cim, s5d=s5d, convw=convw, pvec=pvec.reshape(128, 112), gfin=gfin,
        w_in=g("w_in")[0], lru_wa=g("lru_wa")[0].reshape(2048, 256), lru_wx=g("lru_wx")[0].reshape(2048, 256),
        lru_proj=g("lru_proj")[0], s5_glu_wv=g("s5_glu_wv")[0], s5_glu_wg=g("s5_glu_wg")[0], w_out=g("w_out")[0],
        ffn_w_gate=g("ffn_w_gate")[0], ffn_w_up=g("ffn_w_up")[0], ffn_w_down=g("ffn_w_down")[0],
    )
    in_maps = []
    for c in range(8):
        b, half = c // 2, c % 2
        xf = np.zeros((2064, D), f32)
        if half == 1:
            xf[0:1024] = xp[b, 0:1024]
        xf[1024:2048] = xp[b, half * 1024:(half + 1) * 1024]
        xf[2048:2064] = xs[c * 16:(c + 1) * 16]
        flag = np.zeros((128, 2), f32); flag[:, 0] = float(half); flag[:, 1] = 1.0 - float(half)
        sl = slice(c * 16, (c + 1) * 16)
        csT = np.ascontiguousarray(conv[sl].reshape(16, 3, 16, 128).transpose(3, 2, 0, 1)).reshape(128, 768)
        h0T = np.ascontiguousarray(h0[sl].reshape(16, 16, 128).transpose(2, 1, 0)).reshape(128, 256)
        s5rT = np.ascontiguousarray(s5r[sl].reshape(16, 32, 128).transpose(2, 1, 0)).reshape(128, 512)
        s5iT = np.ascontiguousarray(s5i[sl].reshape(16, 32, 128).transpose(2, 1, 0)).reshape(128, 512)
        m = dict(shared); m.update(xf=xf, flag=flag, csT=csT, h0T=h0T, s5rT=s5rT, s5iT=s5iT)
        in_maps.append(m)
    if _NC is None:
        _NC = build_program()
    res = run_bass_kernel_spmd(_NC, in_maps, core_ids=list(range(8)))
    R = res.results
    y_prompt = np.zeros((4, 2048, D), f32); y_sample = np.zeros((128, 1, D), f32)
    convp = np.zeros((1, 4, 3, D), f32); hp = np.zeros((1, 4, D), f32)
    srp = np.zeros((1, 4, 64, 64), f32); sip = np.zeros((1, 4, 64, 64), f32)
    convs = np.zeros((1, 128, 3, D), f32); hs = np.zeros((1, 128, D), f32)
    srs = np.zeros((1, 128, 64, 64), f32); sis = np.zeros((1, 128, 64, 64), f32)
    for c in range(8):
        b, half = c // 2, c % 2
        r = R[c]
        y_prompt[b, half * 1024:(half + 1) * 1024] = r["o_y"][0:1024]
        y_sample[c * 16:(c + 1) * 16, 0] = r["o_y"][1024:1040]
        sl = slice(c * 16, (c + 1) * 16)
        convs[0, sl] = r["o_cs"].reshape(128, 16, 16, 3).transpose(2, 3, 1, 0).reshape(16, 3, D)
        hs[0, sl] = r["o_hs"].reshape(128, 16, 16).transpose(2, 1, 0).reshape(16, D)
        ss = r["o_ss"].reshape(128, 2, 32, 16)
        srs[0, sl] = ss[:, 0].transpose(2, 1, 0).reshape(16, 64, 64)
        sis[0, sl] = ss[:, 1].transpose(2, 1, 0).reshape(16, 64, 64)
        if half == 1:
            convp[0, b] = r["o_cp"].reshape(128, 16, 3).transpose(2, 1, 0).reshape(3, D)
            hp[0, b] = r["o_hp"].T.reshape(D)
            sp = r["o_sp"].reshape(128, 32, 2)
            srp[0, b] = sp[:, :, 0].T.reshape(64, 64)
            sip[0, b] = sp[:, :, 1].T.reshape(64, 64)
    return (y_prompt, y_sample, convp, hp, srp, sip, convs, hs, srs, sis)
```

```python
import math, os
from contextlib import ExitStack
import numpy as np
import concourse.bass as bass
import concourse.mybir as mybir
from concourse.bass_utils import run_bass_kernel_spmd

F32 = mybir.dt.float32
BF16 = mybir.dt.bfloat16
I32 = mybir.dt.int32
AF = mybir.ActivationFunctionType
ALU = mybir.AluOpType

D = 2048
DFF = 5632
NCH = 16
NS = 16
NT = 528
SAME_SYNC = True
GELU_C0 = math.sqrt(2.0 / math.pi)
GELU_C1 = 0.044715


class Res:
    __slots__ = ("name", "w", "r")

    def __init__(self, name):
        self.name = name
        self.w = None
        self.r = []


class Prog:
    ENG = ["pe", "act", "dve", "pool", "sp"]

    def __init__(self, nc, es):
        self.nc = nc
        self.es = es
        self.q = {e: [] for e in self.ENG}
        self.cnt = {e: 0 for e in self.ENG}
        self.sems = {}
        self.seen = {e: {} for e in self.ENG}
        self.dcnt = {}
        for e in ["pe", "act", "dve"]:
            self.sems[e] = es.enter_context(nc.semaphore("s_" + e))

    def dma_sem(self, name):
        self.sems[name] = self.es.enter_context(self.nc.semaphore(name))
        self.dcnt[name] = 0
        return name

    def _waits(self, eng, reads, writes, after=()):
        need = {}
        def add(m):
            if m is None:
                return
            k, v = m
            if k == eng and not SAME_SYNC:
                return
            if v > need.get(k, 0):
                need[k] = v
        for m in after:
            add(m)
        for r in reads:
            add(r.w)
        for w in writes:
            add(w.w)
            for m in w.r:
                add(m)
        out = []
        for k, v in need.items():
            if self.seen[eng].get(k, 0) >= v:
                continue
            self.seen[eng][k] = v
            out.append((k, v))
        return out

    def _push_waits(self, eng, waits):
        for k, v in waits:
            sem = self.sems[k]
            self.q[eng].append(lambda e, sem=sem, v=v: e.wait_ge(sem, v))

    def op(self, eng, fn, reads=(), writes=(), after=()):
        waits = self._waits(eng, reads, writes, after)
        self._push_waits(eng, waits)
        self.cnt[eng] += 1
        c = self.cnt[eng]
        sem = self.sems[eng]
        self.q[eng].append(lambda e, fn=fn, sem=sem: fn(e).then_inc(sem, 1))
        m = (eng, c)
        for r in reads:
            r.r.append(m)
        for w in writes:
            w.w = m
            w.r = []
        return m

    def dma(self, eng, semname, fn, reads=(), writes=()):
        waits = self._waits(eng, reads, writes)
        self._push_waits(eng, waits)
        self.dcnt[semname] += 16
        v = self.dcnt[semname]
        sem = self.sems[semname]
        self.q[eng].append(lambda e, fn=fn, sem=sem: fn(e).then_inc(sem, 16))
        m = (semname, v)
        for r in reads:
            r.r.append(m)
        for w in writes:
            w.w = m
            w.r = []

    def barrier(self, engs=("pe", "act", "dve", "sp")):
        for e in engs:
            waits = []
            for k in ["pe", "act", "dve"]:
                v = self.cnt[k]
                if k == e and not SAME_SYNC:
                    continue
                if v > 0 and self.seen[e].get(k, 0) < v:
                    self.seen[e][k] = v
                    waits.append((k, v))
            for k, v in self.dcnt.items():
                if k.startswith("w"):
                    continue
                if v > 0 and self.seen[e].get(k, 0) < v:
                    self.seen[e][k] = v
                    waits.append((k, v))
            self._push_waits(e, waits)


def build_program():
    nc = bass.Bass("TRN2", target_bir_lowering=False)
    es = ExitStack()

    def din(name, shape, dt=F32):
        return nc.dram_tensor(name, list(shape), dt, kind="ExternalInput").ap()

    def dout(name, shape):
        return nc.dram_tensor(name, list(shape), F32, kind="ExternalOutput").ap()

    xf = din("xf", [2064, D])
    flag_d = din("flag", [128, 2])
    ident_d = din("ident", [128, 128])
    maskc_d = din("maskc", [128, 4 * 128])
    cs_d = din("csT", [128, NCH * NS * 3])
    h0_d = din("h0T", [128, NCH * NS])
    s5r_d = din("s5rT", [128, 32 * NS])
    s5i_d = din("s5iT", [128, 32 * NS])
    ldt_d = din("ldt", [128, 32])
    lre_d = din("lre", [128, 32])
    lim_d = din("lim", [128, 32])
    bre_d = din("bre", [128, 32 * 16])
    bim_d = din("bim", [128, 32 * 16])
    cre_d = din("cre", [128, 8 * 64])
    cim_d = din("cim", [128, 8 * 64])
    sd_d = din("s5d", [128, 8])
    cw_d = din("convw", [128, NCH * 4])
    pv_d = din("pvec", [128, 7 * NCH])
    gfin_d = din("gfin", [128, D])
    w_in = din("w_in", [D, 7168])
    lru_wa = din("lru_wa", [8 * 256, 256])
    lru_wx = din("lru_wx", [8 * 256, 256])
    lru_proj = din("lru_proj", [D, D])
    wv_d = din("s5_glu_wv", [1024, D])
    wg_d = din("s5_glu_wg", [1024, D])
    w_out = din("w_out", [D, D])
    w_gate = din("ffn_w_gate", [D, DFF])
    w_up = din("ffn_w_up", [D, DFF])
    w_down = din("ffn_w_down", [DFF, D])

    o_y = dout("o_y", [1040, D])
    o_cp = dout("o_cp", [128, NCH * 3])
    o_hp = dout("o_hp", [128, NCH])
    o_sp = dout("o_sp", [128, 64])
    o_cs = dout("o_cs", [128, NCH * NS * 3])
    o_hs = dout("o_hs", [128, NCH * NS])
    o_ss = dout("o_ss", [128, 2 * 32 * NS])

    def sb(name, shape, dt=F32):
        return es.enter_context(nc.sbuf_tensor(name, list(shape), dt))

    P = Prog(nc, es)
    PS = es.enter_context(nc.psum_tensor("ps", [128, 8, 512], F32))
    ps_res = [Res("ps%d" % i) for i in range(8)]
    ps_ptr = [0]

    def bank():
        b = ps_ptr[0] % 6
        ps_ptr[0] += 1
        return PS[:, b, :], ps_res[b]

    def ybank(i):
        return PS[:, 6 + i, :], ps_res[6 + i]

    UT = sb("UT", [128, NCH, NT], BF16)
    R2 = sb("R2", [128, 44 * NT], BF16)
    R1 = sb("R1", [128, 5 * D], F32)
    WR = sb("WR", [128, 4, 16, 256], BF16)
    WB = sb("WB", [128, 32, 2, 128], BF16)
    CW = sb("CW", [128, 32, 2, 128], BF16)
    IDN = sb("IDN", [128, 128])
    FLG = sb("FLG", [128, 2])
    CS = sb("CS", [128, NCH, NS, 3])
    H0 = sb("H0", [128, NCH, NS])
    S5R = sb("S5R", [128, 32, NS])
    S5I = sb("S5I", [128, 32, NS])
    SD = sb("SD", [128, 8])
    CWV = sb("CWV", [128, NCH, 4])
    PV = sb("PV", [128, 7, NCH])
    LP2 = sb("LP2", [128, 11, 32, 4])
    HC = sb("HC", [128, NCH])
    XLT = sb("XLT", [128, NCH, 3])
    S5C = sb("S5C", [128, 32, 2])
    OCS = sb("OCS", [128, NCH, NS, 3])
    OHS = sb("OHS", [128, NCH, NS])
    OSS = sb("OSS", [128, 2, 32, NS])
    XTN = sb("XTN", [128, D])
    TGF = XTN[:, 0:1024].rearrange("p (a b) -> p a b", b=512)
    GW = sb("GW", [128, 2, 2, 2, 256], BF16)
    r_GW = [Res("GW0"), Res("GW1")]
    r_XTN_g = Res("XTN")
    SM = sb("SM", [128, 64])
    HALFB = sb("HALFB", [128, 2 * NCH])
    CNG = sb("CNG", [128, 2 * NCH])

    r_UT = [Res("UT%d" % k) for k in range(NCH)]
    r_WR = [Res("WR%d" % k) for k in range(4)]
    r_WBCW = Res("WBCW")
    r_small = Res("small")
    r_HC = Res("HC")
    r_XLT = Res("XLT")
    r_S5C = Res("S5C")
    r_out = Res("outs")
    r_SM = Res("SM")

    HS = R2[:, 0:NCH * NT].rearrange("p (a b) -> p a b", b=NT)
    MG = R2[:, NCH * NT:2 * NCH * NT].rearrange("p (a b) -> p a b", b=NT)
    VT = R2[:, 32 * NT:40 * NT].rearrange("p (a b) -> p a b", b=NT)
    XCB = R2[:, 40 * NT:42 * NT].rearrange("p (a b) -> p a b", b=NT)
    XS = R2[:, 42 * NT:44 * NT].rearrange("p (a b) -> p a b", b=NT)
    HM = R2[:, :].rearrange("p (a b) -> p a b", b=NT)
    r_HS = [Res("HS%d" % k) for k in range(NCH)]
    r_MG = [Res("MG%d" % k) for k in range(NCH)]
    r_VT = [Res("VT%d" % k) for k in range(8)]
    r_XCB = [Res("XCB%d" % k) for k in range(2)]
    r_XS = [Res("XS%d" % k) for k in range(2)]
    r_HM = [Res("HM%d" % k) for k in range(44)]
    XP1 = R1[:, :].rearrange("p (a b) -> p a b", b=D)
    r_XP1 = [Res("XP1_%d" % k) for k in range(5)]

    def r1v(off, a, b):
        return R1[:, off:off + a * b].rearrange("p (a b) -> p a b", b=b)

    wsem = [P.dma_sem("w%d" % i) for i in range(4)]
    xsem = [P.dma_sem("x%d" % i) for i in range(5)]
    osem = P.dma_sem("o")
    gwsem = [P.dma_sem("wg0"), P.dma_sem("wg1")]
    isem = P.dma_sem("i")
    wr_ptr = [0]
    cssem = [P.dma_sem("wcs%d" % i) for i in range(4)]
    bsem = [P.dma_sem("wb%d" % i) for i in range(4)]
    SCR = nc.dram_tensor("wscr", [40, 128, 4096], BF16, kind="Internal").ap()
    r_scr = [Res("scr%d" % i) for i in range(40)]

    def wslot(skip=()):
        while True:
            s = wr_ptr[0] % 4
            wr_ptr[0] += 1
            if s not in skip:
                return s

    def wload(s, dst_ap, src_ap):
        P.dma("pool", wsem[s], lambda e: e.dma_start(out=dst_ap, in_=src_ap), writes=[r_WR[s]])

    NCACHE = 40
    back_jobs = []

    def _jobs():
        for sl in range(8):
            back_jobs.append([(0, 16, wview(w_in, 0, D, 3072 + sl * 256, 256))])
            back_jobs.append([(0, 16, wview(lru_proj, 0, D, sl * 256, 256))])
            back_jobs.append([(0, 16, wview(w_in, 0, D, 5120 + sl * 256, 256))])
            back_jobs.append([(0, 8, wview(wv_d, 0, 1024, sl * 256, 256)), (8, 16, wview(wg_d, 0, 1024, sl * 256, 256))])
        for nb in range(8):
            back_jobs.append([(0, 16, wview(w_out, 0, D, nb * 256, 256))])
        for fp in range(22):
            back_jobs.append([(0, 16, wview(w_gate, 0, D, fp * 256, 256))])
            back_jobs.append([(0, 16, wview(w_up, 0, D, fp * 256, 256))])
        for nb in range(8):
            for (f0, nf) in [(0, 16), (16, 16), (32, 12)]:
                back_jobs.append([(0, nf, wview(w_down, f0 * 128, nf * 128, nb * 256, 256))])

    conv_ptr = [0]
    back_ptr = [0]
    pinned = []

    def convert_some(n):
        for _ in range(n):
            j = conv_ptr[0]
            if j >= NCACHE:
                return
            conv_ptr[0] += 1
            s = wslot(skip=tuple(pinned))
            for (r0, r1, srcap) in back_jobs[j]:
                wload(s, WR[:, s, r0:r1, :], srcap)
            P.dma("sp", cssem[s], lambda e, s=s, j=j: e.dma_start(out=SCR[j], in_=WR[:, s, :, :].rearrange("p a b -> p (a b)")),
                  reads=[r_WR[s]], writes=[r_scr[j]])

    def bload(s):
        j = back_ptr[0]
        back_ptr[0] += 1
        if j < NCACHE:
            P.dma("sp", bsem[s], lambda e, s=s, j=j: e.dma_start(out=WR[:, s, :, :].rearrange("p a b -> p (a b)"), in_=SCR[j]),
                  reads=[r_scr[j]], writes=[r_WR[s]])
        else:
            for (r0, r1, srcap) in back_jobs[j]:
                wload(s, WR[:, s, r0:r1, :], srcap)

    def wview(w, r0, nr, c0, ncols):
        return w[r0:r0 + nr, c0:c0 + ncols].rearrange("(kt p) n -> p kt n", p=128)

    _jobs()
    assert len(back_jobs) == 108
    def ld(dst, src):
        P.dma("sp", isem, lambda e: e.dma_start(out=dst, in_=src), writes=[r_small])

    ld(IDN[:], ident_d[:, :])
    ld(FLG[:], flag_d[:, :])
    ld(CS[:].rearrange("p a b c -> p (a b c)"), cs_d[:, :])
    ld(H0[:].rearrange("p a b -> p (a b)"), h0_d[:, :])
    ld(S5R[:].rearrange("p a b -> p (a b)"), s5r_d[:, :])
    ld(S5I[:].rearrange("p a b -> p (a b)"), s5i_d[:, :])
    ld(SD[:], sd_d[:, :])
    ld(CWV[:].rearrange("p a b -> p (a b)"), cw_d[:, :])
    ld(PV[:].rearrange("p a b -> p (a b)"), pv_d[:, :])
    o = [0]

    def t1(n):
        v = R1[:, o[0]:o[0] + n]
        o[0] += n
        return v

    LDT = t1(32); LRE = t1(32); LIM = t1(32)
    BRE = t1(512); BIM = t1(512); CRE = t1(512); CIM = t1(512)
    MASKC = t1(512)
    ld(LDT, ldt_d[:, :]); ld(LRE, lre_d[:, :]); ld(LIM, lim_d[:, :])
    ld(BRE, bre_d[:, :]); ld(BIM, bim_d[:, :]); ld(CRE, cre_d[:, :]); ld(CIM, cim_d[:, :])
    ld(MASKC, maskc_d[:, :])
    rs = [r_small]

    def V(fn, reads=None, writes=None):
        P.op("dve", fn, reads=rs if reads is None else reads, writes=rs if writes is None else writes)

    def A(fn, reads=None, writes=None):
        P.op("act", fn, reads=rs if reads is None else reads, writes=rs if writes is None else writes)

    def ts(out, in0, s1, s2, op0, op1=None):
        if op1 is None:
            return lambda e: e.tensor_scalar(out=out, in0=in0, scalar1=s1, scalar2=None, op0=op0)
        return lambda e: e.tensor_scalar(out=out, in0=in0, scalar1=s1, scalar2=s2, op0=op0, op1=op1)

    def tt(out, in0, in1, op):
        return lambda e: e.tensor_tensor(out=out, in0=in0, in1=in1, op=op)

    def stt(out, in0, s, in1, op0, op1):
        return lambda e: e.scalar_tensor_tensor(out=out, in0=in0, scalar=s, in1=in1, op0=op0, op1=op1)

    def act(out, in_, func, bias=None, scale=None, accum_out=None):
        kw = {}
        if bias is not None:
            kw["bias"] = bias
        if scale is not None:
            kw["scale"] = scale
        if accum_out is not None:
            kw["accum_out"] = accum_out
        return lambda e: e.activation(out=out, in_=in_, func=func, **kw)

    def poly_exp(out, x, nterm, tmp):
        V(ts(out, x, 1.0 / nterm, 1.0, ALU.mult, ALU.add))
        for k in range(nterm - 1, 0, -1):
            V(tt(tmp, out, x, ALU.mult))
            V(ts(out, tmp, 1.0 / k, 1.0, ALU.mult, ALU.add))

    T = [t1(32) for _ in range(16)]
    DT, AL, TH, MAG, C_, S_, ABR, ABI, COR, COI = T[:10]
    X0, X1, X2, X3, X4, X5 = T[10:16]
    V(ts(X0, LDT, 1.0 / 32, None, ALU.mult))
    poly_exp(DT, X0, 9, X1)
    for _ in range(5):
        V(tt(DT, DT, DT, ALU.mult))
    V(tt(AL, LRE, DT, ALU.mult))
    V(tt(TH, LIM, DT, ALU.mult))
    poly_exp(MAG, AL, 7, X1)
    KI = sb("KI", [128, 32], I32)
    V(ts(X0, TH, 1.0 / (2 * math.pi), None, ALU.mult))
    V(lambda e: e.tensor_copy(out=KI[:], in_=X0))
    V(lambda e: e.tensor_copy(out=X1, in_=KI[:]))
    C1 = float(np.float32(2 * math.pi))
    C2 = 2 * math.pi - C1
    V(stt(X2, X1, -C1, TH, ALU.mult, ALU.add))
    V(stt(X2, X1, -C2, X2, ALU.mult, ALU.add))
    V(ts(X2, X2, 0.25, None, ALU.mult))
    V(tt(X3, X2, X2, ALU.mult))
    def poly_trig(out, q2, denoms):
        V(ts(out, q2, -1.0 / denoms[-1], 1.0, ALU.mult, ALU.add))
        for dd in reversed(denoms[:-1]):
            V(tt(X5, out, q2, ALU.mult))
            V(ts(out, X5, -1.0 / dd, 1.0, ALU.mult, ALU.add))
    poly_trig(X4, X3, [6.0, 20.0, 42.0, 72.0, 110.0, 156.0, 210.0])
    V(tt(S_, X4, X2, ALU.mult))
    poly_trig(C_, X3, [2.0, 12.0, 30.0, 56.0, 90.0, 132.0, 182.0])
    for _ in range(2):
        V(tt(X0, C_, C_, ALU.mult))
        V(tt(X1, S_, S_, ALU.mult))
        V(tt(X4, S_, C_, ALU.mult))
        V(tt(C_, X0, X1, ALU.subtract))
        V(ts(S_, X4, 2.0, None, ALU.mult))
    V(tt(ABR, MAG, C_, ALU.mult))
    V(tt(ABI, MAG, S_, ALU.mult))
    V(ts(X0, ABR, -1.0, None, ALU.add))
    V(tt(X1, LRE, LRE, ALU.mult))
    V(tt(X2, LIM, LIM, ALU.mult))
    V(tt(X1, X1, X2, ALU.add))
    V(lambda e: e.reciprocal(out=X1, in_=X1))
    V(tt(X2, X0, LRE, ALU.mult))
    V(tt(X3, ABI, LIM, ALU.mult))
    V(tt(X2, X2, X3, ALU.add))
    V(tt(COR, X2, X1, ALU.mult))
    V(tt(X2, ABI, LRE, ALU.mult))
    V(tt(X3, X0, LIM, ALU.mult))
    V(tt(X2, X2, X3, ALU.subtract))
    V(tt(COI, X2, X1, ALU.mult))
    V(lambda e: e.tensor_copy(out=LP2[:, 0, :, 0], in_=ABR))
    V(lambda e: e.tensor_copy(out=LP2[:, 0, :, 2], in_=ABI))
    for l in range(1, 11):
        pr, pi_ = LP2[:, l - 1, :, 0], LP2[:, l - 1, :, 2]
        V(tt(X0, pr, pr, ALU.mult))
        V(tt(X1, pi_, pi_, ALU.mult))
        V(tt(LP2[:, l, :, 0], X0, X1, ALU.subtract))
        V(tt(X2, pr, pi_, ALU.mult))
        V(ts(LP2[:, l, :, 2], X2, 2.0, None, ALU.mult))
    V(lambda e: e.tensor_copy(out=LP2[:, :, :, 1], in_=LP2[:, :, :, 0]))
    V(ts(LP2[:, :, :, 3], LP2[:, :, :, 2], -1.0, None, ALU.mult))
    BBR = t1(512); BBI = t1(512); TB = t1(512)
    def v3(x):
        return x.rearrange("p (a b) -> p a b", b=16)
    def bc(x):
        return x.unsqueeze(2).broadcast_to([128, 32, 16])
    V(tt(v3(BBR), v3(BRE), bc(COR), ALU.mult))
    V(tt(v3(TB), v3(BIM), bc(COI), ALU.mult))
    V(tt(BBR, BBR, TB, ALU.subtract))
    V(tt(v3(BBI), v3(BIM), bc(COR), ALU.mult))
    V(tt(v3(TB), v3(BRE), bc(COI), ALU.mult))
    V(tt(BBI, BBI, TB, ALU.add))
    Z = t1(32 * 128)
    Z4 = Z.rearrange("p (k j c) -> p k j c", k=8, j=4)
    for ri, BB in enumerate([BBR, BBI]):
        V(lambda e: e.memset(Z, 0.0))
        BB4 = BB.rearrange("p (k j h) -> p k j h", k=8, j=4)
        for jl in range(4):
            for gl in range(2):
                pp = slice(64 * gl, 64 * gl + 64)
                c0 = (2 * jl + gl) * 16
                V(lambda e, pp=pp, jl=jl, c0=c0, BB4=BB4: e.tensor_copy(out=Z4[pp, :, jl, c0:c0 + 16], in_=BB4[pp, :, jl, :]))
        for j in range(32):
            pb, rb = bank()
            P.op("pe", lambda e, j=j, pb=pb: e.transpose(out=pb[:, 0:128], in_=Z[:, j * 128:(j + 1) * 128], identity=IDN[:]),
                 reads=rs, writes=[rb])
            P.op("act", act(WB[:, j, ri, :], pb[:, 0:128], AF.Copy), reads=[rb], writes=[r_WBCW])
    Y = t1(128)
    for ri, CC in enumerate([CRE, CIM]):
        for j in range(32):
            kt, jl = j // 4, j % 4
            cc = CC[:, kt * 64:(kt + 1) * 64]
            mk = MASKC[:, jl * 128:(jl + 1) * 128]
            V(tt(Y[:, 0:64], cc, mk[:, 0:64], ALU.mult))
            V(tt(Y[:, 64:128], cc, mk[:, 64:128], ALU.mult))
            pb, rb = bank()
            P.op("pe", lambda e, pb=pb: e.transpose(out=pb[:, 0:128], in_=Y, identity=IDN[:]), reads=rs, writes=[rb])
            P.op("act", act(CW[:, j, ri, :], pb[:, 0:128], AF.Copy, scale=(1.0 if ri == 0 else -1.0)),
                 reads=[rb], writes=[r_WBCW])
    LAM = PV[:, 3, :]
    A(act(X0[:, 0:NCH], LAM, AF.Exp, scale=-1.0))
    V(ts(X0[:, 0:NCH], X0[:, 0:NCH], 1.0, None, ALU.add))
    A(act(X1[:, 0:NCH], X0[:, 0:NCH], AF.Ln))
    V(ts(CNG[:, 0:NCH], X1[:, 0:NCH], -8.0, None, ALU.mult))
    V(ts(CNG[:, NCH:2 * NCH], X1[:, 0:NCH], -4.0, None, ALU.mult))
    V(ts(HALFB[:, 0:NCH], PV[:, 1, :], 0.5, None, ALU.mult))
    V(ts(HALFB[:, NCH:2 * NCH], PV[:, 2, :], 0.5, None, ALU.mult))
    V(lambda e: e.memset(HC[:], 0.0), writes=[r_HC])
    V(lambda e: e.memset(XLT[:], 0.0), writes=[r_XLT])
    V(lambda e: e.memset(S5C[:], 0.0), writes=[r_S5C])
    EPS = 1e-6
    EPSB = sb("EPSB", [128, 1])
    V(lambda e: e.memset(EPSB[:], EPS))
    ONEB = sb("ONEB", [128, 1])
    V(lambda e: e.memset(ONEB[:], 1.0))
    P.barrier()

    groups = [
        dict(name="P1", row0=0, npr=512, ns=0, full=False, first=True, mask=False),
        dict(name="P2", row0=512, npr=512, ns=0, full=False, first=False, mask=False),
        dict(name="M1", row0=1024, npr=512, ns=0, full=True, first=False, mask=True),
        dict(name="M2", row0=1536, npr=512, ns=NS, full=True, first=False, mask=False),
    ]

    def rms_stats(src, np_, junk, col):
        ssq = SM[0:np_, col:col + 1]
        A(act(junk, src, AF.Square, accum_out=ssq), reads=[r_SM], writes=[r_SM])
        A(act(ssq, ssq, AF.Sqrt, bias=EPSB[0:np_, :], scale=1.0 / D), reads=[r_SM], writes=[r_SM])
        V(lambda e: e.reciprocal(out=ssq, in_=ssq), reads=[r_SM], writes=[r_SM])
        return ssq

    def do_group(g, chk):
        npr, ns, full = g["npr"], g["ns"], g["full"]
        N = npr + ns
        ntiles = [(c, 512) for c in range(0, npr, 512)] + ([(npr, ns)] if ns else [])
        ttiles = [(c, 128) for c in range(0, npr, 128)] + ([(npr, ns)] if ns else [])
        row0 = g["row0"]

        XT = r1v(0, 2, D)
        JK = R1[:, 2 * D:3 * D]
        r_XT = [Res("XT0"), Res("XT1")]
        r_JK = Res("JK")
        for ti, (c0, n) in enumerate(ttiles):
            s = ti % 2
            xt = XT[0:n, s, :]
            P.dma("sp", xsem[s], lambda e, xt=xt, c0=c0, n=n: e.dma_start(out=xt, in_=xf[row0 + c0:row0 + c0 + n, :]),
                  writes=[r_XT[s]])
            chk("A1")
            ssq = SM[0:n, ti:ti + 1]
            P.op("act", act(JK[0:n, :], xt, AF.Square, accum_out=ssq), reads=[r_XT[s]], writes=[r_JK, r_SM])
            chk("A2")
            P.op("act", act(ssq, ssq, AF.Sqrt, bias=EPSB[0:n, :], scale=1.0 / D), reads=[r_SM], writes=[r_SM])
            chk("A3")
            P.op("dve", lambda e, ssq=ssq: e.reciprocal(out=ssq, in_=ssq), reads=[r_SM], writes=[r_SM])
            P.op("act", act(xt, xt, AF.Identity, scale=ssq), reads=[r_SM, r_XT[s]], writes=[r_XT[s]])
            chk("A4")
            for k4 in range(4):
                pb, rb = bank()
                def tr(e, pb=pb, xt=xt, k4=k4, n=n):
                    ins = None
                    for kk in range(4):
                        k = k4 * 4 + kk
                        ins = e.transpose(out=pb[:, kk * 128:kk * 128 + n], in_=xt[:, k * 128:(k + 1) * 128],
                                          identity=IDN[0:n, 0:n])
                    return ins
                P.op("pe", tr, reads=[r_XT[s]], writes=[rb])
                chk("A5")
                for kk in range(4):
                    k = k4 * 4 + kk
                    eng = "dve"
                    if eng == "dve":
                        P.op("dve", ts(UT[:, k, c0:c0 + n], pb[:, kk * 128:kk * 128 + n], PV[:, 4, k:k + 1], None, ALU.mult),
                             reads=[rb], writes=[r_UT[k]])
                        chk("A6")
                    else:
                        P.op("act", act(UT[:, k, c0:c0 + n], pb[:, kk * 128:kk * 128 + n], AF.Identity, scale=PV[:, 4, k:k + 1]),
                             reads=[rb], writes=[r_UT[k]])
                        chk("A7")
            chk("A8")
        P.barrier()
        chk(g["name"] + "A")

        NP3 = N + 3
        off = [0]
        def tl(a, b):
            v = r1v(off[0], a, b)
            off[0] += a * b
            return v
        XL = tl(2, NP3); XC = tl(2, N); TR = tl(2, N); TI = tl(2, N); AA = tl(2, N)
        G1 = R2[:, NCH * NT:NCH * NT + 4 * N].bitcast(F32).rearrange("p (a b) -> p a b", b=N)
        HF = TR
        A2 = TI
        X4 = R1[:, off[0]:off[0] + 8 * N].rearrange("p (t r n) -> p t r n", t=4, r=2)
        off[0] += 8 * N
        TM1 = R1[:, off[0]:off[0] + 256].rearrange("p (t r n) -> p t r n", t=4, r=2)
        TM2 = R1[:, off[0] + 256:off[0] + 384].rearrange("p (t n) -> p t n", t=4)
        TM3 = R1[:, off[0] + 384:off[0] + 512].rearrange("p (t n) -> p t n", t=4)
        off[0] += 512
        assert off[0] <= 5 * D, off[0]
        YT = XTN[:, 0:N]; Y2 = XTN[:, N:2 * N]; Y3 = XTN[:, 2 * N:3 * N]
        r_t = {n_: [Res(n_ + "0"), Res(n_ + "1")] for n_ in ["XL", "XC", "TR", "TI", "AA"]}
        r_t["G1"] = [Res("G1_0"), Res("G1_1")]
        r_t["HF"] = r_t["TR"]
        r_t["A2"] = r_t["TI"]
        XRB = XCB
        r_XRe = [Res("XRe0"), Res("XRe1")]
        r_XIm = [Res("XIm0"), Res("XIm1")]
        r_XSF = [Res("XSF0"), Res("XSF1")]
        r_Y = r_XTN_g
        L = 9
        gw_ptr = [0]

        def lru_block(hb, nxt=None):
            gb = gw_ptr[0] % 2
            gw_ptr[0] += 1
            P.dma("pool", gwsem[gb], lambda e: e.dma_start(out=GW[:, gb, 0, :, :], in_=lru_wa[hb * 256:(hb + 1) * 256, :].rearrange("(k p) n -> p k n", p=128)),
                  writes=[r_GW[gb]])
            P.dma("pool", gwsem[gb], lambda e: e.dma_start(out=GW[:, gb, 1, :, :], in_=lru_wx[hb * 256:(hb + 1) * 256, :].rearrange("(k p) n -> p k n", p=128)),
                  writes=[r_GW[gb]])
            s = wslot(skip=tuple(pinned))
            wload(s, WR[:, s, :, :], wview(w_in, 0, D, hb * 256, 256))
            for j in range(2):
                c = hb * 2 + j
                mc = j
                P.op("dve", lambda e, j=j, c=c: e.tensor_copy(out=XL[:, j, 0:3], in_=XLT[:, c, :]),
                     reads=[r_XLT], writes=[r_t["XL"][j]])
                if g["mask"]:
                    P.op("dve", ts(XL[:, j, 0:3], XL[:, j, 0:3], FLG[:, 0:1], None, ALU.mult),
                         reads=[r_small, r_t["XL"][j]], writes=[r_t["XL"][j]])
                for (n0, nn) in ntiles:
                    pb, rb = bank()
                    def mm(e, pb=pb, s=s, mc=mc, n0=n0, nn=nn):
                        ins = None
                        for k in range(NCH):
                            ins = e.matmul(out=pb[:, 0:nn], lhsT=WR[:, s, k, mc * 128:(mc + 1) * 128],
                                           rhs=UT[:, k, n0:n0 + nn], start=(k == 0), stop=(k == NCH - 1))
                        return ins
                    P.op("pe", mm, reads=[r_WR[s]] + r_UT, writes=[rb])
                    P.op("act", act(XL[:, j, 3 + n0:3 + n0 + nn], pb[:, 0:nn], AF.Copy), reads=[rb], writes=[r_t["XL"][j]])
            yield
            if nxt is not None:
                s5_inproj(nxt)
            yield
            for j in range(2):
                c = hb * 2 + j
                rr = [r_t["XL"][j], r_small]
                P.op("act", act(XC[:, j, 0:npr], XL[:, j, 3:3 + npr], AF.Identity, bias=PV[:, 0, c:c + 1],
                                scale=CWV[:, c, 3:4]), reads=rr, writes=[r_t["XC"][j]])
                if ns:
                    P.op("act", act(XC[:, j, npr:N], XL[:, j, 3 + npr:3 + N], AF.Identity, bias=PV[:, 0, c:c + 1], scale=CWV[:, c, 3:4]),
                         reads=rr + [r_t["XC"][j]], writes=[r_t["XC"][j]])
            yield
            for k in range(3):
                if k:
                    yield
                for j in range(2):
                    c = hb * 2 + j
                    rr = [r_t["XL"][j], r_small]
                    P.op("dve", stt(XC[:, j, 0:npr], XL[:, j, k:k + npr], CWV[:, c, k:k + 1], XC[:, j, 0:npr], ALU.mult, ALU.add),
                         reads=rr + [r_t["XC"][j]], writes=[r_t["XC"][j]])
                    if ns:
                        xs_ = XC[:, j, npr:N]
                        P.op("dve", stt(xs_, CS[:, c, :, k], CWV[:, c, k:k + 1], xs_, ALU.mult, ALU.add),
                             reads=rr + [r_t["XC"][j]], writes=[r_t["XC"][j]])
            for j in range(2):
                c = hb * 2 + j
                if ns:
                    P.op("dve", lambda e, c=c: e.tensor_copy(out=OCS[:, c, :, 0:2], in_=CS[:, c, :, 1:3]),
                         reads=[r_small], writes=[r_out])
                    P.op("dve", lambda e, c=c, j=j: e.tensor_copy(out=OCS[:, c, :, 2], in_=XL[:, j, 3 + npr:3 + N]),
                         reads=[r_t["XL"][j]], writes=[r_out])
                P.op("dve", lambda e, j=j, c=c: e.tensor_copy(out=XLT[:, c, :], in_=XL[:, j, npr:npr + 3]),
                     reads=[r_t["XL"][j]], writes=[r_XLT])
                P.op("act", act(XCB[:, j, 0:N], XC[:, j, 0:N], AF.Copy), reads=[r_t["XC"][j]], writes=[r_XCB[j]])
            yield
            for j in range(2):
                c = hb * 2 + j
                for (n0, nn) in ntiles:
                    pr_, rr_ = bank()
                    pi_, ri_ = bank()
                    def mg(e, pr_=pr_, pi_=pi_, j=j, n0=n0, nn=nn, gb=gb):
                        ins = None
                        for k in range(2):
                            ins = e.matmul(out=pr_[:, 0:nn], lhsT=GW[:, gb, 0, k, j * 128:(j + 1) * 128],
                                           rhs=XCB[:, k, n0:n0 + nn], start=(k == 0), stop=(k == 1))
                        for k in range(2):
                            ins = e.matmul(out=pi_[:, 0:nn], lhsT=GW[:, gb, 1, k, j * 128:(j + 1) * 128],
                                           rhs=XCB[:, k, n0:n0 + nn], start=(k == 0), stop=(k == 1))
                        return ins
                    P.op("pe", mg, reads=[r_GW[gb]] + r_XCB, writes=[rr_, ri_])
                    P.op("act", act(TR[:, j, n0:n0 + nn], pr_[:, 0:nn], AF.Tanh, bias=HALFB[:, c:c + 1], scale=0.5),
                         reads=[rr_, r_small], writes=[r_t["TR"][j]])
                    P.op("act", act(TI[:, j, n0:n0 + nn], pi_[:, 0:nn], AF.Tanh, bias=HALFB[:, NCH + c:NCH + c + 1], scale=0.5),
                         reads=[ri_, r_small], writes=[r_t["TI"][j]])
                P.op("act", act(AA[:, j, 0:N], TR[:, j, 0:N], AF.Exp, bias=CNG[:, NCH + c:NCH + c + 1], scale=CNG[:, NCH + c:NCH + c + 1]),
                     reads=[r_t["TR"][j], r_small], writes=[r_t["AA"][j]])
                yield
                P.op("dve", stt(G1[:, j, 0:N], TI[:, j, 0:N], 1.0, XC[:, j, 0:N], ALU.add, ALU.mult),
                     reads=[r_t["TI"][j], r_t["XC"][j]], writes=[r_t["G1"][j]])
                P.op("act", act(A2[:, j, 0:N], TR[:, j, 0:N], AF.Exp, bias=CNG[:, c:c + 1], scale=CNG[:, c:c + 1]),
                     reads=[r_t["TR"][j], r_t["G1"][j], r_small], writes=[r_t["A2"][j]])
            for j in range(2):
                P.op("act", act(A2[:, j, 0:N], A2[:, j, 0:N], AF.Sqrt, bias=ONEB[:, 0:1], scale=-1.0),
                     reads=[r_t["A2"][j]], writes=[r_t["A2"][j]])
            yield
            for j in range(2):
                c = hb * 2 + j
                if g["first"]:
                    P.op("dve", lambda e, j=j: e.memset(A2[:, j, 0:1], 1.0), reads=[r_t["A2"][j]], writes=[r_t["A2"][j]])
                if g["mask"]:
                    P.op("dve", ts(A2[:, j, 0:1], A2[:, j, 0:1], FLG[:, 0:1], FLG[:, 1:2], ALU.mult, ALU.add),
                         reads=[r_t["A2"][j], r_small], writes=[r_t["A2"][j]])
                    P.op("dve", ts(AA[:, j, 0:1], AA[:, j, 0:1], FLG[:, 0:1], None, ALU.mult),
                         reads=[r_t["AA"][j], r_small], writes=[r_t["AA"][j]])
                P.op("dve", stt(G1[:, j, 0:N], G1[:, j, 0:N], 0.5, A2[:, j, 0:N], ALU.mult, ALU.mult),
                     reads=[r_t["G1"][j], r_t["A2"][j]], writes=[r_t["G1"][j]])
            yield
            for j in range(2):
                c = hb * 2 + j
                P.op("dve", lambda e, j=j, c=c: e.tensor_tensor_scan(out=HF[:, j, 0:npr], data0=AA[:, j, 0:npr], data1=G1[:, j, 0:npr],
                                                                initial=HC[:, c:c + 1], op0=ALU.mult, op1=ALU.add),
                     reads=[r_t["AA"][j], r_t["G1"][j], r_HC], writes=[r_t["HF"][j]])
            yield
            for j in range(2):
                c = hb * 2 + j
                P.op("dve", lambda e, j=j, c=c: e.tensor_copy(out=HC[:, c:c + 1], in_=HF[:, j, npr - 1:npr]),
                     reads=[r_t["HF"][j]], writes=[r_HC])
                if ns:
                    hs_ = HF[:, j, npr:N]
                    P.op("dve", tt(hs_, AA[:, j, npr:N], H0[:, c, :], ALU.mult), reads=[r_t["AA"][j], r_small], writes=[r_t["HF"][j]])
                    P.op("dve", tt(hs_, hs_, G1[:, j, npr:N], ALU.add), reads=[r_t["HF"][j], r_t["G1"][j]], writes=[r_t["HF"][j]])
                    P.op("dve", lambda e, c=c, hs_=hs_: e.tensor_copy(out=OHS[:, c, :], in_=hs_), reads=[r_t["HF"][j]], writes=[r_out])
                if full:
                    P.op("act", act(HS[:, c, 0:N], HF[:, j, 0:N], AF.Copy), reads=[r_t["HF"][j]], writes=[r_HS[c]])

        s5slot = [None]

        r_X4 = [Res("X4_%d" % t) for t in range(4)]
        tmp_last = [[]]

        lru_gen = [None]
        tick_ctr = [0]
        TICK = 10 if full else 6

        def dv(fn, after):
            m = P.op("dve", fn, after=[m for m in after if m is not None])
            tick_ctr[0] += 1
            if lru_gen[0] is not None and tick_ctr[0] % TICK == 0:
                try:
                    next(lru_gen[0])
                except StopIteration:
                    lru_gen[0] = None
            return m

        def s5_inproj(kt):
            if kt % 2 == 0:
                s5slot[0] = wslot()
                pinned[:] = [s5slot[0]]
                wload(s5slot[0], WR[:, s5slot[0], :, :], wview(w_in, 0, D, 2048 + (kt // 2) * 256, 256))
            s = s5slot[0]
            mc = kt % 2
            xs = kt % 2
            for (n0, nn) in ntiles:
                pb, rb = bank()
                def mm(e, pb=pb, s=s, mc=mc, n0=n0, nn=nn):
                    ins = None
                    for k in range(NCH):
                        ins = e.matmul(out=pb[:, 0:nn], lhsT=WR[:, s, k, mc * 128:(mc + 1) * 128],
                                       rhs=UT[:, k, n0:n0 + nn], start=(k == 0), stop=(k == NCH - 1))
                    return ins
                P.op("pe", mm, reads=[r_WR[s]] + r_UT, writes=[rb])
                P.op("act", act(XS[:, xs, n0:n0 + nn], pb[:, 0:nn], AF.Copy), reads=[rb], writes=[r_XS[xs]])

        def s5_chunk(kt):
            xs = kt % 2
            ybanks = [ybank(i) for i in range(len(ntiles))] if full else []
            j0 = kt * 4
            for t in range(4):
                j = j0 + t
                for (n0, nn) in ntiles:
                    pr_, rr_ = bank()
                    pi_, ri_ = bank()
                    def mb(e, pr_=pr_, pi_=pi_, j=j, xs=xs, n0=n0, nn=nn):
                        e.matmul(out=pr_[:, 0:nn], lhsT=WB[:, j, 0, :], rhs=XS[:, xs, n0:n0 + nn], start=True, stop=True)
                        return e.matmul(out=pi_[:, 0:nn], lhsT=WB[:, j, 1, :], rhs=XS[:, xs, n0:n0 + nn], start=True, stop=True)
                    P.op("pe", mb, reads=[r_WBCW, r_XS[xs]], writes=[rr_, ri_])
                    P.op("act", act(X4[:, t, 0, n0:n0 + nn], pr_[:, 0:nn], AF.Copy), reads=[rr_], writes=[r_X4[t]])
                    P.op("dve", lambda e, t=t, pi_=pi_, n0=n0, nn=nn: e.tensor_copy(out=X4[:, t, 1, n0:n0 + nn], in_=pi_[:, 0:nn]),
                         reads=[ri_], writes=[r_X4[t]])
            C4 = S5C[:, j0:j0 + 4, :]
            if g["mask"]:
                P.op("dve", ts(C4, C4, FLG[:, 0:1], None, ALU.mult), reads=[r_S5C, r_small], writes=[r_S5C])
            prev_t = [[r_X4[t].w] + list(r_X4[t].r) for t in range(4)]
            carry_dep = [r_S5C.w]

            def batched(l, s_both, s_re, s_im, t_re, t_im, cnt, after):
                arp = LP2[:, l, j0:j0 + 4, 0:2].unsqueeze(3).broadcast_to([128, 4, 2, cnt])
                ai_ = LP2[:, l, j0:j0 + 4, 2].unsqueeze(2).broadcast_to([128, 4, cnt])
                nai_ = LP2[:, l, j0:j0 + 4, 3].unsqueeze(2).broadcast_to([128, 4, cnt])
                t1, t2, t3 = TM1[:, :, :, 0:cnt], TM2[:, :, 0:cnt], TM3[:, :, 0:cnt]
                aft = list(after) + tmp_last[0]
                m1 = dv(tt(t1, s_both, arp, ALU.mult), aft)
                m2 = dv(tt(t2, s_im, nai_, ALU.mult), aft)
                m3 = dv(tt(t3, s_re, ai_, ALU.mult), aft)
                c2 = dv(tt(t2, t2, TM1[:, :, 0, 0:cnt], ALU.add), [m1, m2])
                c3 = dv(tt(t3, t3, TM1[:, :, 1, 0:cnt], ALU.add), [m1, m3])
                a_re = dv(tt(t_re, t_re, t2, ALU.add), [c2] + list(after))
                a_im = dv(tt(t_im, t_im, t3, ALU.add), [c3] + list(after))
                tmp_last[0] = [a_re, a_im]
                return [a_re, a_im]

            def per_tile(l, sl_s, sl_t, prevs):
                ab = []
                for t in range(4):
                    ar = LP2[:, l, j0 + t, 0:1]
                    ab.append(dv(stt(X4[:, t, :, sl_t], X4[:, t, :, sl_s], ar, X4[:, t, :, sl_t], ALU.mult, ALU.add), prevs[t]))
                out = [[] for _ in range(4)]
                for t in range(4):
                    nai = LP2[:, l, j0 + t, 3:4]
                    out[t].append(dv(stt(X4[:, t, 0, sl_t], X4[:, t, 1, sl_s], nai, X4[:, t, 0, sl_t], ALU.mult, ALU.add), [ab[t]] + prevs[t]))
                for t in range(4):
                    ai = LP2[:, l, j0 + t, 2:3]
                    out[t].append(dv(stt(X4[:, t, 1, sl_t], X4[:, t, 0, sl_s], ai, X4[:, t, 1, sl_t], ALU.mult, ALU.add), [ab[t]] + prevs[t]))
                return out

            BL = 3
            allprev = [m for p in prev_t for m in p]
            d_all = batched(0, C4, C4[:, :, 0:1], C4[:, :, 1:2], X4[:, :, 0, 0:1], X4[:, :, 1, 0:1], 1, allprev + carry_dep)
            prevs = [list(d_all) for _ in range(4)]
            for l in range(L):
                d = 1 << l
                sl_t = slice(2 * d - 1, npr, 2 * d)
                sl_s = slice(d - 1, npr - d, 2 * d)
                cnt = npr // (2 * d)
                if l < BL:
                    prevs = per_tile(l, sl_s, sl_t, prevs)
                else:
                    flat = [m for p in prevs for m in p]
                    d_all = batched(l, X4[:, :, :, sl_s], X4[:, :, 0, sl_s], X4[:, :, 1, sl_s], X4[:, :, 0, sl_t], X4[:, :, 1, sl_t], cnt, flat)
                    prevs = [list(d_all) for _ in range(4)]
            if full:
                for l in range(L - 2, -1, -1):
                    d = 1 << l
                    sl_t = slice(3 * d - 1, npr, 2 * d)
                    sl_s = slice(2 * d - 1, npr - d, 2 * d)
                    cnt = npr // (2 * d) - 1
                    if l < BL:
                        prevs = per_tile(l, sl_s, sl_t, prevs)
                    else:
                        flat = [m for p in prevs for m in p]
                        d_all = batched(l, X4[:, :, :, sl_s], X4[:, :, 0, sl_s], X4[:, :, 1, sl_s], X4[:, :, 0, sl_t], X4[:, :, 1, sl_t], cnt, flat)
                        prevs = [list(d_all) for _ in range(4)]
            if ns:
                flat = [m for p in prevs for m in p]
                d_all = batched(0, None, None, None, None, None, 0, flat) if False else None
                cnt = ns
                arp = LP2[:, 0, j0:j0 + 4, 0:1].broadcast_to([128, 4, cnt])
                ai_ = LP2[:, 0, j0:j0 + 4, 2:3].broadcast_to([128, 4, cnt])
                nai_ = LP2[:, 0, j0:j0 + 4, 3:4].broadcast_to([128, 4, cnt])
                sR, sI = S5R[:, j0:j0 + 4, :], S5I[:, j0:j0 + 4, :]
                t1r, t1i = TM1[:, :, 0, 0:cnt], TM1[:, :, 1, 0:cnt]
                t2, t3 = TM2[:, :, 0:cnt], TM3[:, :, 0:cnt]
                aft = flat + tmp_last[0] + [r_small.w]
                m1 = dv(tt(t1r, sR, arp, ALU.mult), aft)
                m1b = dv(tt(t1i, sI, arp, ALU.mult), aft)
                m2 = dv(tt(t2, sI, nai_, ALU.mult), aft)
                m3 = dv(tt(t3, sR, ai_, ALU.mult), aft)
                c2 = dv(tt(t2, t2, t1r, ALU.add), [m1, m2])
                c3 = dv(tt(t3, t3, t1i, ALU.add), [m1b, m3])
                a_re = dv(tt(X4[:, :, 0, npr:N], X4[:, :, 0, npr:N], t2, ALU.add), [c2] + flat)
                a_im = dv(tt(X4[:, :, 1, npr:N], X4[:, :, 1, npr:N], t3, ALU.add), [c3] + flat)
                tmp_last[0] = [a_re, a_im]
                prevs = [p + [a_re, a_im] for p in prevs]
            for t in range(4):
                last = max(m[1] for m in prevs[t] if m[0] == "dve")
                r_X4[t].w = ("dve", last); r_X4[t].r = []
            if lru_gen[0] is not None:
                for _ in lru_gen[0]:
                    pass
                lru_gen[0] = None
            P.op("dve", lambda e: e.tensor_copy(out=C4, in_=X4[:, :, :, npr - 1]), reads=r_X4, writes=[r_S5C])
            if ns:
                P.op("dve", lambda e: e.tensor_copy(out=OSS[:, 0, j0:j0 + 4, :], in_=X4[:, :, 0, npr:N]), reads=r_X4, writes=[r_out])
                P.op("dve", lambda e: e.tensor_copy(out=OSS[:, 1, j0:j0 + 4, :], in_=X4[:, :, 1, npr:N]), reads=r_X4, writes=[r_out])
            if full:
                for t in range(4):
                    j = j0 + t
                    P.op("act", act(XRB[:, 0, 0:N], X4[:, t, 0, 0:N], AF.Copy), reads=[r_X4[t]], writes=[r_XCB[0]])
                    P.op("act", act(XRB[:, 1, 0:N], X4[:, t, 1, 0:N], AF.Copy), reads=[r_X4[t]], writes=[r_XCB[1]])
                    for ti_n, (n0, nn) in enumerate(ntiles):
                        yb, ryb = ybanks[ti_n]
                        def mc_(e, yb=yb, j=j, t=t, n0=n0, nn=nn):
                            e.matmul(out=yb[:, 0:nn], lhsT=CW[:, j, 0, :], rhs=XRB[:, 0, n0:n0 + nn], start=(t == 0), stop=False)
                            return e.matmul(out=yb[:, 0:nn], lhsT=CW[:, j, 1, :], rhs=XRB[:, 1, n0:n0 + nn], start=False, stop=(t == 3))
                        P.op("pe", mc_, reads=[r_WBCW] + r_XCB, writes=[ryb])
                for ti_n, (n0, nn) in enumerate(ntiles):
                    yb, ryb = ybanks[ti_n]
                    P.op("dve", stt(YT[:, n0:n0 + nn], XS[:, xs, n0:n0 + nn], SD[:, kt:kt + 1], yb[:, 0:nn], ALU.mult, ALU.add),
                         reads=[ryb, r_XS[xs], r_small], writes=[r_Y])
                yy, y2, y3 = YT[:, 0:N], Y2[:, 0:N], Y3[:, 0:N]
                P.op("act", act(y2, yy, AF.Square), reads=[r_Y], writes=[r_Y])
                P.op("dve", ts(y2, y2, GELU_C1, 1.0, ALU.mult, ALU.add), reads=[r_Y], writes=[r_Y])
                P.op("dve", tt(y2, y2, yy, ALU.mult), reads=[r_Y], writes=[r_Y])
                P.op("act", act(y3, y2, AF.Tanh, scale=GELU_C0), reads=[r_Y], writes=[r_Y])
                P.op("dve", stt(y3, y3, 1.0, yy, ALU.add, ALU.mult), reads=[r_Y], writes=[r_Y])
                P.op("act", act(VT[:, kt, 0:N], y3, AF.Copy, scale=0.5), reads=[r_Y], writes=[r_VT[kt]])

        lru_cur = [None]

        def lru_pipeline(kt):
            stage = 0
            for _ in lru_cur[0]:
                stage += 1
                if stage == 5 and kt < 7:
                    nb_ = lru_block(kt + 1, kt + 2 if kt + 1 < 7 else None)
                    next(nb_)
                    lru_cur[0] = nb_
                yield

        s5_inproj(0)
        lru_cur[0] = lru_block(0, 1)
        next(lru_cur[0])
        for kt in range(8):
            convert_some(2)
            lru_gen[0] = lru_pipeline(kt)
            s5_chunk(kt)
        P.barrier()
        chk(g["name"] + "C")

        if not full:
            return

        pinned[:] = []
        back_ptr[0] = 0
        convert_some(1000)
        off[0] = 0
        MA = tl(4, N); TG = tl(2, N); TQ = tl(2, N)
        r_MA = [Res("MA%d" % k) for k in range(4)]
        r_TG = [Res("TG0"), Res("TG1")]
        r_TQ = [Res("TQ0"), Res("TQ1")]
        tcount = [0]
        for sl in range(8):
            s1 = wslot(); bload(s1)
            s2 = wslot(); bload(s2)
            for mc in range(2):
                for (n0, nn) in ntiles:
                    pg, rg = bank()
                    py, ry = bank()
                    def mm(e, pg=pg, py=py, mc=mc, n0=n0, nn=nn, s1=s1, s2=s2):
                        ins = None
                        for k in range(NCH):
                            ins = e.matmul(out=pg[:, 0:nn], lhsT=WR[:, s1, k, mc * 128:(mc + 1) * 128], rhs=UT[:, k, n0:n0 + nn],
                                           start=(k == 0), stop=(k == NCH - 1))
                        for k in range(NCH):
                            ins = e.matmul(out=py[:, 0:nn], lhsT=WR[:, s2, k, mc * 128:(mc + 1) * 128], rhs=HS[:, k, n0:n0 + nn],
                                           start=(k == 0), stop=(k == NCH - 1))
                        return ins
                    P.op("pe", mm, reads=[r_WR[s1], r_WR[s2]] + r_UT + r_HS, writes=[rg, ry])
                    tb = tcount[0] % 2; tcount[0] += 1
                    P.op("act", act(TG[:, tb, 0:nn], pg[:, 0:nn], AF.Tanh, scale=0.5), reads=[rg], writes=[r_TG[tb]])
                    P.op("dve", stt(MA[:, mc, n0:n0 + nn], TG[:, tb, 0:nn], 1.0, py[:, 0:nn], ALU.add, ALU.mult),
                         reads=[r_TG[tb], ry], writes=[r_MA[mc]])
            s3 = wslot(); bload(s3)
            s4 = wslot(); bload(s4)
            for mc in range(2):
                m = sl * 2 + mc
                for (n0, nn) in ntiles:
                    pg, rg = bank()
                    pv_, rv = bank()
                    pw, rw = bank()
                    def mm(e, pg=pg, pv_=pv_, pw=pw, mc=mc, n0=n0, nn=nn, s3=s3, s4=s4):
                        ins = None
                        for k in range(NCH):
                            ins = e.matmul(out=pg[:, 0:nn], lhsT=WR[:, s3, k, mc * 128:(mc + 1) * 128], rhs=UT[:, k, n0:n0 + nn],
                                           start=(k == 0), stop=(k == NCH - 1))
                        for k in range(8):
                            ins = e.matmul(out=pv_[:, 0:nn], lhsT=WR[:, s4, k, mc * 128:(mc + 1) * 128], rhs=VT[:, k, n0:n0 + nn],
                                           start=(k == 0), stop=(k == 7))
                        for k in range(8):
                            ins = e.matmul(out=pw[:, 0:nn], lhsT=WR[:, s4, 8 + k, mc * 128:(mc + 1) * 128], rhs=VT[:, k, n0:n0 + nn],
                                           start=(k == 0), stop=(k == 7))
                        return ins
                    P.op("pe", mm, reads=[r_WR[s3], r_WR[s4]] + r_UT + r_VT, writes=[rg, rv, rw])
                    tb = tcount[0] % 2; tcount[0] += 1
                    P.op("act", act(TG[:, tb, 0:nn], pg[:, 0:nn], AF.Tanh, scale=0.5), reads=[rg], writes=[r_TG[tb]])
                    P.op("act", act(TQ[:, tb, 0:nn], pw[:, 0:nn], AF.Tanh, scale=0.5), reads=[rw], writes=[r_TQ[tb]])
                    P.op("dve", stt(TQ[:, tb, 0:nn], TQ[:, tb, 0:nn], 1.0, pv_[:, 0:nn], ALU.add, ALU.mult),
                         reads=[r_TQ[tb], rv], writes=[r_TQ[tb]])
                    P.op("dve", stt(TQ[:, tb, 0:nn], TG[:, tb, 0:nn], 1.0, TQ[:, tb, 0:nn], ALU.add, ALU.mult),
                         reads=[r_TQ[tb], r_TG[tb]], writes=[r_TQ[tb]])
                    P.op("dve", stt(TQ[:, tb, 0:nn], TQ[:, tb, 0:nn], 0.5, MA[:, mc, n0:n0 + nn], ALU.mult, ALU.add),
                         reads=[r_TQ[tb], r_MA[mc]], writes=[r_TQ[tb]])
                    P.op("act", act(MG[:, m, n0:n0 + nn], TQ[:, tb, 0:nn], AF.Copy, scale=0.5), reads=[r_TQ[tb]], writes=[r_MG[m]])
        P.barrier()
        chk(g["name"] + "D")

        for ti, (c0, n) in enumerate(ttiles):
            P.dma("sp", xsem[ti], lambda e, ti=ti, c0=c0, n=n: e.dma_start(out=XP1[0:n, ti, :], in_=xf[row0 + c0:row0 + c0 + n, :]),
                  writes=[r_XP1[ti]])
        for nb in range(8):
            s = wslot(); bload(s)
            for ti, (c0, n) in enumerate(ttiles):
                pb, rb = bank()
                def mm(e, pb=pb, s=s, c0=c0, n=n):
                    ins = None
                    for k in range(NCH):
                        ins = e.matmul(out=pb[0:n, 0:256], lhsT=MG[:, k, c0:c0 + n], rhs=WR[:, s, k, :], start=(k == 0), stop=(k == NCH - 1))
                    return ins
                P.op("pe", mm, reads=[r_WR[s]] + r_MG, writes=[rb])
                dst = XP1[0:n, ti, nb * 256:(nb + 1) * 256]
                P.op("dve", tt(dst, pb[0:n, 0:256], dst, ALU.add), reads=[rb, r_XP1[ti]], writes=[r_XP1[ti]])
        P.barrier()
        chk(g["name"] + "E")

        JK2 = HM[:, 40:44, :].rearrange("p a b -> p (a b)")
        r_JK2 = Res("JK2")
        r_XTN = r_XTN_g
        for ti, (c0, n) in enumerate(ttiles):
            xt = XP1[0:n, ti, :]
            ssq = SM[0:n, 8 + ti:9 + ti]
            P.op("act", act(JK2[0:n, 0:D], xt, AF.Square, accum_out=ssq), reads=[r_XP1[ti]], writes=[r_JK2, r_SM])
            P.op("act", act(ssq, ssq, AF.Sqrt, bias=EPSB[0:n, :], scale=1.0 / D), reads=[r_SM], writes=[r_SM])
            P.op("dve", lambda e, ssq=ssq: e.reciprocal(out=ssq, in_=ssq), reads=[r_SM], writes=[r_SM])
            xn = XTN[0:n, :]
            P.op("act", act(xn, xt, AF.Identity, scale=ssq), reads=[r_SM, r_XP1[ti]], writes=[r_XTN])
            for k4 in range(4):
                pb, rb = bank()
                def tr(e, pb=pb, xn=xn, k4=k4, n=n):
                    ins = None
                    for kk in range(4):
                        k = k4 * 4 + kk
                        ins = e.transpose(out=pb[:, kk * 128:kk * 128 + n], in_=xn[:, k * 128:(k + 1) * 128], identity=IDN[0:n, 0:n])
                    return ins
                P.op("pe", tr, reads=[r_XTN], writes=[rb])
                for kk in range(4):
                    k = k4 * 4 + kk
                    if True:
                        P.op("dve", ts(UT[:, k, c0:c0 + n], pb[:, kk * 128:kk * 128 + n], PV[:, 5, k:k + 1], None, ALU.mult),
                             reads=[rb], writes=[r_UT[k]])
                    else:
                        P.op("act", act(UT[:, k, c0:c0 + n], pb[:, kk * 128:kk * 128 + n], AF.Identity, scale=PV[:, 5, k:k + 1]),
                             reads=[rb], writes=[r_UT[k]])
        P.barrier()
        chk(g["name"] + "F")

        r_TGF = [Res("TGF0"), Res("TGF1")]
        tcount[0] = 0
        for f in range(44):
            fi = f % 2
            if fi == 0:
                sg = wslot()
                bload(sg)
                su = wslot()
                bload(su)
            for (n0, nn) in ntiles:
                pg, rg = bank()
                pu, ru = bank()
                def mm(e, pg=pg, pu=pu, sg=sg, su=su, fi=fi, n0=n0, nn=nn):
                    ins = None
                    for k in range(NCH):
                        ins = e.matmul(out=pg[:, 0:nn], lhsT=WR[:, sg, k, fi * 128:(fi + 1) * 128], rhs=UT[:, k, n0:n0 + nn], start=(k == 0), stop=(k == NCH - 1))
                    for k in range(NCH):
                        ins = e.matmul(out=pu[:, 0:nn], lhsT=WR[:, su, k, fi * 128:(fi + 1) * 128], rhs=UT[:, k, n0:n0 + nn], start=(k == 0), stop=(k == NCH - 1))
                    return ins
                P.op("pe", mm, reads=[r_WR[sg], r_WR[su]] + r_UT, writes=[rg, ru])
                tb = tcount[0] % 2; tcount[0] += 1
                P.op("act", act(TGF[:, tb, 0:nn], pg[:, 0:nn], AF.Tanh, scale=0.5), reads=[rg], writes=[r_TGF[tb]])
                P.op("dve", stt(TGF[:, tb, 0:nn], TGF[:, tb, 0:nn], 1.0, pg[:, 0:nn], ALU.add, ALU.mult), reads=[rg, r_TGF[tb]], writes=[r_TGF[tb]])
                P.op("dve", stt(HM[:, f, n0:n0 + nn], TGF[:, tb, 0:nn], 0.5, pu[:, 0:nn], ALU.mult, ALU.mult), reads=[ru, r_TGF[tb]], writes=[r_HM[f]])

        fgroups = [(0, 16), (16, 16), (32, 12)]
        for nb in range(8):
            banks = [bank() for _ in ttiles]
            for (f0, nf) in fgroups:
                s = wslot()
                bload(s)
                def mm(e, s=s, f0=f0, nf=nf, banks=banks):
                    ins = None
                    for fk in range(nf):
                        f = f0 + fk
                        for ti, (c0, n) in enumerate(ttiles):
                            ins = e.matmul(out=banks[ti][0][0:n, 0:256], lhsT=HM[:, f, c0:c0 + n], rhs=WR[:, s, fk, :],
                                           start=(f == 0), stop=(f == 43))
                    return ins
                P.op("pe", mm, reads=[r_WR[s]] + r_HM, writes=[b[1] for b in banks])
            for ti, (c0, n) in enumerate(ttiles):
                dst = XP1[0:n, ti, nb * 256:(nb + 1) * 256]
                P.op("dve", tt(dst, banks[ti][0][0:n, 0:256], dst, ALU.add), reads=[banks[ti][1], r_XP1[ti]], writes=[r_XP1[ti]])

        P.dma("sp", isem, lambda e: e.dma_start(out=XTN[:], in_=gfin_d[:, :]), writes=[r_XTN] + r_TGF)
        for ti, (c0, n) in enumerate(ttiles):
            xt = XP1[0:n, ti, :]
            ssq = SM[0:n, 16 + ti:17 + ti]
            P.op("act", act(JK2[0:n, 0:D], xt, AF.Square, accum_out=ssq), reads=[r_XP1[ti]], writes=[r_JK2, r_SM])
            P.op("act", act(ssq, ssq, AF.Sqrt, bias=EPSB[0:n, :], scale=1.0 / D), reads=[r_SM], writes=[r_SM])
            P.op("dve", lambda e, ssq=ssq: e.reciprocal(out=ssq, in_=ssq), reads=[r_SM], writes=[r_SM])
            P.op("dve", stt(xt, xt, ssq, XTN[0:n, :], ALU.mult, ALU.mult), reads=[r_SM, r_XP1[ti], r_XTN], writes=[r_XP1[ti]])
            orow = row0 - 1024 + c0
            P.dma("sp", osem, lambda e, xt=xt, orow=orow, n=n: e.dma_start(out=o_y[orow:orow + n, :], in_=xt), reads=[r_XP1[ti]])
        P.barrier()

    import os
    stop = os.environ.get("KSTOP", "")
    class _Stop(Exception):
        pass
    def chk(tag):
        if stop and tag == stop:
            raise _Stop()
    try:
        chk("setup")
        for g_ in groups:
            do_group(g_, chk)
    except _Stop:
        pass

    def st(dst, src, res):
        P.dma("sp", osem, lambda e: e.dma_start(out=dst, in_=src), reads=res)
    st(o_cp[:, :], XLT[:].rearrange("p a b -> p (a b)"), [r_XLT])
    st(o_hp[:, :], HC[:], [r_HC])
    st(o_sp[:, :], S5C[:].rearrange("p a b -> p (a b)"), [r_S5C])
    st(o_cs[:, :], OCS[:].rearrange("p a b c -> p (a b c)"), [r_out])
    st(o_hs[:, :], OHS[:].rearrange("p a b -> p (a b)"), [r_out])
    st(o_ss[:, :], OSS[:].rearrange("p a b c -> p (a b c)"), [r_out])
    fin = P.dcnt[osem]
    P.q["sp"].append(lambda e: e.wait_ge(P.sems[osem], fin))

    with nc.Block() as block:
        @block.tensor
        def _(e):
            for f in P.q["pe"]:
                f(e)

        @block.scalar
        def _(e):
            for f in P.q["act"]:
                f(e)

        @block.vector
        def _(e):
            for f in P.q["dve"]:
                f(e)

        @block.gpsimd
        def _(e):
            for f in P.q["pool"]:
                f(e)

        @block.sync
        def _(e):
            for f in P.q["sp"]:
                f(e)
    es.close()
    return nc


_NC = None


def kernel(**inp):
    global _NC
    f32 = np.float32
    g = lambda k: np.asarray(inp[k], dtype=f32)
    xp = g("x_prompt"); xs = g("x_sample").reshape(128, D)
    conv = g("state_lru_conv")[0]; h0 = g("state_lru_h")[0]
    s5r = g("state_s5_re")[0].reshape(128, 4096); s5i = g("state_s5_im")[0].reshape(128, 4096)
    ident = np.eye(128, dtype=f32)
    maskc = np.zeros((128, 4, 128), f32)
    for jl in range(4):
        for gl in range(2):
            g8 = 2 * jl + gl
            maskc[g8 * 16:(g8 + 1) * 16, jl, gl * 64:(gl + 1) * 64] = 1.0
    maskc = maskc.reshape(128, 512)
    def fm16(v):
        return np.ascontiguousarray(v.reshape(16, 128).T)
    def st32(v):
        return np.ascontiguousarray(v.reshape(32, 128).T)
    ldt = np.repeat(g("s5_log_dt")[0], 64)
    pvec = np.zeros((128, 7, 16), f32)
    pvec[:, 0] = fm16(g("lru_conv_b")[0]); pvec[:, 1] = fm16(g("lru_ba")[0].reshape(-1)); pvec[:, 2] = fm16(g("lru_bx")[0].reshape(-1))
    pvec[:, 3] = fm16(g("lru_lambda")[0]); pvec[:, 4] = fm16(g("norm_mix_g")[0]); pvec[:, 5] = fm16(g("norm_ffn_g")[0])
    convw = np.ascontiguousarray(g("lru_conv_w")[0].reshape(4, 16, 128).transpose(2, 1, 0)).reshape(128, 64)
    bre = np.ascontiguousarray(g("s5_b_re")[0].reshape(32, 128, 16).transpose(1, 0, 2)).reshape(128, 512)
    bim = np.ascontiguousarray(g("s5_b_im")[0].reshape(32, 128, 16).transpose(1, 0, 2)).reshape(128, 512)
    cre = np.ascontiguousarray(g("s5_c_re")[0].reshape(8, 128, 64).transpose(1, 0, 2)).reshape(128, 512)
    cim = np.ascontiguousarray(g("s5_c_im")[0].reshape(8, 128, 64).transpose(1, 0, 2)).reshape(128, 512)
    s5d = np.ascontiguousarray(g("s5_d")[0].reshape(8, 128).T)
    gfin = np.ascontiguousarray(np.broadcast_to(g("norm_final_g")[None, :], (128, D)))
    shared = dict(
        ident=ident, maskc=maskc, ldt=st32(ldt), lre=st32(g("s5_lambda_re")[0].reshape(-1)), lim=st32(g("s5_lambda_im")[0].reshape(-1)),
        bre=bre, bim=bim, cre=cre, cim=cim, s5d=s5d, convw=convw, pvec=pvec.reshape(128, 112), gfin=gfin,
        w_in=g("w_in")[0], lru_wa=g("lru_wa")[0].reshape(2048, 256), lru_wx=g("lru_wx")[0].reshape(2048, 256),
        lru_proj=g("lru_proj")[0], s5_glu_wv=g("s5_glu_wv")[0], s5_glu_wg=g("s5_glu_wg")[0], w_out=g("w_out")[0],
        ffn_w_gate=g("ffn_w_gate")[0], ffn_w_up=g("ffn_w_up")[0], ffn_w_down=g("ffn_w_down")[0],
    )
    in_maps = []
    for c in range(8):
        b, half = c // 2, c % 2
        xf = np.zeros((2064, D), f32)
        if half == 1:
            xf[0:1024] = xp[b, 0:1024]
        xf[1024:2048] = xp[b, half * 1024:(half + 1) * 1024]
        xf[2048:2064] = xs[c * 16:(c + 1) * 16]
        flag = np.zeros((128, 2), f32); flag[:, 0] = float(half); flag[:, 1] = 1.0 - float(half)
        sl = slice(c * 16, (c + 1) * 16)
        csT = np.ascontiguousarray(conv[sl].reshape(16, 3, 16, 128).transpose(3, 2, 0, 1)).reshape(128, 768)
        h0T = np.ascontiguousarray(h0[sl].reshape(16, 16, 128).transpose(2, 1, 0)).reshape(128, 256)
        s5rT = np.ascontiguousarray(s5r[sl].reshape(16, 32, 128).transpose(2, 1, 0)).reshape(128, 512)
        s5iT = np.ascontiguousarray(s5i[sl].reshape(16, 32, 128).transpose(2, 1, 0)).reshape(128, 512)
        m = dict(shared); m.update(xf=xf, flag=flag, csT=csT, h0T=h0T, s5rT=s5rT, s5iT=s5iT)
        in_maps.append(m)
    if _NC is None:
        _NC = build_program()
    res = run_bass_kernel_spmd(_NC, in_maps, core_ids=list(range(8)))
    R = res.results
    y_prompt = np.zeros((4, 2048, D), f32); y_sample = np.zeros((128, 1, D), f32)
    convp = np.zeros((1, 4, 3, D), f32); hp = np.zeros((1, 4, D), f32)
    srp = np.zeros((1, 4, 64, 64), f32); sip = np.zeros((1, 4, 64, 64), f32)
    convs = np.zeros((1, 128, 3, D), f32); hs = np.zeros((1, 128, D), f32)
    srs = np.zeros((1, 128, 64, 64), f32); sis = np.zeros((1, 128, 64, 64), f32)
    for c in range(8):
        b, half = c // 2, c % 2
        r = R[c]
        y_prompt[b, half * 1024:(half + 1) * 1024] = r["o_y"][0:1024]
        y_sample[c * 16:(c + 1) * 16, 0] = r["o_y"][1024:1040]
        sl = slice(c * 16, (c + 1) * 16)
        convs[0, sl] = r["o_cs"].reshape(128, 16, 16, 3).transpose(2, 3, 1, 0).reshape(16, 3, D)
        hs[0, sl] = r["o_hs"].reshape(128, 16, 16).transpose(2, 1, 0).reshape(16, D)
        ss = r["o_ss"].reshape(128, 2, 32, 16)
        srs[0, sl] = ss[:, 0].transpose(2, 1, 0).reshape(16, 64, 64)
        sis[0, sl] = ss[:, 1].transpose(2, 1, 0).reshape(16, 64, 64)
        if half == 1:
            convp[0, b] = r["o_cp"].reshape(128, 16, 3).transpose(2, 1, 0).reshape(3, D)
            hp[0, b] = r["o_hp"].T.reshape(D)
            sp = r["o_sp"].reshape(128, 32, 2)
            srp[0, b] = sp[:, :, 0].T.reshape(64, 64)
            sip[0, b] = sp[:, :, 1].T.reshape(64, 64)
    return (y_prompt, y_sample, convp, hp, srp, sip, convs, hs, srs, sis)
```
